# Optimizing a Trainium2 kernel written in Bass

```python
import math
import jax, jax.numpy as jnp
from jax import lax
import numpy as np

D_MODEL = 1024
BATCH = 8
SEQ = 4096
DEPTH = 4

CHUNK = 64
N_HEADS_A = 8
HEAD_DIM_A = 64
V_DIM_A = 2 * HEAD_DIM_A
WIDTH_A = N_HEADS_A * V_DIM_A
N_QK = N_HEADS_A * 2 * HEAD_DIM_A
ROPE_THETA = 10000.0
Q_BLOCK = 128
N_GROUPS_B = 8
WIDTH_B = D_MODEL
GROUP_DIM_B = WIDTH_B // N_GROUPS_B
SGU_CHUNK = 128
D_FF = -(-8 * D_MODEL // (3 * 256)) * 256
NORM_EPS = 1e-6
N_IN = 2 * N_QK + WIDTH_A + 2 * WIDTH_B + 2 * D_MODEL
SPLITS = (N_QK, 2 * N_QK, 2 * N_QK + WIDTH_A, 2 * N_QK + WIDTH_A + WIDTH_B,
          2 * N_QK + WIDTH_A + 2 * WIDTH_B, 2 * N_QK + WIDTH_A + 2 * WIDTH_B + D_MODEL)

kernel_name = "hybrid_diffattn_gmlp_gated_block"


def rms_norm(x, g):
    xf = x.astype(jnp.float32)
    y = xf * lax.rsqrt(jnp.mean(xf * xf, axis=-1, keepdims=True) + NORM_EPS)
    return (y * g.astype(jnp.float32)).astype(x.dtype)


def layer_norm(x, g, b):
    xf = x.astype(jnp.float32)
    mu = jnp.mean(xf, axis=-1, keepdims=True)
    var = jnp.mean(jnp.square(xf - mu), axis=-1, keepdims=True)
    y = (xf - mu) * lax.rsqrt(var + NORM_EPS)
    return (y * g.astype(jnp.float32) + b.astype(jnp.float32)).astype(x.dtype)


def rotary(x, cos, sin):
    half = x.shape[-1] // 2
    x1, x2 = x[..., :half], x[..., half:]
    c = cos[None, :, None, None, :].astype(x.dtype)
    s = sin[None, :, None, None, :].astype(x.dtype)
    return jnp.concatenate([x1 * c - x2 * s, x2 * c + x1 * s], axis=-1)


def diff_attention(q, k, v, lam):
    B, S = q.shape[0], q.shape[1]
    nb = S // Q_BLOCK
    scale = HEAD_DIM_A ** -0.5
    qb = q.reshape(B, nb, Q_BLOCK, N_HEADS_A, 2, HEAD_DIM_A).transpose(1, 0, 3, 4, 2, 5)
    kt = k.transpose(0, 2, 3, 1, 4)
    vt = v.transpose(0, 2, 1, 3)
    k_chunk = jnp.arange(S) // CHUNK

    def block(args):
        q_blk, i = args
        s = jnp.einsum('bhmqd,bhmkd->bhmqk', q_blk, kt,
                       preferred_element_type=jnp.float32) * scale
        q_chunk = (i * Q_BLOCK + jnp.arange(Q_BLOCK)) // CHUNK
        allowed = k_chunk[None, :] <= q_chunk[:, None]
        s = jnp.where(allowed, s, jnp.float32(-1e30))
        p = jax.nn.softmax(s, axis=-1)
        a = p[:, :, 0] - lam * p[:, :, 1]
        return jnp.einsum('bhqk,bhke->bhqe', a.astype(vt.dtype), vt)

    o = lax.map(block, (qb, jnp.arange(nb)))
    return o.transpose(1, 0, 3, 2, 4).reshape(B, S, N_HEADS_A, V_DIM_A)


def spatial_gating(u, v, w_s, b_s, ln_g, ln_b):
    B, S = u.shape[0], u.shape[1]
    v = layer_norm(v, ln_g, ln_b)
    v = v.reshape(B, S // SGU_CHUNK, SGU_CHUNK, N_GROUPS_B, GROUP_DIM_B)
    pos_chunk = jnp.arange(SGU_CHUNK) // CHUNK
    mask = pos_chunk[None, :] <= pos_chunk[:, None]
    w = jnp.where(mask[None], w_s, jnp.zeros((), w_s.dtype))
    mixed = jnp.einsum('gts,bnsgc->bntgc', w, v) + b_s.T[:, :, None]
    return u * mixed.reshape(B, S, WIDTH_B)


def setup_inputs(seed: int = 0) -> dict:
    key = jax.random.key(seed)
    ks = jax.random.split(key, 24)
    f32 = jnp.float32

    def nrm(k, shape, scale):
        return jax.random.normal(k, shape, f32) * scale

    return {
        "x": nrm(ks[0], (BATCH, SEQ, D_MODEL), 1.0),
        "attn_norm_g": 1.0 + nrm(ks[1], (DEPTH, D_MODEL), 0.02),
        "w_in": nrm(ks[2], (DEPTH, D_MODEL, N_IN), D_MODEL ** -0.5),
        "lam_q1": nrm(ks[3], (DEPTH, HEAD_DIM_A), 0.1),
        "lam_k1": nrm(ks[4], (DEPTH, HEAD_DIM_A), 0.1),
        "lam_q2": nrm(ks[5], (DEPTH, HEAD_DIM_A), 0.1),
        "lam_k2": nrm(ks[6], (DEPTH, HEAD_DIM_A), 0.1),
        "subln_g": 1.0 + nrm(ks[7], (DEPTH, V_DIM_A), 0.02),
        "sgu_ln_g": 1.0 + nrm(ks[8], (DEPTH, WIDTH_B), 0.02),
        "sgu_ln_b": nrm(ks[9], (DEPTH, WIDTH_B), 0.02),
        "w_spatial": nrm(ks[10], (DEPTH, N_GROUPS_B, SGU_CHUNK, SGU_CHUNK), SGU_CHUNK ** -0.5),
        "b_spatial": 1.0 + nrm(ks[11], (DEPTH, N_GROUPS_B, SGU_CHUNK), 0.01),
        "w_proj_a": nrm(ks[12], (DEPTH, WIDTH_A, D_MODEL), WIDTH_A ** -0.5),
        "w_proj_b": nrm(ks[13], (DEPTH, WIDTH_B, D_MODEL), WIDTH_B ** -0.5),
        "w_out": nrm(ks[14], (DEPTH, D_MODEL, D_MODEL), D_MODEL ** -0.5),
        "ffn_norm_g": 1.0 + nrm(ks[15], (DEPTH, D_MODEL), 0.02),
        "w_gate": nrm(ks[16], (DEPTH, D_MODEL, D_FF), D_MODEL ** -0.5),
        "w_up": nrm(ks[17], (DEPTH, D_MODEL, D_FF), D_MODEL ** -0.5),
        "w_down": nrm(ks[18], (DEPTH, D_FF, D_MODEL), D_FF ** -0.5),
        "final_norm_g": 1.0 + nrm(ks[19], (D_MODEL,), 0.02),
    }


def reference(x, attn_norm_g, w_in, lam_q1, lam_k1, lam_q2, lam_k2, subln_g, sgu_ln_g, sgu_ln_b,
              w_spatial, b_spatial, w_proj_a, w_proj_b, w_out, ffn_norm_g, w_gate, w_up, w_down,
              final_norm_g):
    B, S = x.shape[0], x.shape[1]
    inv_freq = ROPE_THETA ** (-jnp.arange(0, HEAD_DIM_A, 2, dtype=jnp.float32) / HEAD_DIM_A)
    ang = jnp.arange(S, dtype=jnp.float32)[:, None] * inv_freq[None, :]
    cos, sin = jnp.cos(ang), jnp.sin(ang)

    for l in range(DEPTH):
        h = rms_norm(x, attn_norm_g[l])
        z = h @ w_in[l]
        q, k, v_a, u_b, v_b, g_a, g_b = jnp.split(z, SPLITS, axis=-1)

        lam_init = 0.8 - 0.6 * math.exp(-0.3 * l)
        lam = (jnp.exp(jnp.sum(lam_q1[l].astype(jnp.float32) * lam_k1[l].astype(jnp.float32)))
               - jnp.exp(jnp.sum(lam_q2[l].astype(jnp.float32) * lam_k2[l].astype(jnp.float32)))
               + lam_init)
        q = rotary(q.reshape(B, S, N_HEADS_A, 2, HEAD_DIM_A), cos, sin)
        k = rotary(k.reshape(B, S, N_HEADS_A, 2, HEAD_DIM_A), cos, sin)
        o_a = diff_attention(q, k, v_a.reshape(B, S, N_HEADS_A, V_DIM_A), lam)
        o_a = (rms_norm(o_a, subln_g[l]) * (1.0 - lam_init)).reshape(B, S, WIDTH_A)

        o_b = spatial_gating(jax.nn.gelu(u_b, approximate=False), jax.nn.gelu(v_b, approximate=False),
                             w_spatial[l], b_spatial[l], sgu_ln_g[l], sgu_ln_b[l])

        y = jax.nn.sigmoid(g_a) * (o_a @ w_proj_a[l]) + jax.nn.sigmoid(g_b) * (o_b @ w_proj_b[l])
        x = x + y @ w_out[l]

        h = rms_norm(x, ffn_norm_g[l])
        x = x + (jax.nn.silu(h @ w_gate[l]) * (h @ w_up[l])) @ w_down[l]

    return rms_norm(x, final_norm_g)
```

```python
import math
from contextlib import ExitStack

import numpy as np
import concourse.bass as bass
import concourse.mybir as mybir
from concourse.bass_utils import run_bass_kernel_spmd

F32 = mybir.dt.float32
BF16 = mybir.dt.bfloat16
AF = mybir.ActivationFunctionType
ALU = mybir.AluOpType

S = 4096
D = 1024
NIN = 7168
DFF = 2816
NFC = DFF // 128
NH = 8
NT = S // 128
NST = S // 512
DEPTH = 4
EPS = 1e-6
N_CORES = 8

LAYERS_PER_LAUNCH = 4
DEBUG_SCRATCH = False


class Sched:
    def __init__(self, nc, es):
        self.nc = nc
        self.es = es
        self.eng = {"pe": nc.tensor, "act": nc.scalar, "dve": nc.vector, "pool": nc.gpsimd, "sp": nc.sync}
        self.sem = {e: es.enter_context(nc.semaphore("sem_" + e)) for e in self.eng}
        self.cnt = {e: 0 for e in self.eng}
        self.seen = {e: {} for e in self.eng}
        self.semobj = {}
        for e in self.eng:
            self.semobj[id(self.sem[e])] = self.sem[e]
        self.last_write = {}
        self.readers = {}
        self.dma_sems = {}
        self.n_inst = 0

    def _deps(self, reads, writes):
        deps = []
        for k in reads:
            t = self.last_write.get(k)
            if t is not None:
                deps.append(t)
        for k in writes:
            t = self.last_write.get(k)
            if t is not None:
                deps.append(t)
            r = self.readers.get(k)
            if r:
                deps.extend(r.values())
        return deps

    def _wait(self, engine, deps):
        need = {}
        for sem, val, src in deps:
            if src == engine and engine == "pe":
                continue
            sid = id(sem)
            if need.get(sid, (None, 0))[1] < val:
                need[sid] = (sem, val)
        seen = self.seen[engine]
        eo = self.eng[engine]
        for sid, (sem, val) in need.items():
            if seen.get(sid, 0) >= val:
                continue
            eo.wait_ge(sem, val)
            seen[sid] = val

    def _record(self, tok, reads, writes):
        for k in writes:
            self.last_write[k] = tok
            self.readers[k] = {}
        for k in reads:
            r = self.readers.setdefault(k, {})
            key = (id(tok[0]), tok[2])
            if key not in r or r[key][1] < tok[1]:
                r[key] = tok

    def op(self, engine, fns, reads=(), writes=()):
        if not isinstance(fns, (list, tuple)):
            fns = [fns]
        self._wait(engine, self._deps(reads, writes))
        eo = self.eng[engine]
        inst = None
        for fn in fns:
            inst = fn(eo)
            self.n_inst += 1
        self.cnt[engine] += 1
        inst.then_inc(self.sem[engine], 1)
        tok = (self.sem[engine], self.cnt[engine], engine)
        self._record(tok, reads, writes)

    def dma(self, queue, out, in_, reads=(), writes=(), sem=None):
        assert sem is not None
        self._wait(queue, self._deps(reads, writes))
        if sem not in self.dma_sems:
            self.dma_sems[sem] = [self.es.enter_context(self.nc.semaphore("dsem_" + sem)), 0]
        d = self.dma_sems[sem]
        inst = self.eng[queue].dma_start(out=out, in_=in_)
        d[1] += 16
        inst.then_inc(d[0], 16)
        self.n_inst += 1
        tok = (d[0], d[1], "dma:" + sem)
        self._record(tok, reads, writes)

    def barrier(self):
        for e, eo in self.eng.items():
            seen = self.seen[e]
            for e2 in self.eng:
                if e2 == e:
                    continue
                v = self.cnt[e2]
                sid = id(self.sem[e2])
                if v > 0 and seen.get(sid, 0) < v:
                    eo.wait_ge(self.sem[e2], v)
                    seen[sid] = v
            for name, (sem, v) in self.dma_sems.items():
                sid = id(sem)
                if v > 0 and seen.get(sid, 0) < v:
                    eo.wait_ge(sem, v)
                    seen[sid] = v
        self.last_write = {}
        self.readers = {}


def build_program(layers, final_norm, x_is_input=True):
    nc = bass.Bass("TRN2", target_bir_lowering=False)
    dt = nc.dram_tensor

    def din(name, shape, dtype=F32):
        return dt(name, list(shape), dtype, kind="ExternalInput").ap()

    x_in = din("x", [S, D])
    attn_norm_g = din("attn_norm_g", [DEPTH, D])
    w_in = din("w_in", [DEPTH, D, NIN])
    lam_q1 = din("lam_q1", [DEPTH, 64])
    lam_k1 = din("lam_k1", [DEPTH, 64])
    lam_q2 = din("lam_q2", [DEPTH, 64])
    lam_k2 = din("lam_k2", [DEPTH, 64])
    subln_g = din("subln_g", [DEPTH, 128])
    sgu_ln_g = din("sgu_ln_g", [DEPTH, D])
    sgu_ln_b = din("sgu_ln_b", [DEPTH, D])
    w_spatial = din("w_spatial", [DEPTH, 8, 128, 128])
    b_spatial = din("b_spatial", [DEPTH, 8, 128])
    w_proj_a = din("w_proj_a", [DEPTH, D, D])
    w_proj_b = din("w_proj_b", [DEPTH, D, D])
    w_out = din("w_out", [DEPTH, D, D])
    ffn_norm_g = din("ffn_norm_g", [DEPTH, D])
    w_gate = din("w_gate", [DEPTH, D, DFF])
    w_up = din("w_up", [DEPTH, D, DFF])
    w_down = din("w_down", [DEPTH, DFF, D])
    final_norm_g = din("final_norm_g", [1, D])
    cos_t = din("cos_t", [128, S])
    sin_t = din("sin_t", [128, S])
    pmat_in = din("pmat", [128, 128])
    ident_in = din("ident", [128, 128])

    y_out = dt("y", [S, D], F32, kind="ExternalOutput").ap()

    def dscr(name, shape, dtype):
        return dt(name, list(shape), dtype, kind=("ExternalOutput" if DEBUG_SCRATCH else "Internal")).ap()

    XA = dscr("XA", [S, D], F32)
    XB = dscr("XB", [S, D], F32)
    QT = dscr("QT", [NH, 128, S], BF16)
    KT = dscr("KT", [NH, 128, S], BF16)
    VV = dscr("VV", [S, D], BF16)
    UT = dscr("UT", [D, S], BF16)
    OBT = dscr("OBT", [D, S], BF16)
    SGA = dscr("SGA", [D, S], BF16)
    SGB = dscr("SGB", [D, S], BF16)
    OAT = dscr("OAT", [D, S], BF16)
    AT = dscr("AT", [DFF, S], BF16)

    def fm(ap):
        return ap.rearrange("(cc p) t -> p cc t", p=128)

    with ExitStack() as es:
        s = Sched(nc, es)

        sbn = [0]

        def sb(stack, name, shape, dtype):
            sbn[0] += 1
            return stack.enter_context(nc.sbuf_tensor("sb%d_%s" % (sbn[0], name), list(shape), dtype))

        psA = es.enter_context(nc.psum_tensor("psA", [128, 2, 512], F32))
        psB = es.enter_context(nc.psum_tensor("psB", [128, 2, 512], F32))
        psC = es.enter_context(nc.psum_tensor("psC", [128, 2, 512], F32))
        psT = [es.enter_context(nc.psum_tensor("psT%d" % i, [128, 8, 128], BF16)) for i in range(2)]
        psAB = [psA, psB]

        ident = sb(es, "ident", [128, 128], BF16)
        pmat = sb(es, "pmat", [128, 128], BF16)
        neghalf = sb(es, "neghalf", [128, 1], F32)
        ones_row = sb(es, "ones_row", [1, 128], F32)
        s.dma("pool", ident[:], ident_in, writes=["ident"], sem="c_ident")
        s.dma("pool", pmat[:], pmat_in, writes=["pmat"], sem="c_pmat")
        s.op("pool", lambda e: e.memset(neghalf[:], -0.5), writes=["neghalf"])
        s.op("pool", lambda e: e.memset(ones_row[:], 1.0), writes=["ones_row"])

        def load_w_cast(dst, dst_key, src2d, n_kc, sem, col_split=1):
            C = src2d.shape[1]
            cs = C // col_split
            for i in range(col_split):
                s.dma(
                    "pool",
                    dst[:, :, i * cs:(i + 1) * cs],
                    src2d[:, i * cs:(i + 1) * cs].rearrange("(kc p) c -> p kc c", p=128),
                    writes=[(dst_key, i)],
                    sem="%s_%d" % (sem, i),
                )
            return [(dst_key, i) for i in range(col_split)]

        def rstd_from_sumsq(sumsq_ap, var_ap, rstd_ap, n, keys_in, key_var, key_out):
            s.op("pool", lambda e: e.tensor_scalar(out=var_ap, in0=sumsq_ap, scalar1=1.0 / n, scalar2=EPS,
                                                   op0=ALU.mult, op1=ALU.add),
                 reads=keys_in, writes=[key_var])
            s.op("pool", lambda e: e.tensor_tensor(out=rstd_ap, in0=var_ap, in1=neghalf[:], op=ALU.pow),
                 reads=[key_var, "neghalf"], writes=[key_out])

        tcount = [0]

        def norm_transpose(xt_ap, xkey, gbc, gkey, hb, junk, stat, sidx, hT_dst, hkeys, ph):
            k = tcount[0] % 2
            tcount[0] += 1
            ssq = stat[:, 0, sidx:sidx + 1]
            var = stat[:, 1, sidx:sidx + 1]
            rstd = stat[:, 2, sidx:sidx + 1]
            s.op("act", lambda e: e.activation(out=junk[:], in_=xt_ap, func=AF.Square, accum_out=ssq),
                 reads=[xkey], writes=[(ph, "junk"), (ph, "ssq", sidx)])
            rstd_from_sumsq(ssq, var, rstd, D, [(ph, "ssq", sidx)], (ph, "var", sidx), (ph, "rstd", sidx))
            s.op("dve", lambda e: e.scalar_tensor_tensor(out=hb[k][:], in0=xt_ap, scalar=rstd, in1=gbc[:],
                                                         op0=ALU.mult, op1=ALU.mult),
                 reads=[xkey, (ph, "rstd", sidx), gkey], writes=[(ph, "hb", k)])
            pt = psT[k]
            s.op("pe", [(lambda e, kc=kc: e.transpose(pt[:, kc, :], hb[k][:, kc * 128:(kc + 1) * 128], ident[:]))
                        for kc in range(8)],
                 reads=[(ph, "hb", k), "ident"], writes=[("psT", k)])
            s.op("act", lambda e: e.copy(out=hT_dst, in_=pt[:]), reads=[("psT", k)], writes=hkeys)

        for li, l in enumerate(layers):
            lam_init = 0.8 - 0.6 * math.exp(-0.3 * l)
            if li == 0:
                x_src = x_in
            else:
                x_src = XA
            last = li == len(layers) - 1

            with ExitStack() as p1:
                hT = sb(p1, "hT", [128, 8, S], BF16)
                wF = [sb(p1, "wF%d" % i, [128, 8, 1024], BF16) for i in range(2)]

                def load_family(F):
                    return load_w_cast(wF[F % 2], ("wF", F % 2), w_in[l, :, F * 1024:(F + 1) * 1024], 8,
                                       "wF%d" % (F % 2))

                load_family(0)
                with ExitStack() as pa:
                    gbc = sb(pa, "gbc", [128, D], F32)
                    xs = [sb(pa, "xs%d" % i, [128, 4, D], F32) for i in range(2)]
                    hb = [sb(pa, "hb%d" % i, [128, D], BF16) for i in range(2)]
                    junk = sb(pa, "junk", [128, D], BF16)
                    stat = sb(pa, "stat", [128, 3, NT], F32)
                    s.dma("sp", gbc[:], attn_norm_g[l:l + 1, :].to_broadcast([128, D]), writes=["gbc"], sem="gbc")

                    def ldx(J):
                        s.dma("sp", xs[J % 2][:], x_src[J * 512:(J + 1) * 512, :].rearrange("(j p) c -> p j c", p=128),
                              reads=[("X", J)], writes=[("xs", J % 2)], sem="xs%d" % (J % 2))

                    ldx(0)
                    for J in range(NST):
                        if J + 1 < NST:
                            ldx(J + 1)
                        for j in range(4):
                            tt = J * 4 + j
                            norm_transpose(xs[J % 2][:, j, :], ("xs", J % 2), gbc, "gbc", hb, junk, stat, tt,
                                           hT[:, :, tt * 128:(tt + 1) * 128], [("hT", J)], "p1a")
                s.barrier()

                ucount = [0]

                with ExitStack() as pq:
                    cosb = sb(pq, "cosb", [128, S], F32)
                    sinb = sb(pq, "sinb", [128, S], F32)
                    qs = [sb(pq, "qs%d" % i, [128, 512], BF16) for i in range(2)]
                    t1 = [sb(pq, "t1_%d" % i, [128, 512], F32) for i in range(2)]
                    t2 = [sb(pq, "t2_%d" % i, [128, 512], F32) for i in range(2)]
                    qrot = [sb(pq, "qrot%d" % i, [128, 8, 512], BF16) for i in range(2)]
                    s.dma("sp", cosb[:], cos_t, writes=["cosb"], sem="cosb")
                    s.dma("sp", sinb[:], sin_t, writes=["sinb"], sem="sinb")
                    for F in (0, 1):
                        load_family(F + 1)
                        w = wF[F % 2]
                        dst = QT if F == 0 else KT
                        for J in range(NST):
                            r = (F * NST + J) % 2
                            tsl = slice(J * 512, (J + 1) * 512)
                            for h in range(NH):
                                a = ucount[0] % 2
                                ucount[0] += 1
                                s.op("pe", [(lambda e, kc=kc: e.matmul(psA[:, a, :], lhsT=w[:, kc, h * 128:(h + 1) * 128],
                                                                       rhs=hT[:, kc, tsl], start=(kc == 0), stop=(kc == 7)))
                                            for kc in range(8)],
                                     reads=[(("wF", F % 2), 0), ("hT", J)], writes=[("psA", a)])
                                s.op("act", lambda e: e.copy(out=qs[a][:], in_=psA[:, a, :]),
                                     reads=[("psA", a)], writes=[("qs", a)])
                                s.op("pe", lambda e: e.matmul(psB[:, a, :], lhsT=pmat[:], rhs=qs[a][:], start=True, stop=True),
                                     reads=[("qs", a), "pmat"], writes=[("psB", a)])
                                s.op("pool", lambda e: e.tensor_tensor(out=t1[a][:], in0=qs[a][:], in1=cosb[:, tsl], op=ALU.mult),
                                     reads=[("qs", a), "cosb"], writes=[("t1", a)])
                                s.op("dve", lambda e: e.tensor_tensor(out=t2[a][:], in0=psB[:, a, :], in1=sinb[:, tsl], op=ALU.mult),
                                     reads=[("psB", a), "sinb"], writes=[("t2", a)])
                                s.op("dve", lambda e: e.tensor_tensor(out=qrot[r][:, h, :], in0=t1[a][:], in1=t2[a][:], op=ALU.add),
                                     reads=[("t1", a), ("t2", a)], writes=[("qrot", r)])
                            s.dma("sp", dst.rearrange("h p t -> p h t")[:, :, tsl], qrot[r][:],
                                  reads=[("qrot", r)], writes=[("QK", F, J)], sem="qrot%d" % r)
                s.barrier()

                with ExitStack() as pv:
                    vs = [sb(pv, "vs%d" % i, [128, 4, D], BF16) for i in range(2)]
                    F = 2
                    load_family(F + 1)
                    w = wF[F % 2]
                    for J in range(NST):
                        r = J % 2
                        for j in range(4):
                            tt = J * 4 + j
                            ps = psAB[tt % 2]
                            s.op("pe", [(lambda e, kc=kc, c2=c2: e.matmul(ps[:, c2, :], lhsT=hT[:, kc, tt * 128:(tt + 1) * 128],
                                                                          rhs=w[:, kc, c2 * 512:(c2 + 1) * 512],
                                                                          start=(kc == 0), stop=(kc == 7)))
                                        for c2 in range(2) for kc in range(8)],
                                 reads=[(("wF", F % 2), 0), ("hT", J)], writes=[("psAB", tt % 2)])
                            s.op("act", lambda e: e.copy(out=vs[r][:, j, :], in_=ps[:].rearrange("p a b -> p (a b)")),
                                 reads=[("psAB", tt % 2)], writes=[("vs", r)])
                        s.dma("sp", VV[J * 512:(J + 1) * 512, :].rearrange("(j p) c -> p j c", p=128), vs[r][:],
                              reads=[("vs", r)], writes=[("VV", J)], sem="vs%d" % r)
                s.barrier()

                def fm_family(F, func, dst, stack_name):
                    with ExitStack() as pu:
                        us = [sb(pu, "%s%d" % (stack_name, i), [128, 8, 512], BF16) for i in range(2)]
                        if F + 1 < 7:
                            load_family(F + 1)
                        w = wF[F % 2]
                        for J in range(NST):
                            r = J % 2
                            tsl = slice(J * 512, (J + 1) * 512)
                            for cc in range(8):
                                a = ucount[0] % 2
                                ucount[0] += 1
                                s.op("pe", [(lambda e, kc=kc: e.matmul(psA[:, a, :], lhsT=w[:, kc, cc * 128:(cc + 1) * 128],
                                                                       rhs=hT[:, kc, tsl], start=(kc == 0), stop=(kc == 7)))
                                            for kc in range(8)],
                                     reads=[(("wF", F % 2), 0), ("hT", J)], writes=[("psA", a)])
                                s.op("act", lambda e: e.activation(out=us[r][:, cc, :], in_=psA[:, a, :], func=func),
                                     reads=[("psA", a)], writes=[("us", r)])
                            s.dma("sp", fm(dst)[:, :, tsl], us[r][:], reads=[("us", r)], writes=[("FM", F, J)],
                                  sem="us%d" % r)
                    s.barrier()

                fm_family(3, AF.Gelu, UT, "us")

                with ExitStack() as pb:
                    F = 4
                    load_family(F + 1)
                    w = wF[F % 2]
                    lng = sb(pb, "lng", [128, D], F32)
                    lnb = sb(pb, "lnb", [128, D], F32)
                    wsp = sb(pb, "wsp", [128, 8, 128], F32)
                    wspb = sb(pb, "wspb", [128, 8, 128], BF16)
                    wmT = sb(pb, "wmT", [128, 8, 128], BF16)
                    bsp = sb(pb, "bsp", [1, 8, 128], F32)
                    vg = [sb(pb, "vg%d" % i, [128, D], F32) for i in range(2)]
                    tln = sb(pb, "tln", [128, D], F32)
                    vh = [sb(pb, "vh%d" % i, [128, D], BF16) for i in range(2)]
                    ul = [sb(pb, "ul%d" % i, [128, 8, 512], BF16) for i in range(2)]
                    ob = [sb(pb, "ob%d" % i, [128, 8, 512], BF16) for i in range(2)]
                    bst = sb(pb, "bst", [128, NT, 2, 6], F32)
                    mv = sb(pb, "mv", [128, NT, 4], F32)
                    s.dma("sp", lng[:], sgu_ln_g[l:l + 1, :].to_broadcast([128, D]), writes=["lng"], sem="lng")
                    s.dma("sp", lnb[:], sgu_ln_b[l:l + 1, :].to_broadcast([128, D]), writes=["lnb"], sem="lnb")
                    s.dma("sp", wsp[:], w_spatial[l].rearrange("g t s -> t g s"), writes=["wsp"], sem="wsp")
                    s.dma("sp", bsp[:], b_spatial[l:l + 1, :, :], writes=["bsp"], sem="bsp")
                    s.op("dve", lambda e: e.tensor_copy(out=wspb[:], in_=wsp[:]), reads=["wsp"], writes=["wspb"])
                    s.op("pe", [(lambda e, g=g: e.transpose(psT[0][:, g, :], wspb[:, g, :], ident[:])) for g in range(8)],
                         reads=["wspb", "ident"], writes=[("psT", 0)])
                    s.op("dve", lambda e: e.tensor_copy(out=wmT[:], in_=psT[0][:]), reads=[("psT", 0)], writes=["wmT"])
                    s.op("dve", lambda e: e.memset(wmT[64:128, :, 0:64], 0.0), reads=[], writes=["wmT"])
                    pm = psC[:].rearrange("p a (g t) -> p (a g) t", t=128)

                    def ldu(J):
                        s.dma("sp", ul[J % 2][:], fm(UT)[:, :, J * 512:(J + 1) * 512], reads=[("FM", 3, J)],
                              writes=[("ul", J % 2)], sem="ul%d" % (J % 2))

                    ldu(0)
                    for J in range(NST):
                        if J + 1 < NST:
                            ldu(J + 1)
                        r = J % 2
                        for j in range(4):
                            tt = J * 4 + j
                            k = tt % 2
                            ps = psAB[k]
                            s.op("pe", [(lambda e, kc=kc, c2=c2: e.matmul(ps[:, c2, :], lhsT=hT[:, kc, tt * 128:(tt + 1) * 128],
                                                                          rhs=w[:, kc, c2 * 512:(c2 + 1) * 512],
                                                                          start=(kc == 0), stop=(kc == 7)))
                                        for c2 in range(2) for kc in range(8)],
                                 reads=[(("wF", F % 2), 0), ("hT", J)], writes=[("psAB", k)])
                            s.op("act", lambda e: e.activation(out=vg[k][:], in_=ps[:].rearrange("p a b -> p (a b)"), func=AF.Gelu),
                                 reads=[("psAB", k)], writes=[("vg", k)])
                            s.op("dve", [lambda e: e.bn_stats(out=bst[:, tt, 0, :], in_=vg[k][:, 0:512]),
                                         lambda e: e.bn_stats(out=bst[:, tt, 1, :], in_=vg[k][:, 512:1024])],
                                 reads=[("vg", k)], writes=[("bst", tt)])
                            s.op("dve", lambda e: e.bn_aggr(out=mv[:, tt, 0:2], in_=bst[:, tt, :, :].rearrange("p a b -> p (a b)")),
                                 reads=[("bst", tt)], writes=[("mv", tt)])
                            s.op("pool", lambda e: e.tensor_scalar(out=mv[:, tt, 2:3], in0=mv[:, tt, 1:2], scalar1=1.0, scalar2=EPS,
                                                                   op0=ALU.mult, op1=ALU.add),
                                 reads=[("mv", tt)], writes=[("mv2", tt)])
                            s.op("pool", lambda e: e.tensor_tensor(out=mv[:, tt, 3:4], in0=mv[:, tt, 2:3], in1=neghalf[:], op=ALU.pow),
                                 reads=[("mv2", tt), "neghalf"], writes=[("mv3", tt)])
                            s.op("dve", lambda e: e.scalar_tensor_tensor(out=tln[:], in0=vg[k][:], scalar=mv[:, tt, 0:1], in1=lng[:],
                                                                         op0=ALU.subtract, op1=ALU.mult),
                                 reads=[("vg", k), ("mv", tt), "lng"], writes=["tln"])
                            s.op("dve", lambda e: e.scalar_tensor_tensor(out=vh[k][:], in0=tln[:], scalar=mv[:, tt, 3:4], in1=lnb[:],
                                                                         op0=ALU.mult, op1=ALU.add),
                                 reads=["tln", ("mv3", tt), "lnb"], writes=[("vh", k)])
                            fns = []
                            for g in range(8):
                                fns.append(lambda e, g=g: e.matmul(pm[:, g, :], lhsT=vh[k][:, g * 128:(g + 1) * 128],
                                                                   rhs=wmT[:, g, :], start=True, stop=False))
                                fns.append(lambda e, g=g: e.matmul(pm[:, g, :], lhsT=ones_row[0:1, :], rhs=bsp[0:1, g, :],
                                                                   start=False, stop=True))
                            s.op("pe", fns, reads=[("vh", k), "wmT", "ones_row", "bsp"], writes=["psC"])
                            s.op("dve", lambda e: e.tensor_tensor(out=ob[r][:, :, j * 128:(j + 1) * 128], in0=pm,
                                                                  in1=ul[r][:, :, j * 128:(j + 1) * 128], op=ALU.mult),
                                 reads=["psC", ("ul", r)], writes=[("ob", r)])
                        s.dma("sp", fm(OBT)[:, :, J * 512:(J + 1) * 512], ob[r][:], reads=[("ob", r)],
                              writes=[("OBT", J)], sem="ob%d" % r)
                s.barrier()

                fm_family(5, AF.Sigmoid, SGA, "ga")
                fm_family(6, AF.Sigmoid, SGB, "gb")

            with ExitStack() as p2:
                KTh = [sb(p2, "KTh%d" % i, [128, S], BF16) for i in range(2)]
                QTh = [sb(p2, "QTh%d" % i, [128, S], BF16) for i in range(2)]
                Vh = [sb(p2, "Vh%d" % i, [128, NT, 129], BF16) for i in range(2)]
                OATh = [sb(p2, "OATh%d" % i, [128, S], BF16) for i in range(2)]
                ET = [sb(p2, "ET%d" % i, [128, 2, 512], BF16) for i in range(3)]
                lamv = sb(p2, "lamv", [128, 4, 64], F32)
                lamj = sb(p2, "lamj", [128, 64], F32)
                lams = sb(p2, "lams", [128, 8], F32)
                gsub = sb(p2, "gsub", [128, 128], F32)
                rz = sb(p2, "rz", [128, NT, 4], F32)
                tO = [sb(p2, "tO%d" % i, [128, 128], F32) for i in range(2)]
                oo = [sb(p2, "oo%d" % i, [128, 128], F32) for i in range(2)]
                ojunk = sb(p2, "ojunk", [128, 128], F32)
                onb = [sb(p2, "onb%d" % i, [128, 128], BF16) for i in range(2)]
                sst = sb(p2, "sst", [128, NT, 3], F32)

                for i in range(2):
                    s.op("pool", lambda e, i=i: e.memset(Vh[i][:, :, 128:129], 1.0), writes=[("Vones", i)])
                for i, src in enumerate((lam_q1, lam_k1, lam_q2, lam_k2)):
                    s.dma("sp", lamv[:, i, :], src[l:l + 1, :].to_broadcast([128, 64]), writes=[("lamv", i)], sem="lamv%d" % i)
                s.dma("sp", gsub[:], subln_g[l:l + 1, :].to_broadcast([128, 128]), writes=["gsub_raw"], sem="gsub")
                for di in range(2):
                    s.op("dve", lambda e, di=di: e.tensor_tensor(out=lamj[:], in0=lamv[:, 2 * di, :], in1=lamv[:, 2 * di + 1, :], op=ALU.mult),
                         reads=[("lamv", 2 * di), ("lamv", 2 * di + 1)], writes=["lamj"])
                    s.op("dve", lambda e, di=di: e.tensor_reduce(out=lams[:, di:di + 1], in_=lamj[:], axis=mybir.AxisListType.X, op=ALU.add),
                         reads=["lamj"], writes=["lam_d%d" % (di + 1)])
                s.op("act", lambda e: e.activation(out=lams[:, 2:4], in_=lams[:, 0:2], func=AF.Exp),
                     reads=["lam_d1", "lam_d2"], writes=["lam_e"])
                s.op("dve", lambda e: e.scalar_tensor_tensor(out=lams[:, 4:5], in0=lams[:, 3:4], scalar=-lam_init, in1=lams[:, 2:3],
                                                             op0=ALU.add, op1=ALU.subtract),
                     reads=["lam_e"], writes=["neglam"])
                s.op("dve", lambda e: e.tensor_scalar(out=gsub[:], in0=gsub[:], scalar1=(1.0 - lam_init), scalar2=None, op0=ALU.mult),
                     reads=["gsub_raw"], writes=["gsub"])
                neglam = lams[:, 4:5]

                def ld_head(h):
                    i = h % 2
                    s.dma("sp", KTh[i][:], KT[h], reads=[("QK", 1, J) for J in range(NST)], writes=[("KTh", i)], sem="KTh%d" % i)
                    s.dma("sp", QTh[i][:], QT[h], reads=[("QK", 0, J) for J in range(NST)], writes=[("QTh", i)], sem="QTh%d" % i)
                    s.dma("sp", Vh[i][:, :, 0:128], VV[:, h * 128:(h + 1) * 128].rearrange("(t p) e -> p t e", p=128),
                          reads=[("VV", J) for J in range(NST)] + [("Vones", i)], writes=[("Vh", i)], sem="Vh%d" % i)

                ld_head(0)
                gcount = 0
                qcount = 0
                for h in range(NH):
                    if h + 1 < NH:
                        ld_head(h + 1)
                    hi = h % 2
                    Kt, Qt, Vt, Ot = KTh[hi], QTh[hi], Vh[hi], OATh[hi]
                    for i in range(NT):
                        slot = qcount % 2
                        qcount += 1
                        qsl = slice(i * 128, (i + 1) * 128)
                        Oacc = [psC[:, br, slot * 129:(slot + 1) * 129] for br in range(2)]
                        ngroups = (i + 4) // 4
                        for gi in range(ngroups):
                            kbs = list(range(4 * gi, min(4 * gi + 4, i + 1)))
                            n = len(kbs)
                            sbi = gcount % 2
                            ebi = gcount % 3
                            gcount += 1
                            Sps = psAB[sbi]
                            fns = []
                            for kl, kb in enumerate(kbs):
                                for br in range(2):
                                    fns.append(lambda e, kl=kl, kb=kb, br=br: e.matmul(
                                        Sps[:, br, kl * 128:(kl + 1) * 128],
                                        lhsT=Kt[br * 64:(br + 1) * 64, kb * 128:(kb + 1) * 128],
                                        rhs=Qt[br * 64:(br + 1) * 64, qsl], start=True, stop=True))
                            s.op("pe", fns, reads=[("KTh", hi), ("QTh", hi)], writes=[("psAB", sbi)])
                            s.op("act", lambda e: e.activation(out=ET[ebi][:, :, 0:n * 128], in_=Sps[:, :, 0:n * 128],
                                                               func=AF.Exp, scale=0.125),
                                 reads=[("psAB", sbi)], writes=[("ET", ebi)])
                            if kbs[-1] == i:
                                kl = n - 1
                                s.op("pool", lambda e: e.memset(ET[ebi][64:128, :, kl * 128:kl * 128 + 64], 0.0),
                                     reads=[], writes=[("ET", ebi)])
                            fns = []
                            for kl, kb in enumerate(kbs):
                                for br in range(2):
                                    fns.append(lambda e, kl=kl, kb=kb, br=br: e.matmul(
                                        Oacc[br], lhsT=ET[ebi][:, br, kl * 128:(kl + 1) * 128], rhs=Vt[:, kb, :],
                                        start=(kb == 0), stop=(kb == i), skip_group_check=True))
                            s.op("pe", fns, reads=[("ET", ebi), ("Vh", hi)], writes=[("O", slot)])
                        s.op("dve", [lambda e: e.reciprocal(out=rz[:, i, 0:1], in_=Oacc[0][:, 128:129]),
                                     lambda e: e.reciprocal(out=rz[:, i, 1:2], in_=Oacc[1][:, 128:129])],
                             reads=[("O", slot)], writes=[("rz", i)])
                        s.op("dve", lambda e: e.tensor_tensor(out=rz[:, i, 2:3], in0=rz[:, i, 1:2], in1=neglam, op=ALU.mult),
                             reads=[("rz", i), "neglam"], writes=[("rz2", i)])
                        s.op("dve", lambda e: e.tensor_scalar(out=tO[slot][:], in0=Oacc[0][:, 0:128], scalar1=rz[:, i, 0:1], scalar2=None,
                                                              op0=ALU.mult),
                             reads=[("O", slot), ("rz", i)], writes=[("tO", slot)])
                        s.op("dve", lambda e: e.scalar_tensor_tensor(out=oo[slot][:], in0=Oacc[1][:, 0:128], scalar=rz[:, i, 2:3],
                                                                     in1=tO[slot][:], op0=ALU.mult, op1=ALU.add),
                             reads=[("O", slot), ("rz2", i), ("tO", slot)], writes=[("oo", slot)])
                        s.op("dve", lambda e: e.tensor_tensor(out=ojunk[:], in0=oo[slot][:], in1=oo[slot][:], op=ALU.mult),
                             reads=[("oo", slot)], writes=["ojunk"])
                        s.op("dve", lambda e: e.tensor_reduce(out=sst[:, i, 0:1], in_=ojunk[:], axis=mybir.AxisListType.X, op=ALU.add),
                             reads=["ojunk"], writes=[("sst", i)])
                        rstd_from_sumsq(sst[:, i, 0:1], sst[:, i, 1:2], sst[:, i, 2:3], 128, [("sst", i)], ("sst1", i), ("sst2", i))
                        s.op("dve", lambda e: e.scalar_tensor_tensor(out=onb[slot][:], in0=oo[slot][:], scalar=sst[:, i, 2:3], in1=gsub[:],
                                                                     op0=ALU.mult, op1=ALU.mult),
                             reads=[("oo", slot), ("sst2", i), "gsub"], writes=[("onb", slot)])
                        s.op("pe", lambda e: e.transpose(psT[slot][:, 0, :], onb[slot][:], ident[:]),
                             reads=[("onb", slot), "ident"], writes=[("psT", slot)])
                        s.op("dve", lambda e: e.tensor_copy(out=Ot[:, qsl], in_=psT[slot][:, 0, :]),
                             reads=[("psT", slot)], writes=[("OATh", hi)])
                    s.dma("sp", OAT[h * 128:(h + 1) * 128, :], Ot[:], reads=[("OATh", hi)], writes=[("OAT", h)], sem="OATh%d" % hi)
            s.barrier()

            with ExitStack() as p3:
                wpa = sb(p3, "wpa", [128, 8, D], BF16)
                wpb = sb(p3, "wpb", [128, 8, D], BF16)
                wo = sb(p3, "wo", [128, 8, D], BF16)
                oaT = [sb(p3, "oaT%d" % i, [128, 8, 512], BF16) for i in range(2)]
                obT = [sb(p3, "obT%d" % i, [128, 8, 512], BF16) for i in range(2)]
                sga = [sb(p3, "sga%d" % i, [128, 8, 512], BF16) for i in range(2)]
                sgb = [sb(p3, "sgb%d" % i, [128, 8, 512], BF16) for i in range(2)]
                xs3 = [sb(p3, "xs3_%d" % i, [128, 4, D], F32) for i in range(2)]
                yT = sb(p3, "yT", [128, 8, 512], BF16)
                ta = [sb(p3, "ta%d" % i, [128, 512], F32) for i in range(2)]
                tb = [sb(p3, "tb%d" % i, [128, 512], F32) for i in range(2)]
                load_w_cast(wpa, "wpa", w_proj_a[l], 8, "wpa")
                load_w_cast(wpb, "wpb", w_proj_b[l], 8, "wpb")
                load_w_cast(wo, "wo", w_out[l], 8, "wo")

                def ld3(J):
                    i = J % 2
                    tsl = slice(J * 512, (J + 1) * 512)
                    s.dma("sp", oaT[i][:], fm(OAT)[:, :, tsl], writes=[("oaT", i)], sem="oaT%d" % i)
                    s.dma("sp", obT[i][:], fm(OBT)[:, :, tsl], writes=[("obT", i)], sem="obT%d" % i)
                    s.dma("sp", sga[i][:], fm(SGA)[:, :, tsl], writes=[("sga", i)], sem="sga%d" % i)
                    s.dma("sp", sgb[i][:], fm(SGB)[:, :, tsl], writes=[("sgb", i)], sem="sgb%d" % i)
                    s.dma("sp", xs3[i][:], x_src[tsl, :].rearrange("(j p) c -> p j c", p=128), writes=[("xs3", i)], sem="xs3_%d" % i)

                ld3(0)
                uc = 0
                for J in range(NST):
                    if J + 1 < NST:
                        ld3(J + 1)
                    i = J % 2
                    for cc in range(8):
                        a = uc % 2
                        uc += 1
                        s.op("pe", [(lambda e, kc=kc: e.matmul(psA[:, a, :], lhsT=wpa[:, kc, cc * 128:(cc + 1) * 128], rhs=oaT[i][:, kc, :],
                                                               start=(kc == 0), stop=(kc == 7))) for kc in range(8)],
                             reads=[("wpa", 0), ("oaT", i)], writes=[("psA", a)])
                        s.op("pe", [(lambda e, kc=kc: e.matmul(psB[:, a, :], lhsT=wpb[:, kc, cc * 128:(cc + 1) * 128], rhs=obT[i][:, kc, :],
                                                               start=(kc == 0), stop=(kc == 7))) for kc in range(8)],
                             reads=[("wpb", 0), ("obT", i)], writes=[("psB", a)])
                        s.op("dve", lambda e: e.tensor_tensor(out=ta[a][:], in0=psA[:, a, :], in1=sga[i][:, cc, :], op=ALU.mult),
                             reads=[("psA", a), ("sga", i)], writes=[("ta", a)])
                        s.op("dve", lambda e: e.tensor_tensor(out=tb[a][:], in0=psB[:, a, :], in1=sgb[i][:, cc, :], op=ALU.mult),
                             reads=[("psB", a), ("sgb", i)], writes=[("tb", a)])
                        s.op("pool", lambda e: e.tensor_tensor(out=yT[:, cc, :], in0=ta[a][:], in1=tb[a][:], op=ALU.add),
                             reads=[("ta", a), ("tb", a)], writes=[("yT", cc)])
                    for j in range(4):
                        s.op("pe", [(lambda e, kc=kc, c2=c2: e.matmul(psC[:, c2, :], lhsT=yT[:, kc, j * 128:(j + 1) * 128],
                                                                      rhs=wo[:, kc, c2 * 512:(c2 + 1) * 512],
                                                                      start=(kc == 0), stop=(kc == 7)))
                                    for c2 in range(2) for kc in range(8)],
                             reads=[("wo", 0)] + [("yT", cc) for cc in range(8)], writes=["psC"])
                        s.op("dve", lambda e: e.tensor_tensor(out=xs3[i][:, j, :], in0=psC[:].rearrange("p a b -> p (a b)"),
                                                              in1=xs3[i][:, j, :], op=ALU.add),
                             reads=["psC", ("xs3", i)], writes=[("xs3", i)])
                    s.dma("sp", XB[J * 512:(J + 1) * 512, :].rearrange("(j p) c -> p j c", p=128), xs3[i][:],
                          reads=[("xs3", i)], writes=[("XB", J)], sem="xs3o_%d" % i)
            s.barrier()

            with ExitStack() as p4:
                wg = sb(p4, "wg", [128, 8, DFF], BF16)
                wu = sb(p4, "wu", [128, 8, DFF], BF16)
                gbc2 = sb(p4, "gbc2", [128, D], F32)
                xs4 = [sb(p4, "xs4_%d" % i, [128, 4, D], F32) for i in range(2)]
                hb4 = [sb(p4, "hb4_%d" % i, [128, D], BF16) for i in range(2)]
                junk4 = sb(p4, "junk4", [128, D], BF16)
                stat4 = sb(p4, "stat4", [128, 3, NT], F32)
                h2T = [sb(p4, "h2T%d" % i, [128, 8, 512], BF16) for i in range(2)]
                sgt = [sb(p4, "sgt%d" % i, [128, 512], F32) for i in range(2)]
                aT = [sb(p4, "aT%d" % i, [128, NFC, 512], BF16) for i in range(2)]
                s.dma("sp", gbc2[:], ffn_norm_g[l:l + 1, :].to_broadcast([128, D]), writes=["gbc2"], sem="gbc2")
                load_w_cast(wg, "wg", w_gate[l], 8, "wg", col_split=2)
                load_w_cast(wu, "wu", w_up[l], 8, "wu", col_split=2)

                def ld4(J):
                    s.dma("sp", xs4[J % 2][:], XB[J * 512:(J + 1) * 512, :].rearrange("(j p) c -> p j c", p=128),
                          writes=[("xs4", J % 2)], sem="xs4_%d" % (J % 2))

                ld4(0)
                uc = 0
                for J in range(NST):
                    if J + 1 < NST:
                        ld4(J + 1)
                    i = J % 2
                    for j in range(4):
                        norm_transpose(xs4[i][:, j, :], ("xs4", i), gbc2, "gbc2", hb4, junk4, stat4, J * 4 + j,
                                       h2T[i][:, :, j * 128:(j + 1) * 128], [("h2T", i)], "p4a")
                    for fc in range(NFC):
                        a = uc % 2
                        uc += 1
                        half = 0 if fc < NFC // 2 else 1
                        s.op("pe", [(lambda e, kc=kc: e.matmul(psA[:, a, :], lhsT=wg[:, kc, fc * 128:(fc + 1) * 128], rhs=h2T[i][:, kc, :],
                                                               start=(kc == 0), stop=(kc == 7))) for kc in range(8)],
                             reads=[("wg", half), ("h2T", i)], writes=[("psA", a)])
                        s.op("pe", [(lambda e, kc=kc: e.matmul(psB[:, a, :], lhsT=wu[:, kc, fc * 128:(fc + 1) * 128], rhs=h2T[i][:, kc, :],
                                                               start=(kc == 0), stop=(kc == 7))) for kc in range(8)],
                             reads=[("wu", half), ("h2T", i)], writes=[("psB", a)])
                        s.op("act", lambda e: e.activation(out=sgt[a][:], in_=psA[:, a, :], func=AF.Silu),
                             reads=[("psA", a)], writes=[("sgt", a)])
                        s.op("dve", lambda e: e.tensor_tensor(out=aT[i][:, fc, :], in0=psB[:, a, :], in1=sgt[a][:], op=ALU.mult),
                             reads=[("psB", a), ("sgt", a)], writes=[("aT", i)])
                    s.dma("sp", AT.rearrange("(fc p) t -> p fc t", p=128)[:, :, J * 512:(J + 1) * 512], aT[i][:],
                          reads=[("aT", i)], writes=[("AT", J)], sem="aT%d" % i)
            s.barrier()

            with ExitStack() as p5:
                wd = sb(p5, "wd", [128, NFC, D], BF16)
                aTl = [sb(p5, "aTl%d" % i, [128, NFC, 512], BF16) for i in range(2)]
                xs5 = [sb(p5, "xs5_%d" % i, [128, 4, D], F32) for i in range(2)]
                load_w_cast(wd, "wd", w_down[l], NFC, "wd", col_split=1)
                do_final = last and final_norm
                if do_final:
                    gfin = sb(p5, "gfin", [128, D], F32)
                    junk5 = sb(p5, "junk5", [128, D], BF16)
                    stat5 = sb(p5, "stat5", [128, 3, NT], F32)
                    s.dma("sp", gfin[:], final_norm_g[0:1, :].to_broadcast([128, D]), writes=["gfin"], sem="gfin")
                x_dst = y_out if last else XA

                def ld5(J):
                    i = J % 2
                    tsl = slice(J * 512, (J + 1) * 512)
                    s.dma("sp", aTl[i][:], AT.rearrange("(fc p) t -> p fc t", p=128)[:, :, tsl], writes=[("aTl", i)], sem="aTl%d" % i)
                    s.dma("sp", xs5[i][:], XB[tsl, :].rearrange("(j p) c -> p j c", p=128), writes=[("xs5", i)], sem="xs5_%d" % i)

                ld5(0)
                for J in range(NST):
                    if J + 1 < NST:
                        ld5(J + 1)
                    i = J % 2
                    for j in range(4):
                        tt = J * 4 + j
                        ps = psAB[tt % 2]
                        s.op("pe", [(lambda e, fc=fc, c2=c2: e.matmul(ps[:, c2, :], lhsT=aTl[i][:, fc, j * 128:(j + 1) * 128],
                                                                      rhs=wd[:, fc, c2 * 512:(c2 + 1) * 512],
                                                                      start=(fc == 0), stop=(fc == NFC - 1)))
                                    for c2 in range(2) for fc in range(NFC)],
                             reads=[("wd", 0), ("aTl", i)], writes=[("psAB", tt % 2)])
                        s.op("dve", lambda e: e.tensor_tensor(out=xs5[i][:, j, :], in0=ps[:].rearrange("p a b -> p (a b)"),
                                                              in1=xs5[i][:, j, :], op=ALU.add),
                             reads=[("psAB", tt % 2), ("xs5", i)], writes=[("xs5", i)])
                        if do_final:
                            ssq = stat5[:, 0, tt:tt + 1]
                            s.op("act", lambda e: e.activation(out=junk5[:], in_=xs5[i][:, j, :], func=AF.Square, accum_out=ssq),
                                 reads=[("xs5", i)], writes=["junk5", ("f_ssq", tt)])
                            rstd_from_sumsq(ssq, stat5[:, 1, tt:tt + 1], stat5[:, 2, tt:tt + 1], D, [("f_ssq", tt)],
                                            ("f_var", tt), ("f_rstd", tt))
                            s.op("dve", lambda e: e.scalar_tensor_tensor(out=xs5[i][:, j, :], in0=xs5[i][:, j, :],
                                                                         scalar=stat5[:, 2, tt:tt + 1], in1=gfin[:],
                                                                         op0=ALU.mult, op1=ALU.mult),
                                 reads=[("xs5", i), ("f_rstd", tt), "gfin"], writes=[("xs5", i)])
                    s.dma("sp", x_dst[J * 512:(J + 1) * 512, :].rearrange("(j p) c -> p j c", p=128), xs5[i][:],
                          reads=[("xs5", i)], writes=[("X", J)], sem="xs5o_%d" % i)
            s.barrier()
        print("instructions emitted:", s.n_inst)
    return nc


_CONST_CACHE = {}


def _consts():
    if "c" not in _CONST_CACHE:
        inv_freq = (10000.0 ** (-np.arange(0, 64, 2, dtype=np.float32) / np.float32(64))).astype(np.float32)
        ang = (np.arange(S, dtype=np.float32)[:, None] * inv_freq[None, :]).astype(np.float32)
        cos = np.cos(ang).astype(np.float32)
        sin = np.sin(ang).astype(np.float32)
        d = np.arange(128) % 64
        cos_t = np.ascontiguousarray(cos[:, d % 32].T)
        sgn = np.where(d < 32, -1.0, 1.0).astype(np.float32)
        sin_t = np.ascontiguousarray((sin[:, d % 32] * sgn[None, :]).T)
        pm = np.zeros((128, 128), np.float32)
        for p in range(128):
            partner = p + 32 if (p % 64) < 32 else p - 32
            pm[partner, p] = 1.0
        ident = np.eye(128, dtype=np.float32)
        _CONST_CACHE["c"] = dict(cos_t=cos_t, sin_t=sin_t, pmat=pm, ident=ident)
    return _CONST_CACHE["c"]


_PROG_CACHE = {}


def _get_prog(layers, final_norm):
    key = (tuple(layers), final_norm)
    if key not in _PROG_CACHE:
        _PROG_CACHE[key] = build_program(list(layers), final_norm)
    return _PROG_CACHE[key]


def kernel(x, attn_norm_g, w_in, lam_q1, lam_k1, lam_q2, lam_k2, subln_g, sgu_ln_g, sgu_ln_b,
           w_spatial, b_spatial, w_proj_a, w_proj_b, w_out, ffn_norm_g, w_gate, w_up, w_down, final_norm_g):
    f = lambda a: np.ascontiguousarray(np.asarray(a, dtype=np.float32))
    shared = dict(
        attn_norm_g=f(attn_norm_g), w_in=f(w_in), lam_q1=f(lam_q1), lam_k1=f(lam_k1), lam_q2=f(lam_q2), lam_k2=f(lam_k2),
        subln_g=f(subln_g), sgu_ln_g=f(sgu_ln_g), sgu_ln_b=f(sgu_ln_b), w_spatial=f(w_spatial), b_spatial=f(b_spatial),
        w_proj_a=f(w_proj_a), w_proj_b=f(w_proj_b), w_out=f(w_out), ffn_norm_g=f(ffn_norm_g), w_gate=f(w_gate),
        w_up=f(w_up), w_down=f(w_down), final_norm_g=f(final_norm_g).reshape(1, D),
    )
    shared.update(_consts())
    xcur = f(x)
    groups = [list(range(i, min(i + LAYERS_PER_LAUNCH, DEPTH))) for i in range(0, DEPTH, LAYERS_PER_LAUNCH)]
    for gi, layers in enumerate(groups):
        final = gi == len(groups) - 1
        nc = _get_prog(layers, final)
        in_maps = [dict(shared, x=np.ascontiguousarray(xcur[b])) for b in range(N_CORES)]
        res = run_bass_kernel_spmd(nc, in_maps, core_ids=list(range(N_CORES)))
        xcur = np.stack([np.asarray(res.results[b]["y"], dtype=np.float32) for b in range(N_CORES)], axis=0)
    return xcur
```

```python
import math
from contextlib import ExitStack

import numpy as np
import concourse.bass as bass
import concourse.mybir as mybir
from concourse.bass_utils import run_bass_kernel_spmd

F32 = mybir.dt.float32
BF16 = mybir.dt.bfloat16
AF = mybir.ActivationFunctionType
ALU = mybir.AluOpType

S = 4096
D = 1024
NIN = 7168
DFF = 2816
NFC = DFF // 128
NH = 8
NT = S // 128
NST = S // 512
DEPTH = 4
EPS = 1e-6
N_CORES = 8

LAYERS_PER_LAUNCH = 4
DEBUG_SCRATCH = False


class Sched:
    def __init__(self, nc, es):
        self.nc = nc
        self.es = es
        self.eng = {"pe": nc.tensor, "act": nc.scalar, "dve": nc.vector, "pool": nc.gpsimd, "sp": nc.sync}
        self.sem = {e: es.enter_context(nc.semaphore("sem_" + e)) for e in self.eng}
        self.cnt = {e: 0 for e in self.eng}
        self.seen = {e: {} for e in self.eng}
        self.semobj = {}
        for e in self.eng:
            self.semobj[id(self.sem[e])] = self.sem[e]
        self.last_write = {}
        self.readers = {}
        self.dma_sems = {}
        self.n_inst = 0

    def _deps(self, reads, writes):
        deps = []
        for k in reads:
            t = self.last_write.get(k)
            if t is not None:
                deps.append(t)
        for k in writes:
            t = self.last_write.get(k)
            if t is not None:
                deps.append(t)
            r = self.readers.get(k)
            if r:
                deps.extend(r.values())
        return deps

    def _wait(self, engine, deps):
        need = {}
        for sem, val, src in deps:
            if src == engine and engine == "pe":
                continue
            sid = id(sem)
            if need.get(sid, (None, 0))[1] < val:
                need[sid] = (sem, val)
        seen = self.seen[engine]
        eo = self.eng[engine]
        for sid, (sem, val) in need.items():
            if seen.get(sid, 0) >= val:
                continue
            eo.wait_ge(sem, val)
            seen[sid] = val

    def _record(self, tok, reads, writes):
        for k in writes:
            self.last_write[k] = tok
            self.readers[k] = {}
        for k in reads:
            r = self.readers.setdefault(k, {})
            key = (id(tok[0]), tok[2])
            if key not in r or r[key][1] < tok[1]:
                r[key] = tok

    @staticmethod
    def _banks(k):
        if k == "psC":
            return [("bk", 4), ("bk", 5)]
        if isinstance(k, tuple) and len(k) == 2:
            n, a = k
            if n == "psA":
                return [("bk", a)]
            if n == "psB":
                return [("bk", 2 + a)]
            if n == "psAB":
                return [("bk", 2 * a), ("bk", 2 * a + 1)]
            if n == "psC":
                return [("bk", 4 + a)]
            if n == "psT":
                return [("bk", 6 + a)]
        return None

    def _xl(self, reads, writes):
        r2, w2 = [], []
        for k in reads:
            b = self._banks(k)
            if b is None:
                r2.append(k)
            else:
                w2.extend(b)
        for k in writes:
            b = self._banks(k)
            if b is None:
                w2.append(k)
            else:
                w2.extend(b)
        return r2, w2

    def op(self, engine, fns, reads=(), writes=()):
        reads, writes = self._xl(reads, writes)
        if not isinstance(fns, (list, tuple)):
            fns = [fns]
        self._wait(engine, self._deps(reads, writes))
        eo = self.eng[engine]
        inst = None
        for fn in fns:
            inst = fn(eo)
            self.n_inst += 1
        self.cnt[engine] += 1
        inst.then_inc(self.sem[engine], 1)
        tok = (self.sem[engine], self.cnt[engine], engine)
        self._record(tok, reads, writes)

    def dma(self, queue, out, in_, reads=(), writes=(), sem=None):
        assert sem is not None
        self._wait(queue, self._deps(reads, writes))
        if sem not in self.dma_sems:
            self.dma_sems[sem] = [self.es.enter_context(self.nc.semaphore("dsem_" + sem)), 0]
        d = self.dma_sems[sem]
        inst = self.eng[queue].dma_start(out=out, in_=in_)
        d[1] += 16
        inst.then_inc(d[0], 16)
        self.n_inst += 1
        tok = (d[0], d[1], "dma:" + sem)
        self._record(tok, reads, writes)

    def barrier(self):
        for e, eo in self.eng.items():
            seen = self.seen[e]
            for e2 in self.eng:
                if e2 == e:
                    continue
                v = self.cnt[e2]
                sid = id(self.sem[e2])
                if v > 0 and seen.get(sid, 0) < v:
                    eo.wait_ge(self.sem[e2], v)
                    seen[sid] = v
            for name, (sem, v) in self.dma_sems.items():
                sid = id(sem)
                if v > 0 and seen.get(sid, 0) < v:
                    eo.wait_ge(sem, v)
                    seen[sid] = v
        self.last_write = {}
        self.readers = {}


def build_program(layers, final_norm, x_is_input=True):
    nc = bass.Bass("TRN2", target_bir_lowering=False)
    dt = nc.dram_tensor

    def din(name, shape, dtype=F32):
        return dt(name, list(shape), dtype, kind="ExternalInput").ap()

    x_in = din("x", [S, D])
    attn_norm_g = din("attn_norm_g", [DEPTH, D])
    w_in = din("w_in", [DEPTH, D, NIN])
    lam_q1 = din("lam_q1", [DEPTH, 64])
    lam_k1 = din("lam_k1", [DEPTH, 64])
    lam_q2 = din("lam_q2", [DEPTH, 64])
    lam_k2 = din("lam_k2", [DEPTH, 64])
    subln_g = din("subln_g", [DEPTH, 128])
    sgu_ln_g = din("sgu_ln_g", [DEPTH, D])
    sgu_ln_b = din("sgu_ln_b", [DEPTH, D])
    w_spatial = din("w_spatial", [DEPTH, 8, 128, 128])
    b_spatial = din("b_spatial", [DEPTH, 8, 128])
    w_proj_a = din("w_proj_a", [DEPTH, D, D])
    w_proj_b = din("w_proj_b", [DEPTH, D, D])
    w_out = din("w_out", [DEPTH, D, D])
    ffn_norm_g = din("ffn_norm_g", [DEPTH, D])
    w_gate = din("w_gate", [DEPTH, D, DFF])
    w_up = din("w_up", [DEPTH, D, DFF])
    w_down = din("w_down", [DEPTH, DFF, D])
    final_norm_g = din("final_norm_g", [1, D])
    cos_t = din("cos_t", [128, S])
    sin_t = din("sin_t", [128, S])
    pmat_in = din("pmat", [128, 128])
    ident_in = din("ident", [128, 128])

    y_out = dt("y", [S, D], F32, kind="ExternalOutput").ap()

    def dscr(name, shape, dtype):
        return dt(name, list(shape), dtype, kind=("ExternalOutput" if DEBUG_SCRATCH else "Internal")).ap()

    XA = dscr("XA", [S, D], F32)
    XB = dscr("XB", [S, D], F32)
    QT = dscr("QT", [NH, 128, S], BF16)
    KT = dscr("KT", [NH, 128, S], BF16)
    VV = dscr("VV", [S, D], BF16)
    UT = dscr("UT", [D, S], BF16)
    OBT = dscr("OBT", [D, S], BF16)
    SGA = dscr("SGA", [D, S], BF16)
    SGB = dscr("SGB", [D, S], BF16)
    OAT = dscr("OAT", [D, S], BF16)
    AT = dscr("AT", [DFF, S], BF16)

    def fm(ap):
        return ap.rearrange("(cc p) t -> p cc t", p=128)

    with ExitStack() as es:
        s = Sched(nc, es)

        sbn = [0]

        def sb(stack, name, shape, dtype):
            sbn[0] += 1
            return stack.enter_context(nc.sbuf_tensor("sb%d_%s" % (sbn[0], name), list(shape), dtype))

        psA = es.enter_context(nc.psum_tensor("psA", [128, 2, 512], F32))
        psB = es.enter_context(nc.psum_tensor("psB", [128, 2, 512], F32))
        psC = es.enter_context(nc.psum_tensor("psC", [128, 2, 512], F32))
        psT = [es.enter_context(nc.psum_tensor("psT%d" % i, [128, 8, 128], BF16)) for i in range(2)]
        psAB = [psA, psB]

        ident = sb(es, "ident", [128, 128], BF16)
        pmat = sb(es, "pmat", [128, 128], BF16)
        neghalf = sb(es, "neghalf", [128, 1], F32)
        ones_row = sb(es, "ones_row", [1, 128], F32)
        s.dma("pool", ident[:], ident_in, writes=["ident"], sem="c_ident")
        s.dma("pool", pmat[:], pmat_in, writes=["pmat"], sem="c_pmat")
        s.op("pool", lambda e: e.memset(neghalf[:], -0.5), writes=["neghalf"])
        s.op("pool", lambda e: e.memset(ones_row[:], 1.0), writes=["ones_row"])

        def load_w_cast(dst, dst_key, src2d, n_kc, sem, col_split=1):
            C = src2d.shape[1]
            cs = C // col_split
            for i in range(col_split):
                s.dma(
                    "pool",
                    dst[:, :, i * cs:(i + 1) * cs],
                    src2d[:, i * cs:(i + 1) * cs].rearrange("(kc p) c -> p kc c", p=128),
                    writes=[(dst_key, i)],
                    sem="%s_%d" % (sem, i),
                )
            return [(dst_key, i) for i in range(col_split)]

        def rstd_from_sumsq(sumsq_ap, var_ap, rstd_ap, n, keys_in, key_var, key_out):
            s.op("pool", lambda e: e.tensor_scalar(out=var_ap, in0=sumsq_ap, scalar1=1.0 / n, scalar2=EPS,
                                                   op0=ALU.mult, op1=ALU.add),
                 reads=keys_in, writes=[key_var])
            s.op("pool", lambda e: e.tensor_tensor(out=rstd_ap, in0=var_ap, in1=neghalf[:], op=ALU.pow),
                 reads=[key_var, "neghalf"], writes=[key_out])

        tcount = [0]

        def norm_front(xt_ap, xkey, gbc, gkey, hb, junk, stat, sidx, ph):
            k = tcount[0] % 2
            tcount[0] += 1
            ssq = stat[:, 0, sidx:sidx + 1]
            var = stat[:, 1, sidx:sidx + 1]
            rstd = stat[:, 2, sidx:sidx + 1]
            s.op("act", lambda e: e.activation(out=junk[:], in_=xt_ap, func=AF.Square, accum_out=ssq),
                 reads=[xkey], writes=[(ph, "junk"), (ph, "ssq", sidx)])
            rstd_from_sumsq(ssq, var, rstd, D, [(ph, "ssq", sidx)], (ph, "var", sidx), (ph, "rstd", sidx))
            s.op("dve", lambda e: e.scalar_tensor_tensor(out=hb[k][:], in0=xt_ap, scalar=rstd, in1=gbc[:],
                                                         op0=ALU.mult, op1=ALU.mult),
                 reads=[xkey, (ph, "rstd", sidx), gkey], writes=[(ph, "hb", k)])
            pt = psT[k]
            s.op("pe", [(lambda e, kc=kc: e.transpose(pt[:, kc, :], hb[k][:, kc * 128:(kc + 1) * 128], ident[:]))
                        for kc in range(8)],
                 reads=[(ph, "hb", k), "ident"], writes=[("psT", k)])
            return k

        def norm_back(k, hT_dst, hkeys):
            s.op("act", lambda e: e.copy(out=hT_dst, in_=psT[k][:]), reads=[("psT", k)], writes=hkeys)

        for li, l in enumerate(layers):
            lam_init = 0.8 - 0.6 * math.exp(-0.3 * l)
            if li == 0:
                x_src = x_in
            else:
                x_src = XA
            last = li == len(layers) - 1

            with ExitStack() as p1:
                hT = sb(p1, "hT", [128, 8, S], BF16)
                wF = [sb(p1, "wF%d" % i, [128, 8, 1024], BF16) for i in range(2)]

                def load_family(F):
                    return load_w_cast(wF[F % 2], ("wF", F % 2), w_in[l, :, F * 1024:(F + 1) * 1024], 8,
                                       "wF%d" % (F % 2))

                load_family(0)
                with ExitStack() as pa:
                    gbc = sb(pa, "gbc", [128, D], F32)
                    xs = [sb(pa, "xs%d" % i, [128, 4, D], F32) for i in range(2)]
                    hb = [sb(pa, "hb%d" % i, [128, D], BF16) for i in range(2)]
                    junk = sb(pa, "junk", [128, D], BF16)
                    stat = sb(pa, "stat", [128, 3, NT], F32)
                    s.dma("sp", gbc[:], attn_norm_g[l:l + 1, :].to_broadcast([128, D]), writes=["gbc"], sem="gbc")

                    def ldx(J):
                        s.dma("sp", xs[J % 2][:], x_src[J * 512:(J + 1) * 512, :].rearrange("(j p) c -> p j c", p=128),
                              reads=[("X", J)], writes=[("xs", J % 2)], sem="xs%d" % (J % 2))

                    ldx(0)
                    pend1 = []
                    for J in range(NST):
                        if J + 1 < NST:
                            ldx(J + 1)
                        for j in range(4):
                            tt = J * 4 + j
                            k_ = norm_front(xs[J % 2][:, j, :], ("xs", J % 2), gbc, "gbc", hb, junk, stat, tt, "p1a")
                            if pend1:
                                norm_back(*pend1.pop())
                            pend1.append((k_, hT[:, :, tt * 128:(tt + 1) * 128], [("hT", J)]))
                    norm_back(*pend1.pop())
                s.barrier()

                ucount = [0]

                with ExitStack() as pq:
                    cosb = sb(pq, "cosb", [128, S], F32)
                    sinb = sb(pq, "sinb", [128, S], F32)
                    qs = [sb(pq, "qs%d" % i, [128, 512], BF16) for i in range(2)]
                    t1 = [sb(pq, "t1_%d" % i, [128, 512], F32) for i in range(2)]
                    t2 = [sb(pq, "t2_%d" % i, [128, 512], F32) for i in range(2)]
                    qrot = [sb(pq, "qrot%d" % i, [128, 8, 512], BF16) for i in range(2)]
                    s.dma("sp", cosb[:], cos_t, writes=["cosb"], sem="cosb")
                    s.dma("sp", sinb[:], sin_t, writes=["sinb"], sem="sinb")
                    for F in (0, 1):
                        load_family(F + 1)
                        w = wF[F % 2]
                        dst = QT if F == 0 else KT
                        units = [(J, h) for J in range(NST) for h in range(NH)]

                        def qk_front(u):
                            J, h = units[u]
                            a = u % 2
                            tsl = slice(J * 512, (J + 1) * 512)
                            s.op("pe", [(lambda e, kc=kc: e.matmul(psA[:, a, :], lhsT=w[:, kc, h * 128:(h + 1) * 128],
                                                                   rhs=hT[:, kc, tsl], start=(kc == 0), stop=(kc == 7)))
                                        for kc in range(8)],
                                 reads=[(("wF", F % 2), 0), ("hT", J)], writes=[("psA", a)])
                            s.op("act", lambda e: e.copy(out=qs[a][:], in_=psA[:, a, :]),
                                 reads=[("psA", a)], writes=[("qs", a)])

                        def qk_back(u):
                            J, h = units[u]
                            a = u % 2
                            r = (F * NST + J) % 2
                            tsl = slice(J * 512, (J + 1) * 512)
                            s.op("pe", lambda e: e.matmul(psB[:, a, :], lhsT=pmat[:], rhs=qs[a][:], start=True, stop=True),
                                 reads=[("qs", a), "pmat"], writes=[("psB", a)])
                            s.op("pool", lambda e: e.tensor_tensor(out=t1[a][:], in0=qs[a][:], in1=cosb[:, tsl], op=ALU.mult),
                                 reads=[("qs", a), "cosb"], writes=[("t1", a)])
                            s.op("dve", lambda e: e.tensor_tensor(out=t2[a][:], in0=psB[:, a, :], in1=sinb[:, tsl], op=ALU.mult),
                                 reads=[("psB", a), "sinb"], writes=[("t2", a)])
                            s.op("dve", lambda e: e.tensor_tensor(out=qrot[r][:, h, :], in0=t1[a][:], in1=t2[a][:], op=ALU.add),
                                 reads=[("t1", a), ("t2", a)], writes=[("qrot", r)])
                            if h == NH - 1:
                                s.dma("sp", dst.rearrange("h p t -> p h t")[:, :, tsl], qrot[r][:],
                                      reads=[("qrot", r)], writes=[("QK", F, J)], sem="qrot%d" % r)

                        qk_front(0)
                        for u in range(len(units)):
                            if u + 1 < len(units):
                                qk_front(u + 1)
                            qk_back(u)
                s.barrier()

                with ExitStack() as pv:
                    vs = [sb(pv, "vs%d" % i, [128, 4, D], BF16) for i in range(2)]
                    F = 2
                    load_family(F + 1)
                    w = wF[F % 2]
                    for J in range(NST):
                        r = J % 2
                        for j in range(4):
                            tt = J * 4 + j
                            ps = psAB[tt % 2]
                            s.op("pe", [(lambda e, kc=kc, c2=c2: e.matmul(ps[:, c2, :], lhsT=hT[:, kc, tt * 128:(tt + 1) * 128],
                                                                          rhs=w[:, kc, c2 * 512:(c2 + 1) * 512],
                                                                          start=(kc == 0), stop=(kc == 7)))
                                        for c2 in range(2) for kc in range(8)],
                                 reads=[(("wF", F % 2), 0), ("hT", J)], writes=[("psAB", tt % 2)])
                            s.op("act", lambda e: e.copy(out=vs[r][:, j, :], in_=ps[:].rearrange("p a b -> p (a b)")),
                                 reads=[("psAB", tt % 2)], writes=[("vs", r)])
                        s.dma("sp", VV[J * 512:(J + 1) * 512, :].rearrange("(j p) c -> p j c", p=128), vs[r][:],
                              reads=[("vs", r)], writes=[("VV", J)], sem="vs%d" % r)
                s.barrier()

                def fm_family(F, func, dst, stack_name):
                    with ExitStack() as pu:
                        us = [sb(pu, "%s%d" % (stack_name, i), [128, 8, 512], BF16) for i in range(2)]
                        if F + 1 < 7:
                            load_family(F + 1)
                        w = wF[F % 2]
                        for J in range(NST):
                            r = J % 2
                            tsl = slice(J * 512, (J + 1) * 512)
                            for cc in range(8):
                                a = ucount[0] % 2
                                ucount[0] += 1
                                s.op("pe", [(lambda e, kc=kc: e.matmul(psA[:, a, :], lhsT=w[:, kc, cc * 128:(cc + 1) * 128],
                                                                       rhs=hT[:, kc, tsl], start=(kc == 0), stop=(kc == 7)))
                                            for kc in range(8)],
                                     reads=[(("wF", F % 2), 0), ("hT", J)], writes=[("psA", a)])
                                s.op("act", lambda e: e.activation(out=us[r][:, cc, :], in_=psA[:, a, :], func=func),
                                     reads=[("psA", a)], writes=[("us", r)])
                            s.dma("sp", fm(dst)[:, :, tsl], us[r][:], reads=[("us", r)], writes=[("FM", F, J)],
                                  sem="us%d" % r)
                    s.barrier()

                fm_family(3, AF.Gelu, UT, "us")

                with ExitStack() as pb:
                    F = 4
                    load_family(F + 1)
                    w = wF[F % 2]
                    lng = sb(pb, "lng", [128, D], F32)
                    lnb = sb(pb, "lnb", [128, D], F32)
                    wsp = sb(pb, "wsp", [128, 8, 128], F32)
                    wspb = sb(pb, "wspb", [128, 8, 128], BF16)
                    wmT = sb(pb, "wmT", [128, 8, 128], BF16)
                    bsp = sb(pb, "bsp", [1, 8, 128], F32)
                    vg = [sb(pb, "vg%d" % i, [128, D], F32) for i in range(2)]
                    tln = sb(pb, "tln", [128, D], F32)
                    vh = [sb(pb, "vh%d" % i, [128, D], BF16) for i in range(2)]
                    ul = [sb(pb, "ul%d" % i, [128, 8, 512], BF16) for i in range(2)]
                    ob = [sb(pb, "ob%d" % i, [128, 8, 512], BF16) for i in range(2)]
                    bst = sb(pb, "bst", [128, NT, 2, 6], F32)
                    mv = sb(pb, "mv", [128, NT, 4], F32)
                    s.dma("sp", lng[:], sgu_ln_g[l:l + 1, :].to_broadcast([128, D]), writes=["lng"], sem="lng")
                    s.dma("sp", lnb[:], sgu_ln_b[l:l + 1, :].to_broadcast([128, D]), writes=["lnb"], sem="lnb")
                    s.dma("sp", wsp[:], w_spatial[l].rearrange("g t s -> t g s"), writes=["wsp"], sem="wsp")
                    s.dma("sp", bsp[:], b_spatial[l:l + 1, :, :], writes=["bsp"], sem="bsp")
                    s.op("dve", lambda e: e.tensor_copy(out=wspb[:], in_=wsp[:]), reads=["wsp"], writes=["wspb"])
                    s.op("pe", [(lambda e, g=g: e.transpose(psT[0][:, g, :], wspb[:, g, :], ident[:])) for g in range(8)],
                         reads=["wspb", "ident"], writes=[("psT", 0)])
                    s.op("dve", lambda e: e.tensor_copy(out=wmT[:], in_=psT[0][:]), reads=[("psT", 0)], writes=["wmT"])
                    s.op("dve", lambda e: e.memset(wmT[64:128, :, 0:64], 0.0), reads=[], writes=["wmT"])
                    pm = psC[:].rearrange("p a (g t) -> p (a g) t", t=128)

                    def ldu(J):
                        s.dma("sp", ul[J % 2][:], fm(UT)[:, :, J * 512:(J + 1) * 512], reads=[("FM", 3, J)],
                              writes=[("ul", J % 2)], sem="ul%d" % (J % 2))

                    def vb_front(tt):
                        J = tt // 4
                        k = tt % 2
                        ps = psAB[k]
                        s.op("pe", [(lambda e, kc=kc, c2=c2: e.matmul(ps[:, c2, :], lhsT=hT[:, kc, tt * 128:(tt + 1) * 128],
                                                                      rhs=w[:, kc, c2 * 512:(c2 + 1) * 512],
                                                                      start=(kc == 0), stop=(kc == 7)))
                                    for c2 in range(2) for kc in range(8)],
                             reads=[(("wF", F % 2), 0), ("hT", J)], writes=[("psAB", k)])
                        s.op("act", lambda e: e.activation(out=vg[k][:], in_=ps[:].rearrange("p a b -> p (a b)"), func=AF.Gelu),
                             reads=[("psAB", k)], writes=[("vg", k)])
                        s.op("dve", [lambda e: e.bn_stats(out=bst[:, tt, 0, :], in_=vg[k][:, 0:512]),
                                     lambda e: e.bn_stats(out=bst[:, tt, 1, :], in_=vg[k][:, 512:1024])],
                             reads=[("vg", k)], writes=[("bst", tt)])
                        s.op("dve", lambda e: e.bn_aggr(out=mv[:, tt, 0:2], in_=bst[:, tt, :, :].rearrange("p a b -> p (a b)")),
                             reads=[("bst", tt)], writes=[("mv", tt)])
                        s.op("pool", lambda e: e.tensor_scalar(out=mv[:, tt, 2:3], in0=mv[:, tt, 1:2], scalar1=1.0, scalar2=EPS,
                                                               op0=ALU.mult, op1=ALU.add),
                             reads=[("mv", tt)], writes=[("mv2", tt)])
                        s.op("pool", lambda e: e.tensor_tensor(out=mv[:, tt, 3:4], in0=mv[:, tt, 2:3], in1=neghalf[:], op=ALU.pow),
                             reads=[("mv2", tt), "neghalf"], writes=[("mv3", tt)])
                        s.op("dve", lambda e: e.scalar_tensor_tensor(out=tln[:], in0=vg[k][:], scalar=mv[:, tt, 0:1], in1=lng[:],
                                                                     op0=ALU.subtract, op1=ALU.mult),
                             reads=[("vg", k), ("mv", tt), "lng"], writes=["tln"])
                        s.op("dve", lambda e: e.scalar_tensor_tensor(out=vh[k][:], in0=tln[:], scalar=mv[:, tt, 3:4], in1=lnb[:],
                                                                     op0=ALU.mult, op1=ALU.add),
                             reads=["tln", ("mv3", tt), "lnb"], writes=[("vh", k)])

                    def vb_back(tt):
                        J, j = tt // 4, tt % 4
                        k = tt % 2
                        r = J % 2
                        fns = []
                        for g in range(8):
                            fns.append(lambda e, g=g: e.matmul(pm[:, g, :], lhsT=vh[k][:, g * 128:(g + 1) * 128],
                                                               rhs=wmT[:, g, :], start=True, stop=False))
                            fns.append(lambda e, g=g: e.matmul(pm[:, g, :], lhsT=ones_row[0:1, :], rhs=bsp[0:1, g, :],
                                                               start=False, stop=True))
                        s.op("pe", fns, reads=[("vh", k), "wmT", "ones_row", "bsp"], writes=["psC"])
                        s.op("dve", lambda e: e.tensor_tensor(out=ob[r][:, :, j * 128:(j + 1) * 128], in0=pm,
                                                              in1=ul[r][:, :, j * 128:(j + 1) * 128], op=ALU.mult),
                             reads=["psC", ("ul", r)], writes=[("ob", r)])
                        if j == 3:
                            s.dma("sp", fm(OBT)[:, :, J * 512:(J + 1) * 512], ob[r][:], reads=[("ob", r)],
                                  writes=[("OBT", J)], sem="ob%d" % r)

                    ldu(0)
                    vb_front(0)
                    for tt in range(NT):
                        if tt % 4 == 0 and tt // 4 + 1 < NST:
                            ldu(tt // 4 + 1)
                        if tt + 1 < NT:
                            vb_front(tt + 1)
                        vb_back(tt)
                s.barrier()

                fm_family(5, AF.Sigmoid, SGA, "ga")
                fm_family(6, AF.Sigmoid, SGB, "gb")

            with ExitStack() as p2:
                KTh = [sb(p2, "KTh%d" % i, [128, S], BF16) for i in range(2)]
                QTh = [sb(p2, "QTh%d" % i, [128, S], BF16) for i in range(2)]
                Vh = [sb(p2, "Vh%d" % i, [128, NT, 129], BF16) for i in range(2)]
                OATh = [sb(p2, "OATh%d" % i, [128, S], BF16) for i in range(2)]
                ET = [sb(p2, "ET%d" % i, [128, 2, 512], BF16) for i in range(3)]
                lamv = sb(p2, "lamv", [128, 4, 64], F32)
                lamj = sb(p2, "lamj", [128, 64], F32)
                lams = sb(p2, "lams", [128, 8], F32)
                gsub = sb(p2, "gsub", [128, 128], F32)
                rz = sb(p2, "rz", [128, NT, 4], F32)
                tO = [sb(p2, "tO%d" % i, [128, 128], F32) for i in range(2)]
                oo = [sb(p2, "oo%d" % i, [128, 128], F32) for i in range(2)]
                ojunk = sb(p2, "ojunk", [128, 128], F32)
                onb = [sb(p2, "onb%d" % i, [128, 128], BF16) for i in range(8)]
                sst = sb(p2, "sst", [128, NT, 3], F32)

                for i in range(2):
                    s.op("pool", lambda e, i=i: e.memset(Vh[i][:, :, 128:129], 1.0), writes=[("Vones", i)])
                for i, src in enumerate((lam_q1, lam_k1, lam_q2, lam_k2)):
                    s.dma("sp", lamv[:, i, :], src[l:l + 1, :].to_broadcast([128, 64]), writes=[("lamv", i)], sem="lamv%d" % i)
                s.dma("sp", gsub[:], subln_g[l:l + 1, :].to_broadcast([128, 128]), writes=["gsub_raw"], sem="gsub")
                for di in range(2):
                    s.op("dve", lambda e, di=di: e.tensor_tensor(out=lamj[:], in0=lamv[:, 2 * di, :], in1=lamv[:, 2 * di + 1, :], op=ALU.mult),
                         reads=[("lamv", 2 * di), ("lamv", 2 * di + 1)], writes=["lamj"])
                    s.op("dve", lambda e, di=di: e.tensor_reduce(out=lams[:, di:di + 1], in_=lamj[:], axis=mybir.AxisListType.X, op=ALU.add),
                         reads=["lamj"], writes=["lam_d%d" % (di + 1)])
                s.op("act", lambda e: e.activation(out=lams[:, 2:4], in_=lams[:, 0:2], func=AF.Exp),
                     reads=["lam_d1", "lam_d2"], writes=["lam_e"])
                s.op("dve", lambda e: e.scalar_tensor_tensor(out=lams[:, 4:5], in0=lams[:, 3:4], scalar=-lam_init, in1=lams[:, 2:3],
                                                             op0=ALU.add, op1=ALU.subtract),
                     reads=["lam_e"], writes=["neglam"])
                s.op("dve", lambda e: e.tensor_scalar(out=gsub[:], in0=gsub[:], scalar1=(1.0 - lam_init), scalar2=None, op0=ALU.mult),
                     reads=["gsub_raw"], writes=["gsub"])
                neglam = lams[:, 4:5]

                def ld_head(h):
                    i = h % 2
                    s.dma("sp", KTh[i][:], KT[h], reads=[("QK", 1, J) for J in range(NST)], writes=[("KTh", i)], sem="KTh%d" % i)
                    s.dma("sp", QTh[i][:], QT[h], reads=[("QK", 0, J) for J in range(NST)], writes=[("QTh", i)], sem="QTh%d" % i)
                    s.dma("sp", Vh[i][:, :, 0:128], VV[:, h * 128:(h + 1) * 128].rearrange("(t p) e -> p t e", p=128),
                          reads=[("VV", J) for J in range(NST)] + [("Vones", i)], writes=[("Vh", i)], sem="Vh%d" % i)

                groups = []
                for h in range(NH):
                    for i in range(NT):
                        for gi in range((i + 4) // 4):
                            groups.append((h, i, gi, list(range(4 * gi, min(4 * gi + 4, i + 1)))))
                NG = len(groups)
                NSLOT = 3
                NONB = 8
                TDEFER = 6

                def oslot(h, i):
                    return (h * NT + i) % 2

                def Oacc(h, i, br):
                    return psC[:, oslot(h, i), br * 129:(br + 1) * 129]

                def emit_G(idx):
                    h, i, gi, kbs = groups[idx]
                    hi = h % 2
                    Kt, Qt = KTh[hi], QTh[hi]
                    n = len(kbs)
                    sbi = idx % 2
                    ebi = idx % 3
                    Sps = psAB[sbi]
                    qsl = slice(i * 128, (i + 1) * 128)
                    fns = []
                    for kl, kb in enumerate(kbs):
                        for br in range(2):
                            fns.append(lambda e, kl=kl, kb=kb, br=br: e.matmul(
                                Sps[:, br, kl * 128:(kl + 1) * 128],
                                lhsT=Kt[br * 64:(br + 1) * 64, kb * 128:(kb + 1) * 128],
                                rhs=Qt[br * 64:(br + 1) * 64, qsl], start=True, stop=True))
                    s.op("pe", fns, reads=[("KTh", hi), ("QTh", hi)], writes=[("psAB", sbi)])
                    s.op("act", lambda e: e.activation(out=ET[ebi][:, :, 0:n * 128], in_=Sps[:, :, 0:n * 128],
                                                       func=AF.Exp, scale=0.125),
                         reads=[("psAB", sbi)], writes=[("ET", ebi)])
                    if kbs[-1] == i:
                        kl = n - 1
                        s.op("pool", lambda e: e.memset(ET[ebi][64:128, :, kl * 128:kl * 128 + 64], 0.0),
                             reads=[], writes=[("ET", ebi)])

                def emit_A(idx):
                    h, i, gi, kbs = groups[idx]
                    hi = h % 2
                    Vt = Vh[hi]
                    ebi = idx % 3
                    fns = []
                    for kl, kb in enumerate(kbs):
                        for br in range(2):
                            fns.append(lambda e, kl=kl, kb=kb, br=br: e.matmul(
                                Oacc(h, i, br), lhsT=ET[ebi][:, br, kl * 128:(kl + 1) * 128], rhs=Vt[:, kb, :],
                                start=(kb == 0 and br == 0), stop=(kb == i), skip_group_check=True))
                    s.op("pe", fns, reads=[("ET", ebi), ("Vh", hi)], writes=[("psC", oslot(h, i))])

                def emit_F(h, i):
                    sl = oslot(h, i)
                    ts = (h * NT + i) % 2
                    ob_ = (h * NT + i) % NONB
                    O0, O1 = Oacc(h, i, 0), Oacc(h, i, 1)
                    s.op("dve", [lambda e: e.reciprocal(out=rz[:, i, 0:1], in_=O0[:, 128:129]),
                                 lambda e: e.reciprocal(out=rz[:, i, 1:2], in_=O1[:, 128:129])],
                         reads=[("psC", sl)], writes=[("rz", i)])
                    s.op("dve", lambda e: e.tensor_tensor(out=rz[:, i, 2:3], in0=rz[:, i, 1:2], in1=neglam, op=ALU.mult),
                         reads=[("rz", i), "neglam"], writes=[("rz2", i)])
                    s.op("dve", lambda e: e.tensor_scalar(out=tO[ts][:], in0=O0[:, 0:128], scalar1=rz[:, i, 0:1], scalar2=None,
                                                          op0=ALU.mult),
                         reads=[("psC", sl), ("rz", i)], writes=[("tO", ts)])
                    s.op("dve", lambda e: e.scalar_tensor_tensor(out=oo[ts][:], in0=O1[:, 0:128], scalar=rz[:, i, 2:3],
                                                                 in1=tO[ts][:], op0=ALU.mult, op1=ALU.add),
                         reads=[("psC", sl), ("rz2", i), ("tO", ts)], writes=[("oo", ts)])
                    s.op("dve", lambda e: e.tensor_tensor(out=ojunk[:], in0=oo[ts][:], in1=oo[ts][:], op=ALU.mult),
                         reads=[("oo", ts)], writes=["ojunk"])
                    s.op("dve", lambda e: e.tensor_reduce(out=sst[:, i, 0:1], in_=ojunk[:], axis=mybir.AxisListType.X, op=ALU.add),
                         reads=["ojunk"], writes=[("sst", i)])
                    rstd_from_sumsq(sst[:, i, 0:1], sst[:, i, 1:2], sst[:, i, 2:3], 128, [("sst", i)], ("sst1", i), ("sst2", i))
                    s.op("dve", lambda e: e.scalar_tensor_tensor(out=onb[ob_][:], in0=oo[ts][:], scalar=sst[:, i, 2:3], in1=gsub[:],
                                                                 op0=ALU.mult, op1=ALU.mult),
                         reads=[("oo", ts), ("sst2", i), "gsub"], writes=[("onb", ob_)])

                def emit_T(h, i):
                    hi = h % 2
                    ob_ = (h * NT + i) % NONB
                    tpar = (h * NT + i) % 2
                    pt = psT[tpar][:, 0, :]
                    s.op("pe", lambda e: e.transpose(pt, onb[ob_][:], ident[:]),
                         reads=[("onb", ob_), "ident"], writes=[("psT", tpar)])
                    s.op("dve", lambda e: e.tensor_copy(out=OATh[hi][:, i * 128:(i + 1) * 128], in_=pt),
                         reads=[("psT", tpar)], writes=[("OATh", hi)])
                    if i == NT - 1:
                        s.dma("sp", OAT[h * 128:(h + 1) * 128, :], OATh[hi][:], reads=[("OATh", hi)], writes=[("OAT", h)],
                              sem="OATh%d" % hi)

                ld_head(0)
                ld_head(1)
                pendingT = []
                emit_G(0)
                for idx in range(NG):
                    if idx + 1 < NG:
                        emit_G(idx + 1)
                    emit_A(idx)
                    h, i, gi, kbs = groups[idx]
                    if kbs[-1] == i:
                        emit_F(h, i)
                        pendingT.append((idx + TDEFER, h, i))
                        if i == NT - 1 and h + 2 < NH:
                            ld_head(h + 2)
                    while pendingT and pendingT[0][0] <= idx:
                        _, th, ti = pendingT.pop(0)
                        emit_T(th, ti)
                for _, th, ti in pendingT:
                    emit_T(th, ti)
            s.barrier()

            with ExitStack() as p3:
                wpa = sb(p3, "wpa", [128, 8, D], BF16)
                wpb = sb(p3, "wpb", [128, 8, D], BF16)
                wo = sb(p3, "wo", [128, 8, D], BF16)
                oaT = [sb(p3, "oaT%d" % i, [128, 8, 512], BF16) for i in range(2)]
                obT = [sb(p3, "obT%d" % i, [128, 8, 512], BF16) for i in range(2)]
                sga = [sb(p3, "sga%d" % i, [128, 8, 512], BF16) for i in range(2)]
                sgb = [sb(p3, "sgb%d" % i, [128, 8, 512], BF16) for i in range(2)]
                xs3 = [sb(p3, "xs3_%d" % i, [128, 4, D], F32) for i in range(2)]
                yT = sb(p3, "yT", [128, 8, 512], BF16)
                ta = [sb(p3, "ta%d" % i, [128, 512], F32) for i in range(2)]
                tb = [sb(p3, "tb%d" % i, [128, 512], F32) for i in range(2)]
                load_w_cast(wpa, "wpa", w_proj_a[l], 8, "wpa")
                load_w_cast(wpb, "wpb", w_proj_b[l], 8, "wpb")
                load_w_cast(wo, "wo", w_out[l], 8, "wo")

                def ld3(J):
                    i = J % 2
                    tsl = slice(J * 512, (J + 1) * 512)
                    s.dma("sp", oaT[i][:], fm(OAT)[:, :, tsl], writes=[("oaT", i)], sem="oaT%d" % i)
                    s.dma("sp", obT[i][:], fm(OBT)[:, :, tsl], writes=[("obT", i)], sem="obT%d" % i)
                    s.dma("sp", sga[i][:], fm(SGA)[:, :, tsl], writes=[("sga", i)], sem="sga%d" % i)
                    s.dma("sp", sgb[i][:], fm(SGB)[:, :, tsl], writes=[("sgb", i)], sem="sgb%d" % i)
                    s.dma("sp", xs3[i][:], x_src[tsl, :].rearrange("(j p) c -> p j c", p=128), writes=[("xs3", i)], sem="xs3_%d" % i)

                ld3(0)
                uc = 0
                for J in range(NST):
                    if J + 1 < NST:
                        ld3(J + 1)
                    i = J % 2
                    for cc in range(8):
                        a = uc % 2
                        uc += 1
                        s.op("pe", [(lambda e, kc=kc: e.matmul(psA[:, a, :], lhsT=wpa[:, kc, cc * 128:(cc + 1) * 128], rhs=oaT[i][:, kc, :],
                                                               start=(kc == 0), stop=(kc == 7))) for kc in range(8)],
                             reads=[("wpa", 0), ("oaT", i)], writes=[("psA", a)])
                        s.op("pe", [(lambda e, kc=kc: e.matmul(psB[:, a, :], lhsT=wpb[:, kc, cc * 128:(cc + 1) * 128], rhs=obT[i][:, kc, :],
                                                               start=(kc == 0), stop=(kc == 7))) for kc in range(8)],
                             reads=[("wpb", 0), ("obT", i)], writes=[("psB", a)])
                        s.op("dve", lambda e: e.tensor_tensor(out=ta[a][:], in0=psA[:, a, :], in1=sga[i][:, cc, :], op=ALU.mult),
                             reads=[("psA", a), ("sga", i)], writes=[("ta", a)])
                        s.op("dve", lambda e: e.tensor_tensor(out=tb[a][:], in0=psB[:, a, :], in1=sgb[i][:, cc, :], op=ALU.mult),
                             reads=[("psB", a), ("sgb", i)], writes=[("tb", a)])
                        s.op("pool", lambda e: e.tensor_tensor(out=yT[:, cc, :], in0=ta[a][:], in1=tb[a][:], op=ALU.add),
                             reads=[("ta", a), ("tb", a)], writes=[("yT", cc)])
                    for j in range(4):
                        for c2 in range(2):
                            s.op("pe", [(lambda e, kc=kc: e.matmul(psC[:, c2, :], lhsT=yT[:, kc, j * 128:(j + 1) * 128],
                                                                   rhs=wo[:, kc, c2 * 512:(c2 + 1) * 512],
                                                                   start=(kc == 0), stop=(kc == 7)))
                                        for kc in range(8)],
                                 reads=[("wo", 0)] + [("yT", cc) for cc in range(8)], writes=[("psC", c2)])
                            s.op("dve", lambda e: e.tensor_tensor(out=xs3[i][:, j, c2 * 512:(c2 + 1) * 512], in0=psC[:, c2, :],
                                                                  in1=xs3[i][:, j, c2 * 512:(c2 + 1) * 512], op=ALU.add),
                                 reads=[("psC", c2), ("xs3", i)], writes=[("xs3", i)])
                    s.dma("sp", XB[J * 512:(J + 1) * 512, :].rearrange("(j p) c -> p j c", p=128), xs3[i][:],
                          reads=[("xs3", i)], writes=[("XB", J)], sem="xs3o_%d" % i)
            s.barrier()

            with ExitStack() as p4:
                wg = sb(p4, "wg", [128, 8, DFF], BF16)
                wu = sb(p4, "wu", [128, 8, DFF], BF16)
                gbc2 = sb(p4, "gbc2", [128, D], F32)
                xs4 = [sb(p4, "xs4_%d" % i, [128, 4, D], F32) for i in range(2)]
                hb4 = [sb(p4, "hb4_%d" % i, [128, D], BF16) for i in range(2)]
                junk4 = sb(p4, "junk4", [128, D], BF16)
                stat4 = sb(p4, "stat4", [128, 3, NT], F32)
                h2T = [sb(p4, "h2T%d" % i, [128, 8, 512], BF16) for i in range(2)]
                sgt = [sb(p4, "sgt%d" % i, [128, 512], F32) for i in range(2)]
                aT = [sb(p4, "aT%d" % i, [128, NFC, 512], BF16) for i in range(2)]
                s.dma("sp", gbc2[:], ffn_norm_g[l:l + 1, :].to_broadcast([128, D]), writes=["gbc2"], sem="gbc2")
                load_w_cast(wg, "wg", w_gate[l], 8, "wg", col_split=2)
                load_w_cast(wu, "wu", w_up[l], 8, "wu", col_split=2)

                def ld4(J):
                    s.dma("sp", xs4[J % 2][:], XB[J * 512:(J + 1) * 512, :].rearrange("(j p) c -> p j c", p=128),
                          writes=[("xs4", J % 2)], sem="xs4_%d" % (J % 2))

                ld4(0)
                uc = 0
                for J in range(NST):
                    if J + 1 < NST:
                        ld4(J + 1)
                    i = J % 2
                    pend4 = []
                    for j in range(4):
                        k_ = norm_front(xs4[i][:, j, :], ("xs4", i), gbc2, "gbc2", hb4, junk4, stat4, J * 4 + j, "p4a")
                        if pend4:
                            norm_back(*pend4.pop())
                        pend4.append((k_, h2T[i][:, :, j * 128:(j + 1) * 128], [("h2T", i)]))
                    norm_back(*pend4.pop())
                    for fc in range(NFC):
                        a = uc % 2
                        uc += 1
                        half = 0 if fc < NFC // 2 else 1
                        s.op("pe", [(lambda e, kc=kc: e.matmul(psA[:, a, :], lhsT=wg[:, kc, fc * 128:(fc + 1) * 128], rhs=h2T[i][:, kc, :],
                                                               start=(kc == 0), stop=(kc == 7))) for kc in range(8)],
                             reads=[("wg", half), ("h2T", i)], writes=[("psA", a)])
                        s.op("pe", [(lambda e, kc=kc: e.matmul(psB[:, a, :], lhsT=wu[:, kc, fc * 128:(fc + 1) * 128], rhs=h2T[i][:, kc, :],
                                                               start=(kc == 0), stop=(kc == 7))) for kc in range(8)],
                             reads=[("wu", half), ("h2T", i)], writes=[("psB", a)])
                        s.op("act", lambda e: e.activation(out=sgt[a][:], in_=psA[:, a, :], func=AF.Silu),
                             reads=[("psA", a)], writes=[("sgt", a)])
                        s.op("dve", lambda e: e.tensor_tensor(out=aT[i][:, fc, :], in0=psB[:, a, :], in1=sgt[a][:], op=ALU.mult),
                             reads=[("psB", a), ("sgt", a)], writes=[("aT", i)])
                    s.dma("sp", AT.rearrange("(fc p) t -> p fc t", p=128)[:, :, J * 512:(J + 1) * 512], aT[i][:],
                          reads=[("aT", i)], writes=[("AT", J)], sem="aT%d" % i)
            s.barrier()

            with ExitStack() as p5:
                wd = sb(p5, "wd", [128, NFC, D], BF16)
                aTl = [sb(p5, "aTl%d" % i, [128, NFC, 512], BF16) for i in range(2)]
                xs5 = [sb(p5, "xs5_%d" % i, [128, 4, D], F32) for i in range(2)]
                load_w_cast(wd, "wd", w_down[l], NFC, "wd", col_split=1)
                do_final = last and final_norm
                if do_final:
                    gfin = sb(p5, "gfin", [128, D], F32)
                    junk5 = sb(p5, "junk5", [128, D], BF16)
                    stat5 = sb(p5, "stat5", [128, 3, NT], F32)
                    s.dma("sp", gfin[:], final_norm_g[0:1, :].to_broadcast([128, D]), writes=["gfin"], sem="gfin")
                x_dst = y_out if last else XA

                def ld5(J):
                    i = J % 2
                    tsl = slice(J * 512, (J + 1) * 512)
                    s.dma("sp", aTl[i][:], AT.rearrange("(fc p) t -> p fc t", p=128)[:, :, tsl], writes=[("aTl", i)], sem="aTl%d" % i)
                    s.dma("sp", xs5[i][:], XB[tsl, :].rearrange("(j p) c -> p j c", p=128), writes=[("xs5", i)], sem="xs5_%d" % i)

                ld5(0)
                for J in range(NST):
                    if J + 1 < NST:
                        ld5(J + 1)
                    i = J % 2
                    for j in range(4):
                        tt = J * 4 + j
                        ps = psAB[tt % 2]
                        s.op("pe", [(lambda e, fc=fc, c2=c2: e.matmul(ps[:, c2, :], lhsT=aTl[i][:, fc, j * 128:(j + 1) * 128],
                                                                      rhs=wd[:, fc, c2 * 512:(c2 + 1) * 512],
                                                                      start=(fc == 0), stop=(fc == NFC - 1)))
                                    for c2 in range(2) for fc in range(NFC)],
                             reads=[("wd", 0), ("aTl", i)], writes=[("psAB", tt % 2)])
                        s.op("dve", lambda e: e.tensor_tensor(out=xs5[i][:, j, :], in0=ps[:].rearrange("p a b -> p (a b)"),
                                                              in1=xs5[i][:, j, :], op=ALU.add),
                             reads=[("psAB", tt % 2), ("xs5", i)], writes=[("xs5", i)])
                        if do_final:
                            ssq = stat5[:, 0, tt:tt + 1]
                            s.op("act", lambda e: e.activation(out=junk5[:], in_=xs5[i][:, j, :], func=AF.Square, accum_out=ssq),
                                 reads=[("xs5", i)], writes=["junk5", ("f_ssq", tt)])
                            rstd_from_sumsq(ssq, stat5[:, 1, tt:tt + 1], stat5[:, 2, tt:tt + 1], D, [("f_ssq", tt)],
                                            ("f_var", tt), ("f_rstd", tt))
                            s.op("dve", lambda e: e.scalar_tensor_tensor(out=xs5[i][:, j, :], in0=xs5[i][:, j, :],
                                                                         scalar=stat5[:, 2, tt:tt + 1], in1=gfin[:],
                                                                         op0=ALU.mult, op1=ALU.mult),
                                 reads=[("xs5", i), ("f_rstd", tt), "gfin"], writes=[("xs5", i)])
                    s.dma("sp", x_dst[J * 512:(J + 1) * 512, :].rearrange("(j p) c -> p j c", p=128), xs5[i][:],
                          reads=[("xs5", i)], writes=[("X", J)], sem="xs5o_%d" % i)
            s.barrier()
        print("instructions emitted:", s.n_inst)
    return nc


_CONST_CACHE = {}


def _consts():
    if "c" not in _CONST_CACHE:
        inv_freq = (10000.0 ** (-np.arange(0, 64, 2, dtype=np.float32) / np.float32(64))).astype(np.float32)
        ang = (np.arange(S, dtype=np.float32)[:, None] * inv_freq[None, :]).astype(np.float32)
        cos = np.cos(ang).astype(np.float32)
        sin = np.sin(ang).astype(np.float32)
        d = np.arange(128) % 64
        cos_t = np.ascontiguousarray(cos[:, d % 32].T)
        sgn = np.where(d < 32, -1.0, 1.0).astype(np.float32)
        sin_t = np.ascontiguousarray((sin[:, d % 32] * sgn[None, :]).T)
        pm = np.zeros((128, 128), np.float32)
        for p in range(128):
            partner = p + 32 if (p % 64) < 32 else p - 32
            pm[partner, p] = 1.0
        ident = np.eye(128, dtype=np.float32)
        _CONST_CACHE["c"] = dict(cos_t=cos_t, sin_t=sin_t, pmat=pm, ident=ident)
    return _CONST_CACHE["c"]


_PROG_CACHE = {}


def _get_prog(layers, final_norm):
    key = (tuple(layers), final_norm)
    if key not in _PROG_CACHE:
        _PROG_CACHE[key] = build_program(list(layers), final_norm)
    return _PROG_CACHE[key]


def kernel(x, attn_norm_g, w_in, lam_q1, lam_k1, lam_q2, lam_k2, subln_g, sgu_ln_g, sgu_ln_b,
           w_spatial, b_spatial, w_proj_a, w_proj_b, w_out, ffn_norm_g, w_gate, w_up, w_down, final_norm_g):
    f = lambda a: np.ascontiguousarray(np.asarray(a, dtype=np.float32))
    shared = dict(
        attn_norm_g=f(attn_norm_g), w_in=f(w_in), lam_q1=f(lam_q1), lam_k1=f(lam_k1), lam_q2=f(lam_q2), lam_k2=f(lam_k2),
        subln_g=f(subln_g), sgu_ln_g=f(sgu_ln_g), sgu_ln_b=f(sgu_ln_b), w_spatial=f(w_spatial), b_spatial=f(b_spatial),
        w_proj_a=f(w_proj_a), w_proj_b=f(w_proj_b), w_out=f(w_out), ffn_norm_g=f(ffn_norm_g), w_gate=f(w_gate),
        w_up=f(w_up), w_down=f(w_down), final_norm_g=f(final_norm_g).reshape(1, D),
    )
    shared.update(_consts())
    xcur = f(x)
    groups = [list(range(i, min(i + LAYERS_PER_LAUNCH, DEPTH))) for i in range(0, DEPTH, LAYERS_PER_LAUNCH)]
    for gi, layers in enumerate(groups):
        final = gi == len(groups) - 1
        nc = _get_prog(layers, final)
        in_maps = [dict(shared, x=np.ascontiguousarray(xcur[b])) for b in range(N_CORES)]
        res = run_bass_kernel_spmd(nc, in_maps, core_ids=list(range(N_CORES)))
        xcur = np.stack([np.asarray(res.results[b]["y"], dtype=np.float32) for b in range(N_CORES)], axis=0)
    return xcur
```

```python
import math
from contextlib import ExitStack

import numpy as np
import concourse.bass as bass
import concourse.mybir as mybir
from concourse.bass_utils import run_bass_kernel_spmd

F32 = mybir.dt.float32
BF16 = mybir.dt.bfloat16
AF = mybir.ActivationFunctionType
ALU = mybir.AluOpType

S = 4096
D = 1024
NIN = 7168
DFF = 2816
NFC = DFF // 128
NH = 8
NT = S // 128
NST = S // 512
DEPTH = 4
EPS = 1e-6
N_CORES = 8

LAYERS_PER_LAUNCH = 4
DEBUG_SCRATCH = False


class Sched:
    def __init__(self, nc, es):
        self.nc = nc
        self.es = es
        self.eng = {"pe": nc.tensor, "act": nc.scalar, "dve": nc.vector, "pool": nc.gpsimd, "sp": nc.sync}
        self.sem = {e: es.enter_context(nc.semaphore("sem_" + e)) for e in self.eng}
        self.cnt = {e: 0 for e in self.eng}
        self.seen = {e: {} for e in self.eng}
        self.semobj = {}
        for e in self.eng:
            self.semobj[id(self.sem[e])] = self.sem[e]
        self.last_write = {}
        self.readers = {}
        self.dma_sems = {}
        self.n_inst = 0

    def _deps(self, reads, writes):
        deps = []
        for k in reads:
            t = self.last_write.get(k)
            if t is not None:
                deps.append(t)
        for k in writes:
            t = self.last_write.get(k)
            if t is not None:
                deps.append(t)
            r = self.readers.get(k)
            if r:
                deps.extend(r.values())
        return deps

    def _wait(self, engine, deps):
        need = {}
        for sem, val, src in deps:
            if src == engine and engine == "pe":
                continue
            sid = id(sem)
            if need.get(sid, (None, 0))[1] < val:
                need[sid] = (sem, val)
        seen = self.seen[engine]
        eo = self.eng[engine]
        for sid, (sem, val) in need.items():
            if seen.get(sid, 0) >= val:
                continue
            eo.wait_ge(sem, val)
            seen[sid] = val

    def _record(self, tok, reads, writes):
        for k in writes:
            self.last_write[k] = tok
            self.readers[k] = {}
        for k in reads:
            r = self.readers.setdefault(k, {})
            key = (id(tok[0]), tok[2])
            if key not in r or r[key][1] < tok[1]:
                r[key] = tok

    @staticmethod
    def _banks(k):
        if k == "psC":
            return [("bk", 4), ("bk", 5)]
        if isinstance(k, tuple) and len(k) == 2:
            n, a = k
            if n == "psA":
                return [("bk", a)]
            if n == "psB":
                return [("bk", 2 + a)]
            if n == "psAB":
                return [("bk", 2 * a), ("bk", 2 * a + 1)]
            if n == "psC":
                return [("bk", 4 + a)]
            if n == "psT":
                return [("bk", 6 + a)]
        return None

    def _xl(self, reads, writes):
        r2, w2 = [], []
        for k in reads:
            b = self._banks(k)
            if b is None:
                r2.append(k)
            else:
                w2.extend(b)
        for k in writes:
            b = self._banks(k)
            if b is None:
                w2.append(k)
            else:
                w2.extend(b)
        return r2, w2

    def op(self, engine, fns, reads=(), writes=()):
        reads, writes = self._xl(reads, writes)
        if not isinstance(fns, (list, tuple)):
            fns = [fns]
        self._wait(engine, self._deps(reads, writes))
        eo = self.eng[engine]
        inst = None
        for fn in fns:
            inst = fn(eo)
            self.n_inst += 1
        self.cnt[engine] += 1
        inst.then_inc(self.sem[engine], 1)
        tok = (self.sem[engine], self.cnt[engine], engine)
        self._record(tok, reads, writes)

    def dma(self, queue, out, in_, reads=(), writes=(), sem=None):
        assert sem is not None
        self._wait(queue, self._deps(reads, writes))
        if sem not in self.dma_sems:
            self.dma_sems[sem] = [self.es.enter_context(self.nc.semaphore("dsem_" + sem)), 0]
        d = self.dma_sems[sem]
        inst = self.eng[queue].dma_start(out=out, in_=in_)
        d[1] += 16
        inst.then_inc(d[0], 16)
        self.n_inst += 1
        tok = (d[0], d[1], "dma:" + sem)
        self._record(tok, reads, writes)

    def barrier(self):
        for e, eo in self.eng.items():
            seen = self.seen[e]
            for e2 in self.eng:
                if e2 == e:
                    continue
                v = self.cnt[e2]
                sid = id(self.sem[e2])
                if v > 0 and seen.get(sid, 0) < v:
                    eo.wait_ge(self.sem[e2], v)
                    seen[sid] = v
            for name, (sem, v) in self.dma_sems.items():
                sid = id(sem)
                if v > 0 and seen.get(sid, 0) < v:
                    eo.wait_ge(sem, v)
                    seen[sid] = v
        self.last_write = {}
        self.readers = {}


def build_program(layers, final_norm, x_is_input=True):
    nc = bass.Bass("TRN2", target_bir_lowering=False)
    dt = nc.dram_tensor

    def din(name, shape, dtype=F32):
        return dt(name, list(shape), dtype, kind="ExternalInput").ap()

    x_in = din("x", [S, D])
    attn_norm_g = din("attn_norm_g", [DEPTH, D])
    w_in = din("w_in", [DEPTH, D, NIN])
    lam_q1 = din("lam_q1", [DEPTH, 64])
    lam_k1 = din("lam_k1", [DEPTH, 64])
    lam_q2 = din("lam_q2", [DEPTH, 64])
    lam_k2 = din("lam_k2", [DEPTH, 64])
    subln_g = din("subln_g", [DEPTH, 128])
    sgu_ln_g = din("sgu_ln_g", [DEPTH, D])
    sgu_ln_b = din("sgu_ln_b", [DEPTH, D])
    w_spatial = din("w_spatial", [DEPTH, 8, 128, 128])
    b_spatial = din("b_spatial", [DEPTH, 8, 128])
    w_proj_a = din("w_proj_a", [DEPTH, D, D])
    w_proj_b = din("w_proj_b", [DEPTH, D, D])
    w_out = din("w_out", [DEPTH, D, D])
    ffn_norm_g = din("ffn_norm_g", [DEPTH, D])
    w_gate = din("w_gate", [DEPTH, D, DFF])
    w_up = din("w_up", [DEPTH, D, DFF])
    w_down = din("w_down", [DEPTH, DFF, D])
    final_norm_g = din("final_norm_g", [1, D])
    cos_t = din("cos_t", [128, S])
    sin_t = din("sin_t", [128, S])
    pmat_in = din("pmat", [128, 128])
    ident_in = din("ident", [128, 128])

    y_out = dt("y", [S, D], F32, kind="ExternalOutput").ap()

    def dscr(name, shape, dtype):
        return dt(name, list(shape), dtype, kind=("ExternalOutput" if DEBUG_SCRATCH else "Internal")).ap()

    XA = dscr("XA", [S, D], F32)
    XB = dscr("XB", [S, D], F32)
    QT = dscr("QT", [NH, 128, S], BF16)
    KT = dscr("KT", [NH, 128, S], BF16)
    VV = dscr("VV", [S, D], BF16)
    UT = dscr("UT", [D, S], BF16)
    OBT = dscr("OBT", [D, S], BF16)
    SGA = dscr("SGA", [D, S], BF16)
    SGB = dscr("SGB", [D, S], BF16)
    OAT = dscr("OAT", [D, S], BF16)
    AT = dscr("AT", [DFF, S], BF16)

    def fm(ap):
        return ap.rearrange("(cc p) t -> p cc t", p=128)

    with ExitStack() as es:
        s = Sched(nc, es)

        sbn = [0]

        def sb(stack, name, shape, dtype):
            sbn[0] += 1
            return stack.enter_context(nc.sbuf_tensor("sb%d_%s" % (sbn[0], name), list(shape), dtype))

        psA = es.enter_context(nc.psum_tensor("psA", [128, 2, 512], F32))
        psB = es.enter_context(nc.psum_tensor("psB", [128, 2, 512], F32))
        psC = es.enter_context(nc.psum_tensor("psC", [128, 2, 512], F32))
        psT = [es.enter_context(nc.psum_tensor("psT%d" % i, [128, 8, 128], BF16)) for i in range(2)]
        psAB = [psA, psB]

        ident = sb(es, "ident", [128, 128], BF16)
        pmat = sb(es, "pmat", [128, 128], BF16)
        neghalf = sb(es, "neghalf", [128, 1], F32)
        ones_row = sb(es, "ones_row", [1, 128], F32)
        s.dma("pool", ident[:], ident_in, writes=["ident"], sem="c_ident")
        s.dma("pool", pmat[:], pmat_in, writes=["pmat"], sem="c_pmat")
        s.op("pool", lambda e: e.memset(neghalf[:], -0.5), writes=["neghalf"])
        s.op("pool", lambda e: e.memset(ones_row[:], 1.0), writes=["ones_row"])

        def load_w_cast(dst, dst_key, src2d, n_kc, sem, col_split=1):
            C = src2d.shape[1]
            cs = C // col_split
            for i in range(col_split):
                s.dma(
                    "pool",
                    dst[:, :, i * cs:(i + 1) * cs],
                    src2d[:, i * cs:(i + 1) * cs].rearrange("(kc p) c -> p kc c", p=128),
                    writes=[(dst_key, i)],
                    sem="%s_%d" % (sem, i),
                )
            return [(dst_key, i) for i in range(col_split)]

        def rstd_from_sumsq(sumsq_ap, var_ap, rstd_ap, n, keys_in, key_var, key_out):
            s.op("pool", lambda e: e.tensor_scalar(out=var_ap, in0=sumsq_ap, scalar1=1.0 / n, scalar2=EPS,
                                                   op0=ALU.mult, op1=ALU.add),
                 reads=keys_in, writes=[key_var])
            s.op("pool", lambda e: e.tensor_tensor(out=rstd_ap, in0=var_ap, in1=neghalf[:], op=ALU.pow),
                 reads=[key_var, "neghalf"], writes=[key_out])

        tcount = [0]

        def norm_front(xt_ap, xkey, gbc, gkey, hb, junk, stat, sidx, ph):
            k = tcount[0] % 2
            tcount[0] += 1
            ssq = stat[:, 0, sidx:sidx + 1]
            var = stat[:, 1, sidx:sidx + 1]
            rstd = stat[:, 2, sidx:sidx + 1]
            s.op("act", lambda e: e.activation(out=junk[:], in_=xt_ap, func=AF.Square, accum_out=ssq),
                 reads=[xkey], writes=[(ph, "junk"), (ph, "ssq", sidx)])
            rstd_from_sumsq(ssq, var, rstd, D, [(ph, "ssq", sidx)], (ph, "var", sidx), (ph, "rstd", sidx))
            s.op("dve", lambda e: e.scalar_tensor_tensor(out=hb[k][:], in0=xt_ap, scalar=rstd, in1=gbc[:],
                                                         op0=ALU.mult, op1=ALU.mult),
                 reads=[xkey, (ph, "rstd", sidx), gkey], writes=[(ph, "hb", k)])
            pt = psT[k]
            s.op("pe", [(lambda e, kc=kc: e.transpose(pt[:, kc, :], hb[k][:, kc * 128:(kc + 1) * 128], ident[:]))
                        for kc in range(8)],
                 reads=[(ph, "hb", k), "ident"], writes=[("psT", k)])
            return k

        def norm_back(k, hT_dst, hkeys):
            s.op("act", lambda e: e.copy(out=hT_dst, in_=psT[k][:]), reads=[("psT", k)], writes=hkeys)

        for li, l in enumerate(layers):
            lam_init = 0.8 - 0.6 * math.exp(-0.3 * l)
            if li == 0:
                x_src = x_in
            else:
                x_src = XA
            last = li == len(layers) - 1

            with ExitStack() as p1:
                hT = sb(p1, "hT", [128, 8, S], BF16)
                wF = [sb(p1, "wF%d" % i, [128, 8, 1024], BF16) for i in range(2)]

                def load_family(F):
                    return load_w_cast(wF[F % 2], ("wF", F % 2), w_in[l, :, F * 1024:(F + 1) * 1024], 8,
                                       "wF%d" % (F % 2))

                load_family(0)
                with ExitStack() as pa:
                    cosb = sb(pa, "cosb", [128, S], F32)
                    sinb = sb(pa, "sinb", [128, S], F32)
                    qs = [sb(pa, "qs%d" % i, [128, 512], BF16) for i in range(2)]
                    t1 = [sb(pa, "t1_%d" % i, [128, 512], F32) for i in range(2)]
                    t2 = [sb(pa, "t2_%d" % i, [128, 512], F32) for i in range(2)]
                    qrot = [sb(pa, "qrot%d" % i, [128, 8, 512], BF16) for i in range(2)]
                    s.dma("sp", cosb[:], cos_t, writes=["cosb"], sem="cosb")
                    s.dma("sp", sinb[:], sin_t, writes=["sinb"], sem="sinb")
                    gbc = sb(pa, "gbc", [128, D], F32)
                    xs = [sb(pa, "xs%d" % i, [128, 4, D], F32) for i in range(2)]
                    hb = [sb(pa, "hb%d" % i, [128, D], BF16) for i in range(2)]
                    junk = sb(pa, "junk", [128, D], BF16)
                    stat = sb(pa, "stat", [128, 3, NT], F32)
                    s.dma("sp", gbc[:], attn_norm_g[l:l + 1, :].to_broadcast([128, D]), writes=["gbc"], sem="gbc")

                    def ldx(J):
                        s.dma("sp", xs[J % 2][:], x_src[J * 512:(J + 1) * 512, :].rearrange("(j p) c -> p j c", p=128),
                              reads=[("X", J)], writes=[("xs", J % 2)], sem="xs%d" % (J % 2))

                    ldx(0)
                    pend1 = []
                    for J in range(NST):
                        if J + 1 < NST:
                            ldx(J + 1)
                        for j in range(4):
                            tt = J * 4 + j
                            k_ = norm_front(xs[J % 2][:, j, :], ("xs", J % 2), gbc, "gbc", hb, junk, stat, tt, "p1a")
                            if pend1:
                                norm_back(*pend1.pop())
                            pend1.append((k_, hT[:, :, tt * 128:(tt + 1) * 128], [("hT", J)]))
                    norm_back(*pend1.pop())

                    ucount = [0]

                    for F in (0, 1):
                        load_family(F + 1)
                        w = wF[F % 2]
                        dst = QT if F == 0 else KT
                        units = [(J, h) for J in range(NST) for h in range(NH)]

                        def qk_front(u):
                            J, h = units[u]
                            a = u % 2
                            tsl = slice(J * 512, (J + 1) * 512)
                            s.op("pe", [(lambda e, kc=kc: e.matmul(psA[:, a, :], lhsT=w[:, kc, h * 128:(h + 1) * 128],
                                                                   rhs=hT[:, kc, tsl], start=(kc == 0), stop=(kc == 7)))
                                        for kc in range(8)],
                                 reads=[(("wF", F % 2), 0), ("hT", J)], writes=[("psA", a)])
                            s.op("act", lambda e: e.copy(out=qs[a][:], in_=psA[:, a, :]),
                                 reads=[("psA", a)], writes=[("qs", a)])

                        def qk_back(u):
                            J, h = units[u]
                            a = u % 2
                            r = (F * NST + J) % 2
                            tsl = slice(J * 512, (J + 1) * 512)
                            s.op("pe", lambda e: e.matmul(psB[:, a, :], lhsT=pmat[:], rhs=qs[a][:], start=True, stop=True),
                                 reads=[("qs", a), "pmat"], writes=[("psB", a)])
                            s.op("pool", lambda e: e.tensor_tensor(out=t1[a][:], in0=qs[a][:], in1=cosb[:, tsl], op=ALU.mult),
                                 reads=[("qs", a), "cosb"], writes=[("t1", a)])
                            s.op("dve", lambda e: e.tensor_tensor(out=t2[a][:], in0=psB[:, a, :], in1=sinb[:, tsl], op=ALU.mult),
                                 reads=[("psB", a), "sinb"], writes=[("t2", a)])
                            s.op("dve", lambda e: e.tensor_tensor(out=qrot[r][:, h, :], in0=t1[a][:], in1=t2[a][:], op=ALU.add),
                                 reads=[("t1", a), ("t2", a)], writes=[("qrot", r)])
                            if h == NH - 1:
                                s.dma("sp", dst.rearrange("h p t -> p h t")[:, :, tsl], qrot[r][:],
                                      reads=[("qrot", r)], writes=[("QK", F, J)], sem="qrot%d" % r)

                        qk_front(0)
                        for u in range(len(units)):
                            if u + 1 < len(units):
                                qk_front(u + 1)
                            qk_back(u)
                s.barrier()

                with ExitStack() as pv:
                    vs = [sb(pv, "vs%d" % i, [128, 4, D], BF16) for i in range(2)]
                    F = 2
                    load_family(F + 1)
                    w = wF[F % 2]
                    for J in range(NST):
                        r = J % 2
                        for j in range(4):
                            tt = J * 4 + j
                            ps = psAB[tt % 2]
                            s.op("pe", [(lambda e, kc=kc, c2=c2: e.matmul(ps[:, c2, :], lhsT=hT[:, kc, tt * 128:(tt + 1) * 128],
                                                                          rhs=w[:, kc, c2 * 512:(c2 + 1) * 512],
                                                                          start=(kc == 0), stop=(kc == 7)))
                                        for c2 in range(2) for kc in range(8)],
                                 reads=[(("wF", F % 2), 0), ("hT", J)], writes=[("psAB", tt % 2)])
                            s.op("act", lambda e: e.copy(out=vs[r][:, j, :], in_=ps[:].rearrange("p a b -> p (a b)")),
                                 reads=[("psAB", tt % 2)], writes=[("vs", r)])
                        s.dma("sp", VV[J * 512:(J + 1) * 512, :].rearrange("(j p) c -> p j c", p=128), vs[r][:],
                              reads=[("vs", r)], writes=[("VV", J)], sem="vs%d" % r)
                s.barrier()

                def fm_family(F, func, dst, stack_name):
                    with ExitStack() as pu:
                        us = [sb(pu, "%s%d" % (stack_name, i), [128, 8, 512], BF16) for i in range(2)]
                        if F + 1 < 7:
                            load_family(F + 1)
                        w = wF[F % 2]
                        for J in range(NST):
                            r = J % 2
                            tsl = slice(J * 512, (J + 1) * 512)
                            for cc in range(8):
                                a = ucount[0] % 2
                                ucount[0] += 1
                                s.op("pe", [(lambda e, kc=kc: e.matmul(psA[:, a, :], lhsT=w[:, kc, cc * 128:(cc + 1) * 128],
                                                                       rhs=hT[:, kc, tsl], start=(kc == 0), stop=(kc == 7)))
                                            for kc in range(8)],
                                     reads=[(("wF", F % 2), 0), ("hT", J)], writes=[("psA", a)])
                                s.op("act", lambda e: e.activation(out=us[r][:, cc, :], in_=psA[:, a, :], func=func),
                                     reads=[("psA", a)], writes=[("us", r)])
                            s.dma("sp", fm(dst)[:, :, tsl], us[r][:], reads=[("us", r)], writes=[("FM", F, J)],
                                  sem="us%d" % r)
                    s.barrier()

                fm_family(3, AF.Gelu, UT, "us")

                with ExitStack() as pb:
                    F = 4
                    load_family(F + 1)
                    w = wF[F % 2]
                    lng = sb(pb, "lng", [128, D], F32)
                    lnb = sb(pb, "lnb", [128, D], F32)
                    wsp = sb(pb, "wsp", [128, 8, 128], F32)
                    wspb = sb(pb, "wspb", [128, 8, 128], BF16)
                    wmT = sb(pb, "wmT", [128, 8, 128], BF16)
                    bsp = sb(pb, "bsp", [1, 8, 128], F32)
                    bhi = sb(pb, "bhi", [1, 8, 128], BF16)
                    blo = sb(pb, "blo", [1, 8, 128], BF16)
                    ones_bf = sb(pb, "ones_bf", [1, 128], BF16)
                    vg = [sb(pb, "vg%d" % i, [128, D], F32) for i in range(2)]
                    tln = sb(pb, "tln", [128, D], F32)
                    vh = [sb(pb, "vh%d" % i, [128, D], BF16) for i in range(2)]
                    ul = [sb(pb, "ul%d" % i, [128, 8, 512], BF16) for i in range(2)]
                    ob = [sb(pb, "ob%d" % i, [128, 8, 512], BF16) for i in range(2)]
                    bst = sb(pb, "bst", [128, NT, 2, 6], F32)
                    mv = sb(pb, "mv", [128, NT, 4], F32)
                    s.dma("sp", lng[:], sgu_ln_g[l:l + 1, :].to_broadcast([128, D]), writes=["lng"], sem="lng")
                    s.dma("sp", lnb[:], sgu_ln_b[l:l + 1, :].to_broadcast([128, D]), writes=["lnb"], sem="lnb")
                    s.dma("sp", wsp[:], w_spatial[l].rearrange("g t s -> t g s"), writes=["wsp"], sem="wsp")
                    s.dma("sp", bsp[:], b_spatial[l:l + 1, :, :], writes=["bsp"], sem="bsp")
                    s.op("dve", lambda e: e.tensor_copy(out=wspb[:], in_=wsp[:]), reads=["wsp"], writes=["wspb"])
                    s.op("dve", lambda e: e.memset(ones_bf[:], 1.0), writes=["ones_bf"])
                    s.op("dve", lambda e: e.tensor_copy(out=bhi[:], in_=bsp[:]), reads=["bsp"], writes=["bhi"])
                    s.op("dve", lambda e: e.tensor_tensor(out=blo[:], in0=bsp[:], in1=bhi[:], op=ALU.subtract),
                         reads=["bsp", "bhi"], writes=["blo"])
                    s.op("pe", [(lambda e, g=g: e.transpose(psT[0][:, g, :], wspb[:, g, :], ident[:])) for g in range(8)],
                         reads=["wspb", "ident"], writes=[("psT", 0)])
                    s.op("dve", lambda e: e.tensor_copy(out=wmT[:], in_=psT[0][:]), reads=[("psT", 0)], writes=["wmT"])
                    s.op("dve", lambda e: e.memset(wmT[64:128, :, 0:64], 0.0), reads=[], writes=["wmT"])
                    pm = psC[:].rearrange("p a (g t) -> p (a g) t", t=128)

                    def ldu(J):
                        s.dma("sp", ul[J % 2][:], fm(UT)[:, :, J * 512:(J + 1) * 512], reads=[("FM", 3, J)],
                              writes=[("ul", J % 2)], sem="ul%d" % (J % 2))

                    def vb_front(tt):
                        J = tt // 4
                        k = tt % 2
                        ps = psAB[k]
                        s.op("pe", [(lambda e, kc=kc, c2=c2: e.matmul(ps[:, c2, :], lhsT=hT[:, kc, tt * 128:(tt + 1) * 128],
                                                                      rhs=w[:, kc, c2 * 512:(c2 + 1) * 512],
                                                                      start=(kc == 0), stop=(kc == 7)))
                                    for c2 in range(2) for kc in range(8)],
                             reads=[(("wF", F % 2), 0), ("hT", J)], writes=[("psAB", k)])
                        s.op("act", lambda e: e.activation(out=vg[k][:], in_=ps[:].rearrange("p a b -> p (a b)"), func=AF.Gelu),
                             reads=[("psAB", k)], writes=[("vg", k)])
                        s.op("dve", [lambda e: e.bn_stats(out=bst[:, tt, 0, :], in_=vg[k][:, 0:512]),
                                     lambda e: e.bn_stats(out=bst[:, tt, 1, :], in_=vg[k][:, 512:1024])],
                             reads=[("vg", k)], writes=[("bst", tt)])
                        s.op("dve", lambda e: e.bn_aggr(out=mv[:, tt, 0:2], in_=bst[:, tt, :, :].rearrange("p a b -> p (a b)")),
                             reads=[("bst", tt)], writes=[("mv", tt)])
                        s.op("pool", lambda e: e.tensor_scalar(out=mv[:, tt, 2:3], in0=mv[:, tt, 1:2], scalar1=1.0, scalar2=EPS,
                                                               op0=ALU.mult, op1=ALU.add),
                             reads=[("mv", tt)], writes=[("mv2", tt)])
                        s.op("pool", lambda e: e.tensor_tensor(out=mv[:, tt, 3:4], in0=mv[:, tt, 2:3], in1=neghalf[:], op=ALU.pow),
                             reads=[("mv2", tt), "neghalf"], writes=[("mv3", tt)])
                        s.op("dve", lambda e: e.scalar_tensor_tensor(out=tln[:], in0=vg[k][:], scalar=mv[:, tt, 0:1], in1=lng[:],
                                                                     op0=ALU.subtract, op1=ALU.mult),
                             reads=[("vg", k), ("mv", tt), "lng"], writes=["tln"])
                        s.op("dve", lambda e: e.scalar_tensor_tensor(out=vh[k][:], in0=tln[:], scalar=mv[:, tt, 3:4], in1=lnb[:],
                                                                     op0=ALU.mult, op1=ALU.add),
                             reads=["tln", ("mv3", tt), "lnb"], writes=[("vh", k)])

                    def vb_back(tt):
                        J, j = tt // 4, tt % 4
                        k = tt % 2
                        r = J % 2
                        fns = []
                        for g in range(8):
                            fns.append(lambda e, g=g: e.matmul(pm[:, g, :], lhsT=vh[k][:, g * 128:(g + 1) * 128],
                                                               rhs=wmT[:, g, :], start=True, stop=False))
                            fns.append(lambda e, g=g: e.matmul(pm[:, g, :], lhsT=ones_bf[0:1, :], rhs=bhi[0:1, g, :],
                                                               start=False, stop=False))
                            fns.append(lambda e, g=g: e.matmul(pm[:, g, :], lhsT=ones_bf[0:1, :], rhs=blo[0:1, g, :],
                                                               start=False, stop=True))
                        s.op("pe", fns, reads=[("vh", k), "wmT", "ones_bf", "bhi", "blo"], writes=["psC"])
                        s.op("dve", lambda e: e.tensor_tensor(out=ob[r][:, :, j * 128:(j + 1) * 128], in0=pm,
                                                              in1=ul[r][:, :, j * 128:(j + 1) * 128], op=ALU.mult),
                             reads=["psC", ("ul", r)], writes=[("ob", r)])
                        if j == 3:
                            s.dma("sp", fm(OBT)[:, :, J * 512:(J + 1) * 512], ob[r][:], reads=[("ob", r)],
                                  writes=[("OBT", J)], sem="ob%d" % r)

                    ldu(0)
                    vb_front(0)
                    for tt in range(NT):
                        if tt % 4 == 0 and tt // 4 + 1 < NST:
                            ldu(tt // 4 + 1)
                        if tt + 1 < NT:
                            vb_front(tt + 1)
                        vb_back(tt)
                s.barrier()

                fm_family(5, AF.Sigmoid, SGA, "ga")
                fm_family(6, AF.Sigmoid, SGB, "gb")

            p23 = ExitStack()
            wpa = sb(p23, "wpa", [128, 8, D], BF16)
            wpb = sb(p23, "wpb", [128, 8, D], BF16)
            wo = sb(p23, "wo", [128, 8, D], BF16)
            load_w_cast(wpa, "wpa", w_proj_a[l], 8, "wpa")
            load_w_cast(wpb, "wpb", w_proj_b[l], 8, "wpb")
            load_w_cast(wo, "wo", w_out[l], 8, "wo")
            with ExitStack() as p2:
                KTh = [sb(p2, "KTh%d" % i, [128, S], BF16) for i in range(2)]
                QTh = [sb(p2, "QTh%d" % i, [128, S], BF16) for i in range(2)]
                Vh = [sb(p2, "Vh%d" % i, [128, NT, 129], BF16) for i in range(2)]
                OATh = [sb(p2, "OATh%d" % i, [128, S], BF16) for i in range(2)]
                ET = [sb(p2, "ET%d" % i, [128, 2, 512], BF16) for i in range(3)]
                lamv = sb(p2, "lamv", [128, 4, 64], F32)
                lamj = sb(p2, "lamj", [128, 64], F32)
                lams = sb(p2, "lams", [128, 8], F32)
                gsub = sb(p2, "gsub", [128, 128], F32)
                rz = sb(p2, "rz", [128, NT, 4], F32)
                tO = [sb(p2, "tO%d" % i, [128, 128], F32) for i in range(2)]
                oo = [sb(p2, "oo%d" % i, [128, 128], F32) for i in range(2)]
                ojunk = sb(p2, "ojunk", [128, 128], F32)
                onb = [sb(p2, "onb%d" % i, [128, 128], BF16) for i in range(8)]
                sst = sb(p2, "sst", [128, NT, 3], F32)

                for i in range(2):
                    s.op("pool", lambda e, i=i: e.memset(Vh[i][:, :, 128:129], 1.0), writes=[("Vones", i)])
                for i, src in enumerate((lam_q1, lam_k1, lam_q2, lam_k2)):
                    s.dma("sp", lamv[:, i, :], src[l:l + 1, :].to_broadcast([128, 64]), writes=[("lamv", i)], sem="lamv%d" % i)
                s.dma("sp", gsub[:], subln_g[l:l + 1, :].to_broadcast([128, 128]), writes=["gsub_raw"], sem="gsub")
                for di in range(2):
                    s.op("dve", lambda e, di=di: e.tensor_tensor(out=lamj[:], in0=lamv[:, 2 * di, :], in1=lamv[:, 2 * di + 1, :], op=ALU.mult),
                         reads=[("lamv", 2 * di), ("lamv", 2 * di + 1)], writes=["lamj"])
                    s.op("dve", lambda e, di=di: e.tensor_reduce(out=lams[:, di:di + 1], in_=lamj[:], axis=mybir.AxisListType.X, op=ALU.add),
                         reads=["lamj"], writes=["lam_d%d" % (di + 1)])
                s.op("act", lambda e: e.activation(out=lams[:, 2:4], in_=lams[:, 0:2], func=AF.Exp),
                     reads=["lam_d1", "lam_d2"], writes=["lam_e"])
                s.op("dve", lambda e: e.scalar_tensor_tensor(out=lams[:, 4:5], in0=lams[:, 3:4], scalar=-lam_init, in1=lams[:, 2:3],
                                                             op0=ALU.add, op1=ALU.subtract),
                     reads=["lam_e"], writes=["neglam"])
                s.op("dve", lambda e: e.tensor_scalar(out=gsub[:], in0=gsub[:], scalar1=(1.0 - lam_init), scalar2=None, op0=ALU.mult),
                     reads=["gsub_raw"], writes=["gsub"])
                neglam = lams[:, 4:5]

                def ld_head(h):
                    i = h % 2
                    s.dma("sp", KTh[i][:], KT[h], reads=[("QK", 1, J) for J in range(NST)], writes=[("KTh", i)], sem="KTh%d" % i)
                    s.dma("sp", QTh[i][:], QT[h], reads=[("QK", 0, J) for J in range(NST)], writes=[("QTh", i)], sem="QTh%d" % i)
                    s.dma("sp", Vh[i][:, :, 0:128], VV[:, h * 128:(h + 1) * 128].rearrange("(t p) e -> p t e", p=128),
                          reads=[("VV", J) for J in range(NST)] + [("Vones", i)], writes=[("Vh", i)], sem="Vh%d" % i)

                groups = []
                for h in range(NH):
                    for i in range(NT):
                        for gi in range((i + 4) // 4):
                            groups.append((h, i, gi, list(range(4 * gi, min(4 * gi + 4, i + 1)))))
                NG = len(groups)
                NSLOT = 3
                NONB = 8
                TDEFER = 6

                def oslot(h, i):
                    return (h * NT + i) % 2

                def Oacc(h, i, br):
                    return psC[:, oslot(h, i), br * 129:(br + 1) * 129]

                def emit_G(idx):
                    h, i, gi, kbs = groups[idx]
                    hi = h % 2
                    Kt, Qt = KTh[hi], QTh[hi]
                    n = len(kbs)
                    sbi = idx % 2
                    ebi = idx % 3
                    Sps = psAB[sbi]
                    qsl = slice(i * 128, (i + 1) * 128)
                    fns = []
                    for kl, kb in enumerate(kbs):
                        for br in range(2):
                            fns.append(lambda e, kl=kl, kb=kb, br=br: e.matmul(
                                Sps[:, br, kl * 128:(kl + 1) * 128],
                                lhsT=Kt[br * 64:(br + 1) * 64, kb * 128:(kb + 1) * 128],
                                rhs=Qt[br * 64:(br + 1) * 64, qsl], start=True, stop=True))
                    s.op("pe", fns, reads=[("KTh", hi), ("QTh", hi)], writes=[("psAB", sbi)])
                    s.op("act", lambda e: e.activation(out=ET[ebi][:, :, 0:n * 128], in_=Sps[:, :, 0:n * 128],
                                                       func=AF.Exp, scale=0.125),
                         reads=[("psAB", sbi)], writes=[("ET", ebi)])
                    if kbs[-1] == i:
                        kl = n - 1
                        s.op("pool", lambda e: e.memset(ET[ebi][64:128, :, kl * 128:kl * 128 + 64], 0.0),
                             reads=[], writes=[("ET", ebi)])

                def emit_A(idx):
                    h, i, gi, kbs = groups[idx]
                    hi = h % 2
                    Vt = Vh[hi]
                    ebi = idx % 3
                    fns = []
                    for kl, kb in enumerate(kbs):
                        for br in range(2):
                            fns.append(lambda e, kl=kl, kb=kb, br=br: e.matmul(
                                Oacc(h, i, br), lhsT=ET[ebi][:, br, kl * 128:(kl + 1) * 128], rhs=Vt[:, kb, :],
                                start=(kb == 0 and br == 0), stop=(kb == i), skip_group_check=True))
                    s.op("pe", fns, reads=[("ET", ebi), ("Vh", hi)], writes=[("psC", oslot(h, i))])

                def emit_F(h, i):
                    sl = oslot(h, i)
                    ts = (h * NT + i) % 2
                    ob_ = (h * NT + i) % NONB
                    O0, O1 = Oacc(h, i, 0), Oacc(h, i, 1)
                    s.op("dve", [lambda e: e.reciprocal(out=rz[:, i, 0:1], in_=O0[:, 128:129]),
                                 lambda e: e.reciprocal(out=rz[:, i, 1:2], in_=O1[:, 128:129])],
                         reads=[("psC", sl)], writes=[("rz", i)])
                    s.op("dve", lambda e: e.tensor_tensor(out=rz[:, i, 2:3], in0=rz[:, i, 1:2], in1=neglam, op=ALU.mult),
                         reads=[("rz", i), "neglam"], writes=[("rz2", i)])
                    s.op("dve", lambda e: e.tensor_scalar(out=tO[ts][:], in0=O0[:, 0:128], scalar1=rz[:, i, 0:1], scalar2=None,
                                                          op0=ALU.mult),
                         reads=[("psC", sl), ("rz", i)], writes=[("tO", ts)])
                    s.op("dve", lambda e: e.scalar_tensor_tensor(out=oo[ts][:], in0=O1[:, 0:128], scalar=rz[:, i, 2:3],
                                                                 in1=tO[ts][:], op0=ALU.mult, op1=ALU.add),
                         reads=[("psC", sl), ("rz2", i), ("tO", ts)], writes=[("oo", ts)])
                    s.op("dve", lambda e: e.tensor_tensor(out=ojunk[:], in0=oo[ts][:], in1=oo[ts][:], op=ALU.mult),
                         reads=[("oo", ts)], writes=["ojunk"])
                    s.op("dve", lambda e: e.tensor_reduce(out=sst[:, i, 0:1], in_=ojunk[:], axis=mybir.AxisListType.X, op=ALU.add),
                         reads=["ojunk"], writes=[("sst", i)])
                    rstd_from_sumsq(sst[:, i, 0:1], sst[:, i, 1:2], sst[:, i, 2:3], 128, [("sst", i)], ("sst1", i), ("sst2", i))
                    s.op("dve", lambda e: e.scalar_tensor_tensor(out=onb[ob_][:], in0=oo[ts][:], scalar=sst[:, i, 2:3], in1=gsub[:],
                                                                 op0=ALU.mult, op1=ALU.mult),
                         reads=[("oo", ts), ("sst2", i), "gsub"], writes=[("onb", ob_)])

                def emit_T(h, i):
                    hi = h % 2
                    ob_ = (h * NT + i) % NONB
                    tpar = (h * NT + i) % 2
                    pt = psT[tpar][:, 0, :]
                    s.op("pe", lambda e: e.transpose(pt, onb[ob_][:], ident[:]),
                         reads=[("onb", ob_), "ident"], writes=[("psT", tpar)])
                    s.op("dve", lambda e: e.tensor_copy(out=OATh[hi][:, i * 128:(i + 1) * 128], in_=pt),
                         reads=[("psT", tpar)], writes=[("OATh", hi)])
                    if i == NT - 1:
                        s.dma("sp", OAT[h * 128:(h + 1) * 128, :], OATh[hi][:], reads=[("OATh", hi)], writes=[("OAT", h)],
                              sem="OATh%d" % hi)

                ld_head(0)
                ld_head(1)
                pendingT = []
                emit_G(0)
                for idx in range(NG):
                    if idx + 1 < NG:
                        emit_G(idx + 1)
                    emit_A(idx)
                    h, i, gi, kbs = groups[idx]
                    if kbs[-1] == i:
                        emit_F(h, i)
                        pendingT.append((idx + TDEFER, h, i))
                        if i == NT - 1 and h + 2 < NH:
                            ld_head(h + 2)
                    while pendingT and pendingT[0][0] <= idx:
                        _, th, ti = pendingT.pop(0)
                        emit_T(th, ti)
                for _, th, ti in pendingT:
                    emit_T(th, ti)
            s.barrier()

            with ExitStack() as p3:
                oaT = [sb(p3, "oaT%d" % i, [128, 8, 512], BF16) for i in range(2)]
                obT = [sb(p3, "obT%d" % i, [128, 8, 512], BF16) for i in range(2)]
                sga = [sb(p3, "sga%d" % i, [128, 8, 512], BF16) for i in range(2)]
                sgb = [sb(p3, "sgb%d" % i, [128, 8, 512], BF16) for i in range(2)]
                xs3 = [sb(p3, "xs3_%d" % i, [128, 4, D], F32) for i in range(2)]
                yT = sb(p3, "yT", [128, 8, 512], BF16)
                ta = [sb(p3, "ta%d" % i, [128, 512], F32) for i in range(2)]
                tb = [sb(p3, "tb%d" % i, [128, 512], F32) for i in range(2)]

                def ld3(J):
                    i = J % 2
                    tsl = slice(J * 512, (J + 1) * 512)
                    s.dma("sp", oaT[i][:], fm(OAT)[:, :, tsl], writes=[("oaT", i)], sem="oaT%d" % i)
                    s.dma("sp", obT[i][:], fm(OBT)[:, :, tsl], writes=[("obT", i)], sem="obT%d" % i)
                    s.dma("sp", sga[i][:], fm(SGA)[:, :, tsl], writes=[("sga", i)], sem="sga%d" % i)
                    s.dma("sp", sgb[i][:], fm(SGB)[:, :, tsl], writes=[("sgb", i)], sem="sgb%d" % i)
                    s.dma("sp", xs3[i][:], x_src[tsl, :].rearrange("(j p) c -> p j c", p=128), writes=[("xs3", i)], sem="xs3_%d" % i)

                ld3(0)
                uc = 0
                for J in range(NST):
                    if J + 1 < NST:
                        ld3(J + 1)
                    i = J % 2
                    for cc in range(8):
                        a = uc % 2
                        uc += 1
                        s.op("pe", [(lambda e, kc=kc: e.matmul(psA[:, a, :], lhsT=wpa[:, kc, cc * 128:(cc + 1) * 128], rhs=oaT[i][:, kc, :],
                                                               start=(kc == 0), stop=(kc == 7))) for kc in range(8)],
                             reads=[("wpa", 0), ("oaT", i)], writes=[("psA", a)])
                        s.op("pe", [(lambda e, kc=kc: e.matmul(psB[:, a, :], lhsT=wpb[:, kc, cc * 128:(cc + 1) * 128], rhs=obT[i][:, kc, :],
                                                               start=(kc == 0), stop=(kc == 7))) for kc in range(8)],
                             reads=[("wpb", 0), ("obT", i)], writes=[("psB", a)])
                        s.op("dve", lambda e: e.tensor_tensor(out=ta[a][:], in0=psA[:, a, :], in1=sga[i][:, cc, :], op=ALU.mult),
                             reads=[("psA", a), ("sga", i)], writes=[("ta", a)])
                        s.op("dve", lambda e: e.tensor_tensor(out=tb[a][:], in0=psB[:, a, :], in1=sgb[i][:, cc, :], op=ALU.mult),
                             reads=[("psB", a), ("sgb", i)], writes=[("tb", a)])
                        s.op("pool", lambda e: e.tensor_tensor(out=yT[:, cc, :], in0=ta[a][:], in1=tb[a][:], op=ALU.add),
                             reads=[("ta", a), ("tb", a)], writes=[("yT", cc)])
                    for j in range(4):
                        for c2 in range(2):
                            s.op("pe", [(lambda e, kc=kc: e.matmul(psC[:, c2, :], lhsT=yT[:, kc, j * 128:(j + 1) * 128],
                                                                   rhs=wo[:, kc, c2 * 512:(c2 + 1) * 512],
                                                                   start=(kc == 0), stop=(kc == 7)))
                                        for kc in range(8)],
                                 reads=[("wo", 0)] + [("yT", cc) for cc in range(8)], writes=[("psC", c2)])
                            s.op("dve", lambda e: e.tensor_tensor(out=xs3[i][:, j, c2 * 512:(c2 + 1) * 512], in0=psC[:, c2, :],
                                                                  in1=xs3[i][:, j, c2 * 512:(c2 + 1) * 512], op=ALU.add),
                                 reads=[("psC", c2), ("xs3", i)], writes=[("xs3", i)])
                    s.dma("sp", XB[J * 512:(J + 1) * 512, :].rearrange("(j p) c -> p j c", p=128), xs3[i][:],
                          reads=[("xs3", i)], writes=[("XB", J)], sem="xs3o_%d" % i)
            s.barrier()
            p23.close()

            with ExitStack() as p4:
                wg = sb(p4, "wg", [128, 8, DFF], BF16)
                wu = sb(p4, "wu", [128, 8, DFF], BF16)
                gbc2 = sb(p4, "gbc2", [128, D], F32)
                xs4 = [sb(p4, "xs4_%d" % i, [128, 4, D], F32) for i in range(2)]
                hb4 = [sb(p4, "hb4_%d" % i, [128, D], BF16) for i in range(2)]
                junk4 = sb(p4, "junk4", [128, D], BF16)
                stat4 = sb(p4, "stat4", [128, 3, NT], F32)
                h2T = [sb(p4, "h2T%d" % i, [128, 8, 512], BF16) for i in range(2)]
                sgt = [sb(p4, "sgt%d" % i, [128, 512], F32) for i in range(2)]
                aT = [sb(p4, "aT%d" % i, [128, NFC, 512], BF16) for i in range(2)]
                s.dma("sp", gbc2[:], ffn_norm_g[l:l + 1, :].to_broadcast([128, D]), writes=["gbc2"], sem="gbc2")
                for hf in range(2):
                    for (dst_, key_, src_) in ((wg, "wg", w_gate[l]), (wu, "wu", w_up[l])):
                        cs_ = DFF // 2
                        s.dma("pool", dst_[:, :, hf * cs_:(hf + 1) * cs_],
                              src_[:, hf * cs_:(hf + 1) * cs_].rearrange("(kc p) c -> p kc c", p=128),
                              writes=[(key_, hf)], sem="%s_%d" % (key_, hf))

                def ld4(J):
                    s.dma("sp", xs4[J % 2][:], XB[J * 512:(J + 1) * 512, :].rearrange("(j p) c -> p j c", p=128),
                          writes=[("xs4", J % 2)], sem="xs4_%d" % (J % 2))

                ld4(0)
                uc = 0
                for J in range(NST):
                    if J + 1 < NST:
                        ld4(J + 1)
                    i = J % 2
                    pend4 = []
                    for j in range(4):
                        k_ = norm_front(xs4[i][:, j, :], ("xs4", i), gbc2, "gbc2", hb4, junk4, stat4, J * 4 + j, "p4a")
                        if pend4:
                            norm_back(*pend4.pop())
                        pend4.append((k_, h2T[i][:, :, j * 128:(j + 1) * 128], [("h2T", i)]))
                    norm_back(*pend4.pop())
                    for fc in range(NFC):
                        a = uc % 2
                        uc += 1
                        half = 0 if fc < NFC // 2 else 1
                        s.op("pe", [(lambda e, kc=kc: e.matmul(psA[:, a, :], lhsT=wg[:, kc, fc * 128:(fc + 1) * 128], rhs=h2T[i][:, kc, :],
                                                               start=(kc == 0), stop=(kc == 7))) for kc in range(8)],
                             reads=[("wg", half), ("h2T", i)], writes=[("psA", a)])
                        s.op("pe", [(lambda e, kc=kc: e.matmul(psB[:, a, :], lhsT=wu[:, kc, fc * 128:(fc + 1) * 128], rhs=h2T[i][:, kc, :],
                                                               start=(kc == 0), stop=(kc == 7))) for kc in range(8)],
                             reads=[("wu", half), ("h2T", i)], writes=[("psB", a)])
                        s.op("act", lambda e: e.activation(out=sgt[a][:], in_=psA[:, a, :], func=AF.Silu),
                             reads=[("psA", a)], writes=[("sgt", a)])
                        s.op("dve", lambda e: e.tensor_tensor(out=aT[i][:, fc, :], in0=psB[:, a, :], in1=sgt[a][:], op=ALU.mult),
                             reads=[("psB", a), ("sgt", a)], writes=[("aT", i)])
                    s.dma("sp", AT.rearrange("(fc p) t -> p fc t", p=128)[:, :, J * 512:(J + 1) * 512], aT[i][:],
                          reads=[("aT", i)], writes=[("AT", J)], sem="aT%d" % i)
            s.barrier()

            with ExitStack() as p5:
                wd = sb(p5, "wd", [128, NFC, D], BF16)
                aTl = [sb(p5, "aTl%d" % i, [128, NFC, 512], BF16) for i in range(2)]
                xs5 = [sb(p5, "xs5_%d" % i, [128, 4, D], F32) for i in range(2)]
                load_w_cast(wd, "wd", w_down[l], NFC, "wd", col_split=1)
                do_final = last and final_norm
                if do_final:
                    gfin = sb(p5, "gfin", [128, D], F32)
                    junk5 = sb(p5, "junk5", [128, D], BF16)
                    stat5 = sb(p5, "stat5", [128, 3, NT], F32)
                    s.dma("sp", gfin[:], final_norm_g[0:1, :].to_broadcast([128, D]), writes=["gfin"], sem="gfin")
                x_dst = y_out if last else XA

                def ld5(J):
                    i = J % 2
                    tsl = slice(J * 512, (J + 1) * 512)
                    s.dma("sp", aTl[i][:], AT.rearrange("(fc p) t -> p fc t", p=128)[:, :, tsl], writes=[("aTl", i)], sem="aTl%d" % i)
                    s.dma("sp", xs5[i][:], XB[tsl, :].rearrange("(j p) c -> p j c", p=128), writes=[("xs5", i)], sem="xs5_%d" % i)

                ld5(0)
                for J in range(NST):
                    if J + 1 < NST:
                        ld5(J + 1)
                    i = J % 2
                    for j in range(4):
                        tt = J * 4 + j
                        ps = psAB[tt % 2]
                        s.op("pe", [(lambda e, fc=fc, c2=c2: e.matmul(ps[:, c2, :], lhsT=aTl[i][:, fc, j * 128:(j + 1) * 128],
                                                                      rhs=wd[:, fc, c2 * 512:(c2 + 1) * 512],
                                                                      start=(fc == 0), stop=(fc == NFC - 1)))
                                    for c2 in range(2) for fc in range(NFC)],
                             reads=[("wd", 0), ("aTl", i)], writes=[("psAB", tt % 2)])
                        s.op("dve", lambda e: e.tensor_tensor(out=xs5[i][:, j, :], in0=ps[:].rearrange("p a b -> p (a b)"),
                                                              in1=xs5[i][:, j, :], op=ALU.add),
                             reads=[("psAB", tt % 2), ("xs5", i)], writes=[("xs5", i)])
                        if do_final:
                            ssq = stat5[:, 0, tt:tt + 1]
                            s.op("act", lambda e: e.activation(out=junk5[:], in_=xs5[i][:, j, :], func=AF.Square, accum_out=ssq),
                                 reads=[("xs5", i)], writes=["junk5", ("f_ssq", tt)])
                            rstd_from_sumsq(ssq, stat5[:, 1, tt:tt + 1], stat5[:, 2, tt:tt + 1], D, [("f_ssq", tt)],
                                            ("f_var", tt), ("f_rstd", tt))
                            s.op("dve", lambda e: e.scalar_tensor_tensor(out=xs5[i][:, j, :], in0=xs5[i][:, j, :],
                                                                         scalar=stat5[:, 2, tt:tt + 1], in1=gfin[:],
                                                                         op0=ALU.mult, op1=ALU.mult),
                                 reads=[("xs5", i), ("f_rstd", tt), "gfin"], writes=[("xs5", i)])
                    s.dma("sp", x_dst[J * 512:(J + 1) * 512, :].rearrange("(j p) c -> p j c", p=128), xs5[i][:],
                          reads=[("xs5", i)], writes=[("X", J)], sem="xs5o_%d" % i)
            s.barrier()
        print("instructions emitted:", s.n_inst)
    return nc


_CONST_CACHE = {}


def _consts():
    if "c" not in _CONST_CACHE:
        inv_freq = (10000.0 ** (-np.arange(0, 64, 2, dtype=np.float32) / np.float32(64))).astype(np.float32)
        ang = (np.arange(S, dtype=np.float32)[:, None] * inv_freq[None, :]).astype(np.float32)
        cos = np.cos(ang).astype(np.float32)
        sin = np.sin(ang).astype(np.float32)
        d = np.arange(128) % 64
        cos_t = np.ascontiguousarray(cos[:, d % 32].T)
        sgn = np.where(d < 32, -1.0, 1.0).astype(np.float32)
        sin_t = np.ascontiguousarray((sin[:, d % 32] * sgn[None, :]).T)
        pm = np.zeros((128, 128), np.float32)
        for p in range(128):
            partner = p + 32 if (p % 64) < 32 else p - 32
            pm[partner, p] = 1.0
        ident = np.eye(128, dtype=np.float32)
        _CONST_CACHE["c"] = dict(cos_t=cos_t, sin_t=sin_t, pmat=pm, ident=ident)
    return _CONST_CACHE["c"]


_PROG_CACHE = {}


def _get_prog(layers, final_norm):
    key = (tuple(layers), final_norm)
    if key not in _PROG_CACHE:
        _PROG_CACHE[key] = build_program(list(layers), final_norm)
    return _PROG_CACHE[key]


def kernel(x, attn_norm_g, w_in, lam_q1, lam_k1, lam_q2, lam_k2, subln_g, sgu_ln_g, sgu_ln_b,
           w_spatial, b_spatial, w_proj_a, w_proj_b, w_out, ffn_norm_g, w_gate, w_up, w_down, final_norm_g):
    f = lambda a: np.ascontiguousarray(np.asarray(a, dtype=np.float32))
    shared = dict(
        attn_norm_g=f(attn_norm_g), w_in=f(w_in), lam_q1=f(lam_q1), lam_k1=f(lam_k1), lam_q2=f(lam_q2), lam_k2=f(lam_k2),
        subln_g=f(subln_g), sgu_ln_g=f(sgu_ln_g), sgu_ln_b=f(sgu_ln_b), w_spatial=f(w_spatial), b_spatial=f(b_spatial),
        w_proj_a=f(w_proj_a), w_proj_b=f(w_proj_b), w_out=f(w_out), ffn_norm_g=f(ffn_norm_g), w_gate=f(w_gate),
        w_up=f(w_up), w_down=f(w_down), final_norm_g=f(final_norm_g).reshape(1, D),
    )
    shared.update(_consts())
    xcur = f(x)
    groups = [list(range(i, min(i + LAYERS_PER_LAUNCH, DEPTH))) for i in range(0, DEPTH, LAYERS_PER_LAUNCH)]
    for gi, layers in enumerate(groups):
        final = gi == len(groups) - 1
        nc = _get_prog(layers, final)
        in_maps = [dict(shared, x=np.ascontiguousarray(xcur[b])) for b in range(N_CORES)]
        res = run_bass_kernel_spmd(nc, in_maps, core_ids=list(range(N_CORES)))
        xcur = np.stack([np.asarray(res.results[b]["y"], dtype=np.float32) for b in range(N_CORES)], axis=0)
    return xcur
```

```python
import math
from contextlib import ExitStack

import numpy as np
import concourse.bass as bass
import concourse.mybir as mybir
from concourse.bass_utils import run_bass_kernel_spmd

F32 = mybir.dt.float32
BF16 = mybir.dt.bfloat16
AF = mybir.ActivationFunctionType
ALU = mybir.AluOpType

S = 4096
D = 1024
NIN = 7168
DFF = 2816
NFC = DFF // 128
NH = 8
NT = S // 128
NST = S // 512
DEPTH = 4
EPS = 1e-6
N_CORES = 8

LAYERS_PER_LAUNCH = 4
DEBUG_SCRATCH = False


class Sched:
    def __init__(self, nc, es):
        self.nc = nc
        self.es = es
        self.eng = {"pe": nc.tensor, "act": nc.scalar, "dve": nc.vector, "pool": nc.gpsimd, "sp": nc.sync}
        self.sem = {e: es.enter_context(nc.semaphore("sem_" + e)) for e in self.eng}
        self.cnt = {e: 0 for e in self.eng}
        self.seen = {e: {} for e in self.eng}
        self.semobj = {}
        for e in self.eng:
            self.semobj[id(self.sem[e])] = self.sem[e]
        self.last_write = {}
        self.readers = {}
        self.dma_sems = {}
        self.n_inst = 0

    def _deps(self, reads, writes):
        deps = []
        for k in reads:
            t = self.last_write.get(k)
            if t is not None:
                deps.append(t)
        for k in writes:
            t = self.last_write.get(k)
            if t is not None:
                deps.append(t)
            r = self.readers.get(k)
            if r:
                deps.extend(r.values())
        return deps

    def _wait(self, engine, deps):
        need = {}
        for sem, val, src in deps:
            if src == engine and engine == "pe":
                continue
            sid = id(sem)
            if need.get(sid, (None, 0))[1] < val:
                need[sid] = (sem, val)
        seen = self.seen[engine]
        eo = self.eng[engine]
        for sid, (sem, val) in need.items():
            if seen.get(sid, 0) >= val:
                continue
            eo.wait_ge(sem, val)
            seen[sid] = val

    def _record(self, tok, reads, writes):
        for k in writes:
            self.last_write[k] = tok
            self.readers[k] = {}
        for k in reads:
            r = self.readers.setdefault(k, {})
            key = (id(tok[0]), tok[2])
            if key not in r or r[key][1] < tok[1]:
                r[key] = tok

    @staticmethod
    def _banks(k):
        if k == "psC":
            return [("bk", 4), ("bk", 5)]
        if isinstance(k, tuple) and len(k) == 2:
            n, a = k
            if n == "psA":
                return [("bk", a)]
            if n == "psB":
                return [("bk", 2 + a)]
            if n == "psAB":
                return [("bk", 2 * a), ("bk", 2 * a + 1)]
            if n == "psC":
                return [("bk", 4 + a)]
            if n == "psT":
                return [("bk", 6 + a)]
            if n == "psS":
                return [("bk", 0), ("bk", 1)] if a == 0 else ([("bk", 2), ("bk", 3)] if a == 1 else [("bk", 6), ("bk", 7)])
        return None

    def _xl(self, reads, writes):
        r2, w2 = [], []
        for k in reads:
            b = self._banks(k)
            if b is None:
                r2.append(k)
            else:
                w2.extend(b)
        for k in writes:
            b = self._banks(k)
            if b is None:
                w2.append(k)
            else:
                w2.extend(b)
        return r2, w2

    def op(self, engine, fns, reads=(), writes=()):
        reads, writes = self._xl(reads, writes)
        if not isinstance(fns, (list, tuple)):
            fns = [fns]
        self._wait(engine, self._deps(reads, writes))
        eo = self.eng[engine]
        inst = None
        for fn in fns:
            inst = fn(eo)
            self.n_inst += 1
        self.cnt[engine] += 1
        inst.then_inc(self.sem[engine], 1)
        tok = (self.sem[engine], self.cnt[engine], engine)
        self._record(tok, reads, writes)

    def dma(self, queue, out, in_, reads=(), writes=(), sem=None):
        assert sem is not None
        self._wait(queue, self._deps(reads, writes))
        if sem not in self.dma_sems:
            self.dma_sems[sem] = [self.es.enter_context(self.nc.semaphore("dsem_" + sem)), 0]
        d = self.dma_sems[sem]
        inst = self.eng[queue].dma_start(out=out, in_=in_)
        d[1] += 16
        inst.then_inc(d[0], 16)
        self.n_inst += 1
        tok = (d[0], d[1], "dma:" + sem)
        self._record(tok, reads, writes)

    def barrier(self):
        for e, eo in self.eng.items():
            seen = self.seen[e]
            for e2 in self.eng:
                if e2 == e:
                    continue
                v = self.cnt[e2]
                sid = id(self.sem[e2])
                if v > 0 and seen.get(sid, 0) < v:
                    eo.wait_ge(self.sem[e2], v)
                    seen[sid] = v
            for name, (sem, v) in self.dma_sems.items():
                sid = id(sem)
                if v > 0 and seen.get(sid, 0) < v:
                    eo.wait_ge(sem, v)
                    seen[sid] = v
        self.last_write = {}
        self.readers = {}


def build_program(layers, final_norm, x_is_input=True):
    nc = bass.Bass("TRN2", target_bir_lowering=False)
    dt = nc.dram_tensor

    def din(name, shape, dtype=F32):
        return dt(name, list(shape), dtype, kind="ExternalInput").ap()

    x_in = din("x", [S, D])
    attn_norm_g = din("attn_norm_g", [DEPTH, D])
    w_in = din("w_in", [DEPTH, D, NIN])
    lam_q1 = din("lam_q1", [DEPTH, 64])
    lam_k1 = din("lam_k1", [DEPTH, 64])
    lam_q2 = din("lam_q2", [DEPTH, 64])
    lam_k2 = din("lam_k2", [DEPTH, 64])
    subln_g = din("subln_g", [DEPTH, 128])
    sgu_ln_g = din("sgu_ln_g", [DEPTH, D])
    sgu_ln_b = din("sgu_ln_b", [DEPTH, D])
    w_spatial = din("w_spatial", [DEPTH, 8, 128, 128])
    b_spatial = din("b_spatial", [DEPTH, 8, 128])
    w_proj_a = din("w_proj_a", [DEPTH, D, D])
    w_proj_b = din("w_proj_b", [DEPTH, D, D])
    w_out = din("w_out", [DEPTH, D, D])
    ffn_norm_g = din("ffn_norm_g", [DEPTH, D])
    w_gate = din("w_gate", [DEPTH, D, DFF])
    w_up = din("w_up", [DEPTH, D, DFF])
    w_down = din("w_down", [DEPTH, DFF, D])
    final_norm_g = din("final_norm_g", [1, D])
    cos_t = din("cos_t", [128, S])
    sin_t = din("sin_t", [128, S])
    pmat_in = din("pmat", [128, 128])
    ident_in = din("ident", [128, 128])

    y_out = dt("y", [S, D], F32, kind="ExternalOutput").ap()

    def dscr(name, shape, dtype):
        return dt(name, list(shape), dtype, kind=("ExternalOutput" if DEBUG_SCRATCH else "Internal")).ap()

    XA = dscr("XA", [S, D], F32)
    XB = dscr("XB", [S, D], F32)
    QT = dscr("QT", [NH, 128, S], BF16)
    KT = dscr("KT", [NH, 128, S], BF16)
    VV = dscr("VV", [S, D], BF16)
    UT = dscr("UT", [D, S], BF16)
    OBT = dscr("OBT", [D, S], BF16)
    SGA = dscr("SGA", [D, S], BF16)
    SGB = dscr("SGB", [D, S], BF16)
    OA = dscr("OA", [S, D], BF16)
    AT = dscr("AT", [DFF, S], BF16)

    def fm(ap):
        return ap.rearrange("(cc p) t -> p cc t", p=128)

    with ExitStack() as es:
        s = Sched(nc, es)

        sbn = [0]

        def sb(stack, name, shape, dtype):
            sbn[0] += 1
            return stack.enter_context(nc.sbuf_tensor("sb%d_%s" % (sbn[0], name), list(shape), dtype))

        psA = es.enter_context(nc.psum_tensor("psA", [128, 2, 512], F32))
        psB = es.enter_context(nc.psum_tensor("psB", [128, 2, 512], F32))
        psC = es.enter_context(nc.psum_tensor("psC", [128, 2, 512], F32))
        psD = es.enter_context(nc.psum_tensor("psD", [128, 2, 512], F32))
        psT = [psD[:, i, :].bitcast(BF16).rearrange("p (a b) -> p a b", b=128) for i in range(2)]
        psAB = [psA, psB]
        psS = [psA, psB, psD]

        ident = sb(es, "ident", [128, 128], BF16)
        pmat = sb(es, "pmat", [128, 128], BF16)
        neghalf = sb(es, "neghalf", [128, 1], F32)
        ones_row = sb(es, "ones_row", [1, 128], F32)
        s.dma("pool", ident[:], ident_in, writes=["ident"], sem="c_ident")
        s.dma("pool", pmat[:], pmat_in, writes=["pmat"], sem="c_pmat")
        s.op("pool", lambda e: e.memset(neghalf[:], -0.5), writes=["neghalf"])
        s.op("pool", lambda e: e.memset(ones_row[:], 1.0), writes=["ones_row"])

        def load_w_cast(dst, dst_key, src2d, n_kc, sem, col_split=1):
            C = src2d.shape[1]
            cs = C // col_split
            for i in range(col_split):
                s.dma(
                    "pool",
                    dst[:, :, i * cs:(i + 1) * cs],
                    src2d[:, i * cs:(i + 1) * cs].rearrange("(kc p) c -> p kc c", p=128),
                    writes=[(dst_key, i)],
                    sem="%s_%d" % (sem, i),
                )
            return [(dst_key, i) for i in range(col_split)]

        def rstd_from_sumsq(sumsq_ap, var_ap, rstd_ap, n, keys_in, key_var, key_out):
            s.op("pool", lambda e: e.tensor_scalar(out=var_ap, in0=sumsq_ap, scalar1=1.0 / n, scalar2=EPS,
                                                   op0=ALU.mult, op1=ALU.add),
                 reads=keys_in, writes=[key_var])
            s.op("pool", lambda e: e.tensor_tensor(out=rstd_ap, in0=var_ap, in1=neghalf[:], op=ALU.pow),
                 reads=[key_var, "neghalf"], writes=[key_out])

        tcount = [0]

        def norm_front(xt_ap, xkey, gbc, gkey, hb, junk, stat, sidx, ph):
            k = tcount[0] % 2
            tcount[0] += 1
            ssq = stat[:, 0, sidx:sidx + 1]
            var = stat[:, 1, sidx:sidx + 1]
            rstd = stat[:, 2, sidx:sidx + 1]
            s.op("act", lambda e: e.activation(out=junk[:], in_=xt_ap, func=AF.Square, accum_out=ssq),
                 reads=[xkey], writes=[(ph, "junk"), (ph, "ssq", sidx)])
            rstd_from_sumsq(ssq, var, rstd, D, [(ph, "ssq", sidx)], (ph, "var", sidx), (ph, "rstd", sidx))
            s.op("dve", lambda e: e.scalar_tensor_tensor(out=hb[k][:], in0=xt_ap, scalar=rstd, in1=gbc[:],
                                                         op0=ALU.mult, op1=ALU.mult),
                 reads=[xkey, (ph, "rstd", sidx), gkey], writes=[(ph, "hb", k)])
            pt = psT[k]
            s.op("pe", [(lambda e, kc=kc: e.transpose(pt[:, kc, :], hb[k][:, kc * 128:(kc + 1) * 128], ident[:]))
                        for kc in range(8)],
                 reads=[(ph, "hb", k), "ident"], writes=[("psT", k)])
            return k

        def norm_back(k, hT_dst, hkeys):
            s.op("act", lambda e: e.copy(out=hT_dst, in_=psT[k]), reads=[("psT", k)], writes=hkeys)

        for li, l in enumerate(layers):
            lam_init = 0.8 - 0.6 * math.exp(-0.3 * l)
            if li == 0:
                x_src = x_in
            else:
                x_src = XA
            last = li == len(layers) - 1

            with ExitStack() as p1:
                hT = sb(p1, "hT", [128, 8, S], BF16)
                wF = [sb(p1, "wF%d" % i, [128, 8, 1024], BF16) for i in range(2)]

                def load_family(F):
                    return load_w_cast(wF[F % 2], ("wF", F % 2), w_in[l, :, F * 1024:(F + 1) * 1024], 8,
                                       "wF%d" % (F % 2))

                load_family(0)
                with ExitStack() as pa:
                    cosb = sb(pa, "cosb", [128, S], F32)
                    sinb = sb(pa, "sinb", [128, S], F32)
                    qs = [sb(pa, "qs%d" % i, [128, 512], BF16) for i in range(2)]
                    t1 = [sb(pa, "t1_%d" % i, [128, 512], F32) for i in range(2)]
                    t2 = [sb(pa, "t2_%d" % i, [128, 512], F32) for i in range(2)]
                    qrot = [sb(pa, "qrot%d" % i, [128, 8, 512], BF16) for i in range(2)]
                    s.dma("sp", cosb[:], cos_t, writes=["cosb"], sem="cosb")
                    s.dma("sp", sinb[:], sin_t, writes=["sinb"], sem="sinb")
                    gbc = sb(pa, "gbc", [128, D], F32)
                    xs = [sb(pa, "xs%d" % i, [128, 4, D], F32) for i in range(2)]
                    hb = [sb(pa, "hb%d" % i, [128, D], BF16) for i in range(2)]
                    junk = sb(pa, "junk", [128, D], BF16)
                    stat = sb(pa, "stat", [128, 3, NT], F32)
                    s.dma("sp", gbc[:], attn_norm_g[l:l + 1, :].to_broadcast([128, D]), writes=["gbc"], sem="gbc")

                    def ldx(J):
                        s.dma("sp", xs[J % 2][:], x_src[J * 512:(J + 1) * 512, :].rearrange("(j p) c -> p j c", p=128),
                              reads=[("X", J)], writes=[("xs", J % 2)], sem="xs%d" % (J % 2))

                    ldx(0)
                    pend1 = []
                    for J in range(NST):
                        if J + 1 < NST:
                            ldx(J + 1)
                        for j in range(4):
                            tt = J * 4 + j
                            k_ = norm_front(xs[J % 2][:, j, :], ("xs", J % 2), gbc, "gbc", hb, junk, stat, tt, "p1a")
                            if pend1:
                                norm_back(*pend1.pop())
                            pend1.append((k_, hT[:, :, tt * 128:(tt + 1) * 128], [("hT", J)]))
                    norm_back(*pend1.pop())

                    ucount = [0]

                    for F in (0, 1):
                        load_family(F + 1)
                        w = wF[F % 2]
                        dst = QT if F == 0 else KT
                        units = [(J, h) for J in range(NST) for h in range(NH)]

                        def qk_front(u):
                            J, h = units[u]
                            a = u % 2
                            tsl = slice(J * 512, (J + 1) * 512)
                            s.op("pe", [(lambda e, kc=kc: e.matmul(psA[:, a, :], lhsT=w[:, kc, h * 128:(h + 1) * 128],
                                                                   rhs=hT[:, kc, tsl], start=(kc == 0), stop=(kc == 7)))
                                        for kc in range(8)],
                                 reads=[(("wF", F % 2), 0), ("hT", J)], writes=[("psA", a)])
                            s.op("act", lambda e: e.copy(out=qs[a][:], in_=psA[:, a, :]),
                                 reads=[("psA", a)], writes=[("qs", a)])

                        def qk_back(u):
                            J, h = units[u]
                            a = u % 2
                            r = (F * NST + J) % 2
                            tsl = slice(J * 512, (J + 1) * 512)
                            s.op("pe", lambda e: e.matmul(psB[:, a, :], lhsT=pmat[:], rhs=qs[a][:], start=True, stop=True),
                                 reads=[("qs", a), "pmat"], writes=[("psB", a)])
                            s.op("pool", lambda e: e.tensor_tensor(out=t1[a][:], in0=qs[a][:], in1=cosb[:, tsl], op=ALU.mult),
                                 reads=[("qs", a), "cosb"], writes=[("t1", a)])
                            s.op("dve", lambda e: e.tensor_tensor(out=t2[a][:], in0=psB[:, a, :], in1=sinb[:, tsl], op=ALU.mult),
                                 reads=[("psB", a), "sinb"], writes=[("t2", a)])
                            s.op("dve", lambda e: e.tensor_tensor(out=qrot[r][:, h, :], in0=t1[a][:], in1=t2[a][:], op=ALU.add),
                                 reads=[("t1", a), ("t2", a)], writes=[("qrot", r)])
                            if h == NH - 1:
                                s.dma("sp", dst.rearrange("h p t -> p h t")[:, :, tsl], qrot[r][:],
                                      reads=[("qrot", r)], writes=[("QK", F, J)], sem="qrot%d" % r)

                        qk_front(0)
                        for u in range(len(units)):
                            if u + 1 < len(units):
                                qk_front(u + 1)
                            qk_back(u)
                s.barrier()

                with ExitStack() as pv:
                    vs = [sb(pv, "vs%d" % i, [128, 4, D], BF16) for i in range(2)]
                    F = 2
                    load_family(F + 1)
                    w = wF[F % 2]
                    for J in range(NST):
                        r = J % 2
                        for j in range(4):
                            tt = J * 4 + j
                            ps = psAB[tt % 2]
                            s.op("pe", [(lambda e, kc=kc, c2=c2: e.matmul(ps[:, c2, :], lhsT=hT[:, kc, tt * 128:(tt + 1) * 128],
                                                                          rhs=w[:, kc, c2 * 512:(c2 + 1) * 512],
                                                                          start=(kc == 0), stop=(kc == 7)))
                                        for c2 in range(2) for kc in range(8)],
                                 reads=[(("wF", F % 2), 0), ("hT", J)], writes=[("psAB", tt % 2)])
                            s.op("act", lambda e: e.copy(out=vs[r][:, j, :], in_=ps[:].rearrange("p a b -> p (a b)")),
                                 reads=[("psAB", tt % 2)], writes=[("vs", r)])
                        s.dma("sp", VV[J * 512:(J + 1) * 512, :].rearrange("(j p) c -> p j c", p=128), vs[r][:],
                              reads=[("vs", r)], writes=[("VV", J)], sem="vs%d" % r)
                s.barrier()

                def fm_family(F, func, dst, stack_name):
                    with ExitStack() as pu:
                        us = [sb(pu, "%s%d" % (stack_name, i), [128, 8, 512], BF16) for i in range(2)]
                        if F + 1 < 7:
                            load_family(F + 1)
                        w = wF[F % 2]
                        for J in range(NST):
                            r = J % 2
                            tsl = slice(J * 512, (J + 1) * 512)
                            for cc in range(8):
                                a = ucount[0] % 2
                                ucount[0] += 1
                                s.op("pe", [(lambda e, kc=kc: e.matmul(psA[:, a, :], lhsT=w[:, kc, cc * 128:(cc + 1) * 128],
                                                                       rhs=hT[:, kc, tsl], start=(kc == 0), stop=(kc == 7)))
                                            for kc in range(8)],
                                     reads=[(("wF", F % 2), 0), ("hT", J)], writes=[("psA", a)])
                                s.op("act", lambda e: e.activation(out=us[r][:, cc, :], in_=psA[:, a, :], func=func),
                                     reads=[("psA", a)], writes=[("us", r)])
                            s.dma("sp", fm(dst)[:, :, tsl], us[r][:], reads=[("us", r)], writes=[("FM", F, J)],
                                  sem="us%d" % r)
                    s.barrier()

                fm_family(3, AF.Gelu, UT, "us")

                with ExitStack() as pb:
                    F = 4
                    load_family(F + 1)
                    w = wF[F % 2]
                    lng = sb(pb, "lng", [128, D], F32)
                    lnb = sb(pb, "lnb", [128, D], F32)
                    wsp = sb(pb, "wsp", [128, 8, 128], F32)
                    wspb = sb(pb, "wspb", [128, 8, 128], BF16)
                    wmT = sb(pb, "wmT", [128, 8, 128], BF16)
                    bsp = sb(pb, "bsp", [1, 8, 128], F32)
                    bhi = sb(pb, "bhi", [1, 8, 128], BF16)
                    blo = sb(pb, "blo", [1, 8, 128], BF16)
                    ones_bf = sb(pb, "ones_bf", [1, 128], BF16)
                    vg = [sb(pb, "vg%d" % i, [128, D], F32) for i in range(2)]
                    tln = sb(pb, "tln", [128, D], F32)
                    vh = [sb(pb, "vh%d" % i, [128, D], BF16) for i in range(2)]
                    ul = [sb(pb, "ul%d" % i, [128, 8, 512], BF16) for i in range(2)]
                    ob = [sb(pb, "ob%d" % i, [128, 8, 512], BF16) for i in range(2)]
                    bst = sb(pb, "bst", [128, NT, 2, 6], F32)
                    mv = sb(pb, "mv", [128, NT, 4], F32)
                    s.dma("sp", lng[:], sgu_ln_g[l:l + 1, :].to_broadcast([128, D]), writes=["lng"], sem="lng")
                    s.dma("sp", lnb[:], sgu_ln_b[l:l + 1, :].to_broadcast([128, D]), writes=["lnb"], sem="lnb")
                    s.dma("sp", wsp[:], w_spatial[l].rearrange("g t s -> t g s"), writes=["wsp"], sem="wsp")
                    s.dma("sp", bsp[:], b_spatial[l:l + 1, :, :], writes=["bsp"], sem="bsp")
                    s.op("dve", lambda e: e.tensor_copy(out=wspb[:], in_=wsp[:]), reads=["wsp"], writes=["wspb"])
                    s.op("dve", lambda e: e.memset(ones_bf[:], 1.0), writes=["ones_bf"])
                    s.op("dve", lambda e: e.tensor_copy(out=bhi[:], in_=bsp[:]), reads=["bsp"], writes=["bhi"])
                    s.op("dve", lambda e: e.tensor_tensor(out=blo[:], in0=bsp[:], in1=bhi[:], op=ALU.subtract),
                         reads=["bsp", "bhi"], writes=["blo"])
                    s.op("pe", [(lambda e, g=g: e.transpose(psT[0][:, g, :], wspb[:, g, :], ident[:])) for g in range(8)],
                         reads=["wspb", "ident"], writes=[("psT", 0)])
                    s.op("dve", lambda e: e.tensor_copy(out=wmT[:], in_=psT[0]), reads=[("psT", 0)], writes=["wmT"])
                    s.op("dve", lambda e: e.memset(wmT[64:128, :, 0:64], 0.0), reads=[], writes=["wmT"])
                    pm = psC[:].rearrange("p a (g t) -> p (a g) t", t=128)

                    def ldu(J):
                        s.dma("sp", ul[J % 2][:], fm(UT)[:, :, J * 512:(J + 1) * 512], reads=[("FM", 3, J)],
                              writes=[("ul", J % 2)], sem="ul%d" % (J % 2))

                    def vb_front(tt):
                        J = tt // 4
                        k = tt % 2
                        ps = psAB[k]
                        s.op("pe", [(lambda e, kc=kc, c2=c2: e.matmul(ps[:, c2, :], lhsT=hT[:, kc, tt * 128:(tt + 1) * 128],
                                                                      rhs=w[:, kc, c2 * 512:(c2 + 1) * 512],
                                                                      start=(kc == 0), stop=(kc == 7)))
                                    for c2 in range(2) for kc in range(8)],
                             reads=[(("wF", F % 2), 0), ("hT", J)], writes=[("psAB", k)])
                        s.op("act", lambda e: e.activation(out=vg[k][:], in_=ps[:].rearrange("p a b -> p (a b)"), func=AF.Gelu),
                             reads=[("psAB", k)], writes=[("vg", k)])
                        s.op("dve", [lambda e: e.bn_stats(out=bst[:, tt, 0, :], in_=vg[k][:, 0:512]),
                                     lambda e: e.bn_stats(out=bst[:, tt, 1, :], in_=vg[k][:, 512:1024])],
                             reads=[("vg", k)], writes=[("bst", tt)])
                        s.op("dve", lambda e: e.bn_aggr(out=mv[:, tt, 0:2], in_=bst[:, tt, :, :].rearrange("p a b -> p (a b)")),
                             reads=[("bst", tt)], writes=[("mv", tt)])
                        s.op("pool", lambda e: e.tensor_scalar(out=mv[:, tt, 2:3], in0=mv[:, tt, 1:2], scalar1=1.0, scalar2=EPS,
                                                               op0=ALU.mult, op1=ALU.add),
                             reads=[("mv", tt)], writes=[("mv2", tt)])
                        s.op("pool", lambda e: e.tensor_tensor(out=mv[:, tt, 3:4], in0=mv[:, tt, 2:3], in1=neghalf[:], op=ALU.pow),
                             reads=[("mv2", tt), "neghalf"], writes=[("mv3", tt)])
                        s.op("dve", lambda e: e.scalar_tensor_tensor(out=tln[:], in0=vg[k][:], scalar=mv[:, tt, 0:1], in1=lng[:],
                                                                     op0=ALU.subtract, op1=ALU.mult),
                             reads=[("vg", k), ("mv", tt), "lng"], writes=["tln"])
                        s.op("dve", lambda e: e.scalar_tensor_tensor(out=vh[k][:], in0=tln[:], scalar=mv[:, tt, 3:4], in1=lnb[:],
                                                                     op0=ALU.mult, op1=ALU.add),
                             reads=["tln", ("mv3", tt), "lnb"], writes=[("vh", k)])

                    def vb_back(tt):
                        J, j = tt // 4, tt % 4
                        k = tt % 2
                        r = J % 2
                        fns = []
                        for g in range(8):
                            fns.append(lambda e, g=g: e.matmul(pm[:, g, :], lhsT=vh[k][:, g * 128:(g + 1) * 128],
                                                               rhs=wmT[:, g, :], start=True, stop=False))
                            fns.append(lambda e, g=g: e.matmul(pm[:, g, :], lhsT=ones_bf[0:1, :], rhs=bhi[0:1, g, :],
                                                               start=False, stop=False))
                            fns.append(lambda e, g=g: e.matmul(pm[:, g, :], lhsT=ones_bf[0:1, :], rhs=blo[0:1, g, :],
                                                               start=False, stop=True))
                        s.op("pe", fns, reads=[("vh", k), "wmT", "ones_bf", "bhi", "blo"], writes=["psC"])
                        s.op("dve", lambda e: e.tensor_tensor(out=ob[r][:, :, j * 128:(j + 1) * 128], in0=pm,
                                                              in1=ul[r][:, :, j * 128:(j + 1) * 128], op=ALU.mult),
                             reads=["psC", ("ul", r)], writes=[("ob", r)])
                        if j == 3:
                            s.dma("sp", fm(OBT)[:, :, J * 512:(J + 1) * 512], ob[r][:], reads=[("ob", r)],
                                  writes=[("OBT", J)], sem="ob%d" % r)

                    ldu(0)
                    vb_front(0)
                    for tt in range(NT):
                        if tt % 4 == 0 and tt // 4 + 1 < NST:
                            ldu(tt // 4 + 1)
                        if tt + 1 < NT:
                            vb_front(tt + 1)
                        vb_back(tt)
                s.barrier()

                fm_family(5, AF.Sigmoid, SGA, "ga")
                fm_family(6, AF.Sigmoid, SGB, "gb")

            p23 = ExitStack()
            wpa = sb(p23, "wpa", [128, 8, D], BF16)
            wpb = sb(p23, "wpb", [128, 8, D], BF16)
            wo = sb(p23, "wo", [128, 8, D], BF16)
            load_w_cast(wpa, "wpa", w_proj_a[l], 8, "wpa")
            load_w_cast(wpb, "wpb", w_proj_b[l], 8, "wpb")
            load_w_cast(wo, "wo", w_out[l], 8, "wo")
            with ExitStack() as p2:
                KTh = [sb(p2, "KTh%d" % i, [128, S], BF16) for i in range(2)]
                QTh = [sb(p2, "QTh%d" % i, [128, S], BF16) for i in range(2)]
                Vh = [sb(p2, "Vh%d" % i, [128, NT, 129], BF16) for i in range(2)]
                OAtok = [sb(p2, "OAtok%d" % i, [128, NT, 128], BF16) for i in range(2)]
                ET = [sb(p2, "ET%d" % i, [128, 2, 512], BF16) for i in range(3)]
                lamv = sb(p2, "lamv", [128, 4, 64], F32)
                lamj = sb(p2, "lamj", [128, 64], F32)
                lams = sb(p2, "lams", [128, 8], F32)
                gsub = sb(p2, "gsub", [128, 128], F32)
                rz = sb(p2, "rz", [128, NT, 4], F32)
                tO = [sb(p2, "tO%d" % i, [128, 128], F32) for i in range(2)]
                oo = [sb(p2, "oo%d" % i, [128, 128], F32) for i in range(2)]
                ojunk = sb(p2, "ojunk", [128, 128], F32)
                sst = sb(p2, "sst", [128, NT, 3], F32)

                for i in range(2):
                    s.op("pool", lambda e, i=i: e.memset(Vh[i][:, :, 128:129], 1.0), writes=[("Vones", i)])
                for i, src in enumerate((lam_q1, lam_k1, lam_q2, lam_k2)):
                    s.dma("sp", lamv[:, i, :], src[l:l + 1, :].to_broadcast([128, 64]), writes=[("lamv", i)], sem="lamv%d" % i)
                s.dma("sp", gsub[:], subln_g[l:l + 1, :].to_broadcast([128, 128]), writes=["gsub_raw"], sem="gsub")
                for di in range(2):
                    s.op("dve", lambda e, di=di: e.tensor_tensor(out=lamj[:], in0=lamv[:, 2 * di, :], in1=lamv[:, 2 * di + 1, :], op=ALU.mult),
                         reads=[("lamv", 2 * di), ("lamv", 2 * di + 1)], writes=["lamj"])
                    s.op("dve", lambda e, di=di: e.tensor_reduce(out=lams[:, di:di + 1], in_=lamj[:], axis=mybir.AxisListType.X, op=ALU.add),
                         reads=["lamj"], writes=["lam_d%d" % (di + 1)])
                s.op("act", lambda e: e.activation(out=lams[:, 2:4], in_=lams[:, 0:2], func=AF.Exp),
                     reads=["lam_d1", "lam_d2"], writes=["lam_e"])
                s.op("dve", lambda e: e.scalar_tensor_tensor(out=lams[:, 4:5], in0=lams[:, 3:4], scalar=-lam_init, in1=lams[:, 2:3],
                                                             op0=ALU.add, op1=ALU.subtract),
                     reads=["lam_e"], writes=["neglam"])
                s.op("dve", lambda e: e.tensor_scalar(out=gsub[:], in0=gsub[:], scalar1=(1.0 - lam_init), scalar2=None, op0=ALU.mult),
                     reads=["gsub_raw"], writes=["gsub"])
                neglam = lams[:, 4:5]

                def ld_head(h):
                    i = h % 2
                    s.dma("sp", KTh[i][:], KT[h], reads=[("QK", 1, J) for J in range(NST)], writes=[("KTh", i)], sem="KTh%d" % i)
                    s.dma("sp", QTh[i][:], QT[h], reads=[("QK", 0, J) for J in range(NST)], writes=[("QTh", i)], sem="QTh%d" % i)
                    s.dma("sp", Vh[i][:, :, 0:128], VV[:, h * 128:(h + 1) * 128].rearrange("(t p) e -> p t e", p=128),
                          reads=[("VV", J) for J in range(NST)] + [("Vones", i)], writes=[("Vh", i)], sem="Vh%d" % i)

                groups = []
                for h in range(NH):
                    for i in range(NT):
                        for gi in range((i + 4) // 4):
                            groups.append((h, i, gi, list(range(4 * gi, min(4 * gi + 4, i + 1)))))
                NG = len(groups)
                NSLOT = 3
                NONB = 8
                TDEFER = 6

                def oslot(h, i):
                    return (h * NT + i) % 2

                def Oacc(h, i, br):
                    return psC[:, oslot(h, i), br * 129:(br + 1) * 129]

                def emit_G(idx):
                    h, i, gi, kbs = groups[idx]
                    hi = h % 2
                    Kt, Qt = KTh[hi], QTh[hi]
                    n = len(kbs)
                    sbi = idx % 3
                    ebi = idx % 3
                    Sps = psS[sbi]
                    qsl = slice(i * 128, (i + 1) * 128)
                    fns = []
                    for kl, kb in enumerate(kbs):
                        for br in range(2):
                            fns.append(lambda e, kl=kl, kb=kb, br=br: e.matmul(
                                Sps[:, br, kl * 128:(kl + 1) * 128],
                                lhsT=Kt[br * 64:(br + 1) * 64, kb * 128:(kb + 1) * 128],
                                rhs=Qt[br * 64:(br + 1) * 64, qsl], start=True, stop=True))
                    s.op("pe", fns, reads=[("KTh", hi), ("QTh", hi)], writes=[("psS", sbi)])
                    s.op("act", lambda e: e.activation(out=ET[ebi][:, :, 0:n * 128], in_=Sps[:, :, 0:n * 128],
                                                       func=AF.Exp, scale=0.125),
                         reads=[("psS", sbi)], writes=[("ET", ebi)])
                    if kbs[-1] == i:
                        kl = n - 1
                        s.op("pool", lambda e: e.memset(ET[ebi][64:128, :, kl * 128:kl * 128 + 64], 0.0),
                             reads=[], writes=[("ET", ebi)])

                def emit_A(idx):
                    h, i, gi, kbs = groups[idx]
                    hi = h % 2
                    Vt = Vh[hi]
                    ebi = idx % 3
                    fns = []
                    for kl, kb in enumerate(kbs):
                        for br in range(2):
                            fns.append(lambda e, kl=kl, kb=kb, br=br: e.matmul(
                                Oacc(h, i, br), lhsT=ET[ebi][:, br, kl * 128:(kl + 1) * 128], rhs=Vt[:, kb, :],
                                start=(kb == 0 and br == 0), stop=(kb == i), skip_group_check=True))
                    s.op("pe", fns, reads=[("ET", ebi), ("Vh", hi)], writes=[("psC", oslot(h, i))])

                def emit_F(h, i):
                    sl = oslot(h, i)
                    ts = (h * NT + i) % 2
                    hi = h % 2
                    O0, O1 = Oacc(h, i, 0), Oacc(h, i, 1)
                    s.op("dve", [lambda e: e.reciprocal(out=rz[:, i, 0:1], in_=O0[:, 128:129]),
                                 lambda e: e.reciprocal(out=rz[:, i, 1:2], in_=O1[:, 128:129])],
                         reads=[("psC", sl)], writes=[("rz", i)])
                    s.op("dve", lambda e: e.tensor_tensor(out=rz[:, i, 2:3], in0=rz[:, i, 1:2], in1=neglam, op=ALU.mult),
                         reads=[("rz", i), "neglam"], writes=[("rz2", i)])
                    s.op("dve", lambda e: e.tensor_scalar(out=tO[ts][:], in0=O0[:, 0:128], scalar1=rz[:, i, 0:1], scalar2=None,
                                                          op0=ALU.mult),
                         reads=[("psC", sl), ("rz", i)], writes=[("tO", ts)])
                    s.op("dve", lambda e: e.scalar_tensor_tensor(out=oo[ts][:], in0=O1[:, 0:128], scalar=rz[:, i, 2:3],
                                                                 in1=tO[ts][:], op0=ALU.mult, op1=ALU.add),
                         reads=[("psC", sl), ("rz2", i), ("tO", ts)], writes=[("oo", ts)])
                    s.op("dve", lambda e: e.tensor_tensor(out=ojunk[:], in0=oo[ts][:], in1=oo[ts][:], op=ALU.mult),
                         reads=[("oo", ts)], writes=["ojunk"])
                    s.op("dve", lambda e: e.tensor_reduce(out=sst[:, i, 0:1], in_=ojunk[:], axis=mybir.AxisListType.X, op=ALU.add),
                         reads=["ojunk"], writes=[("sst", i)])
                    rstd_from_sumsq(sst[:, i, 0:1], sst[:, i, 1:2], sst[:, i, 2:3], 128, [("sst", i)], ("sst1", i), ("sst2", i))
                    s.op("dve", lambda e: e.scalar_tensor_tensor(out=OAtok[hi][:, i, :], in0=oo[ts][:], scalar=sst[:, i, 2:3], in1=gsub[:],
                                                                 op0=ALU.mult, op1=ALU.mult),
                         reads=[("oo", ts), ("sst2", i), "gsub"], writes=[("OAtok", hi)])
                    if i == NT - 1:
                        s.dma("sp", OA[:, h * 128:(h + 1) * 128].rearrange("(t p) e -> p t e", p=128), OAtok[hi][:],
                              reads=[("OAtok", hi)], writes=[("OA", h)], sem="OAtok%d" % hi)

                ld_head(0)
                ld_head(1)
                emit_G(0)
                emit_G(1)
                for idx in range(NG):
                    if idx + 2 < NG:
                        emit_G(idx + 2)
                    emit_A(idx)
                    h, i, gi, kbs = groups[idx]
                    if kbs[-1] == i:
                        emit_F(h, i)
                        if i == NT - 1 and h + 2 < NH:
                            ld_head(h + 2)
            s.barrier()

            with ExitStack() as p3:
                oaT = [sb(p3, "oaT%d" % i, [128, 8, 512], BF16) for i in range(2)]
                oatok = [sb(p3, "oatok%d" % i, [128, 4, D], BF16) for i in range(2)]
                obT = [sb(p3, "obT%d" % i, [128, 8, 512], BF16) for i in range(2)]
                sga = [sb(p3, "sga%d" % i, [128, 8, 512], BF16) for i in range(2)]
                sgb = [sb(p3, "sgb%d" % i, [128, 8, 512], BF16) for i in range(2)]
                xs3 = [sb(p3, "xs3_%d" % i, [128, 4, D], F32) for i in range(2)]
                yT = sb(p3, "yT", [128, 8, 512], BF16)
                ta = [sb(p3, "ta%d" % i, [128, 512], F32) for i in range(2)]
                tb = [sb(p3, "tb%d" % i, [128, 512], F32) for i in range(2)]

                def ld3(J):
                    i = J % 2
                    tsl = slice(J * 512, (J + 1) * 512)
                    s.dma("sp", oatok[i][:], OA[tsl, :].rearrange("(j p) c -> p j c", p=128), writes=[("oatok", i)], sem="oatok%d" % i)
                    s.dma("sp", obT[i][:], fm(OBT)[:, :, tsl], writes=[("obT", i)], sem="obT%d" % i)
                    s.dma("sp", sga[i][:], fm(SGA)[:, :, tsl], writes=[("sga", i)], sem="sga%d" % i)
                    s.dma("sp", sgb[i][:], fm(SGB)[:, :, tsl], writes=[("sgb", i)], sem="sgb%d" % i)
                    s.dma("sp", xs3[i][:], x_src[tsl, :].rearrange("(j p) c -> p j c", p=128), writes=[("xs3", i)], sem="xs3_%d" % i)

                ld3(0)
                uc = 0
                for J in range(NST):
                    if J + 1 < NST:
                        ld3(J + 1)
                    i = J % 2
                    for j in range(4):
                        tk = j % 2
                        s.op("pe", [(lambda e, kc=kc: e.transpose(psT[tk][:, kc, :], oatok[i][:, j, kc * 128:(kc + 1) * 128], ident[:]))
                                    for kc in range(8)],
                             reads=[("oatok", i), "ident"], writes=[("psT", tk)])
                        s.op("act", lambda e: e.copy(out=oaT[i][:, :, j * 128:(j + 1) * 128], in_=psT[tk]),
                             reads=[("psT", tk)], writes=[("oaT", i)])
                    for cc in range(8):
                        a = uc % 2
                        uc += 1
                        s.op("pe", [(lambda e, kc=kc: e.matmul(psA[:, a, :], lhsT=wpa[:, kc, cc * 128:(cc + 1) * 128], rhs=oaT[i][:, kc, :],
                                                               start=(kc == 0), stop=(kc == 7))) for kc in range(8)],
                             reads=[("wpa", 0), ("oaT", i)], writes=[("psA", a)])
                        s.op("pe", [(lambda e, kc=kc: e.matmul(psB[:, a, :], lhsT=wpb[:, kc, cc * 128:(cc + 1) * 128], rhs=obT[i][:, kc, :],
                                                               start=(kc == 0), stop=(kc == 7))) for kc in range(8)],
                             reads=[("wpb", 0), ("obT", i)], writes=[("psB", a)])
                        s.op("dve", lambda e: e.tensor_tensor(out=ta[a][:], in0=psA[:, a, :], in1=sga[i][:, cc, :], op=ALU.mult),
                             reads=[("psA", a), ("sga", i)], writes=[("ta", a)])
                        s.op("dve", lambda e: e.tensor_tensor(out=tb[a][:], in0=psB[:, a, :], in1=sgb[i][:, cc, :], op=ALU.mult),
                             reads=[("psB", a), ("sgb", i)], writes=[("tb", a)])
                        s.op("pool", lambda e: e.tensor_tensor(out=yT[:, cc, :], in0=ta[a][:], in1=tb[a][:], op=ALU.add),
                             reads=[("ta", a), ("tb", a)], writes=[("yT", cc)])
                    for j in range(4):
                        for c2 in range(2):
                            s.op("pe", [(lambda e, kc=kc: e.matmul(psC[:, c2, :], lhsT=yT[:, kc, j * 128:(j + 1) * 128],
                                                                   rhs=wo[:, kc, c2 * 512:(c2 + 1) * 512],
                                                                   start=(kc == 0), stop=(kc == 7)))
                                        for kc in range(8)],
                                 reads=[("wo", 0)] + [("yT", cc) for cc in range(8)], writes=[("psC", c2)])
                            s.op("dve", lambda e: e.tensor_tensor(out=xs3[i][:, j, c2 * 512:(c2 + 1) * 512], in0=psC[:, c2, :],
                                                                  in1=xs3[i][:, j, c2 * 512:(c2 + 1) * 512], op=ALU.add),
                                 reads=[("psC", c2), ("xs3", i)], writes=[("xs3", i)])
                    s.dma("sp", XB[J * 512:(J + 1) * 512, :].rearrange("(j p) c -> p j c", p=128), xs3[i][:],
                          reads=[("xs3", i)], writes=[("XB", J)], sem="xs3o_%d" % i)
            s.barrier()
            p23.close()

            with ExitStack() as p4:
                wg = sb(p4, "wg", [128, 8, DFF], BF16)
                wu = sb(p4, "wu", [128, 8, DFF], BF16)
                gbc2 = sb(p4, "gbc2", [128, D], F32)
                xs4 = [sb(p4, "xs4_%d" % i, [128, 4, D], F32) for i in range(2)]
                hb4 = [sb(p4, "hb4_%d" % i, [128, D], BF16) for i in range(2)]
                junk4 = sb(p4, "junk4", [128, D], BF16)
                stat4 = sb(p4, "stat4", [128, 3, NT], F32)
                h2T = [sb(p4, "h2T%d" % i, [128, 8, 512], BF16) for i in range(2)]
                sgt = [sb(p4, "sgt%d" % i, [128, 512], F32) for i in range(2)]
                aT = [sb(p4, "aT%d" % i, [128, NFC, 512], BF16) for i in range(2)]
                s.dma("sp", gbc2[:], ffn_norm_g[l:l + 1, :].to_broadcast([128, D]), writes=["gbc2"], sem="gbc2")
                for hf in range(2):
                    for (dst_, key_, src_) in ((wg, "wg", w_gate[l]), (wu, "wu", w_up[l])):
                        cs_ = DFF // 2
                        s.dma("pool", dst_[:, :, hf * cs_:(hf + 1) * cs_],
                              src_[:, hf * cs_:(hf + 1) * cs_].rearrange("(kc p) c -> p kc c", p=128),
                              writes=[(key_, hf)], sem="%s_%d" % (key_, hf))

                def ld4(J):
                    s.dma("sp", xs4[J % 2][:], XB[J * 512:(J + 1) * 512, :].rearrange("(j p) c -> p j c", p=128),
                          writes=[("xs4", J % 2)], sem="xs4_%d" % (J % 2))

                ld4(0)
                uc = 0
                for J in range(NST):
                    if J + 1 < NST:
                        ld4(J + 1)
                    i = J % 2
                    pend4 = []
                    for j in range(4):
                        k_ = norm_front(xs4[i][:, j, :], ("xs4", i), gbc2, "gbc2", hb4, junk4, stat4, J * 4 + j, "p4a")
                        if pend4:
                            norm_back(*pend4.pop())
                        pend4.append((k_, h2T[i][:, :, j * 128:(j + 1) * 128], [("h2T", i)]))
                    norm_back(*pend4.pop())
                    for fc in range(NFC):
                        a = uc % 2
                        uc += 1
                        half = 0 if fc < NFC // 2 else 1
                        s.op("pe", [(lambda e, kc=kc: e.matmul(psA[:, a, :], lhsT=wg[:, kc, fc * 128:(fc + 1) * 128], rhs=h2T[i][:, kc, :],
                                                               start=(kc == 0), stop=(kc == 7))) for kc in range(8)],
                             reads=[("wg", half), ("h2T", i)], writes=[("psA", a)])
                        s.op("pe", [(lambda e, kc=kc: e.matmul(psB[:, a, :], lhsT=wu[:, kc, fc * 128:(fc + 1) * 128], rhs=h2T[i][:, kc, :],
                                                               start=(kc == 0), stop=(kc == 7))) for kc in range(8)],
                             reads=[("wu", half), ("h2T", i)], writes=[("psB", a)])
                        s.op("act", lambda e: e.activation(out=sgt[a][:], in_=psA[:, a, :], func=AF.Silu),
                             reads=[("psA", a)], writes=[("sgt", a)])
                        s.op("dve", lambda e: e.tensor_tensor(out=aT[i][:, fc, :], in0=psB[:, a, :], in1=sgt[a][:], op=ALU.mult),
                             reads=[("psB", a), ("sgt", a)], writes=[("aT", i)])
                    s.dma("sp", AT.rearrange("(fc p) t -> p fc t", p=128)[:, :, J * 512:(J + 1) * 512], aT[i][:],
                          reads=[("aT", i)], writes=[("AT", J)], sem="aT%d" % i)
            s.barrier()

            with ExitStack() as p5:
                wd = sb(p5, "wd", [128, NFC, D], BF16)
                aTl = [sb(p5, "aTl%d" % i, [128, NFC, 512], BF16) for i in range(2)]
                xs5 = [sb(p5, "xs5_%d" % i, [128, 4, D], F32) for i in range(2)]
                for ch in range(2):
                    s.dma("pool", wd[:, ch * 11:(ch + 1) * 11, :],
                          w_down[l, ch * 1408:(ch + 1) * 1408, :].rearrange("(kc p) c -> p kc c", p=128),
                          writes=[("wd", ch)], sem="wd_%d" % ch)
                do_final = last and final_norm
                if do_final:
                    gfin = sb(p5, "gfin", [128, D], F32)
                    junk5 = sb(p5, "junk5", [128, D], BF16)
                    stat5 = sb(p5, "stat5", [128, 3, NT], F32)
                    s.dma("sp", gfin[:], final_norm_g[0:1, :].to_broadcast([128, D]), writes=["gfin"], sem="gfin")
                x_dst = y_out if last else XA

                def ld5(J):
                    i = J % 2
                    tsl = slice(J * 512, (J + 1) * 512)
                    s.dma("sp", aTl[i][:], AT.rearrange("(fc p) t -> p fc t", p=128)[:, :, tsl], writes=[("aTl", i)], sem="aTl%d" % i)
                    s.dma("sp", xs5[i][:], XB[tsl, :].rearrange("(j p) c -> p j c", p=128), writes=[("xs5", i)], sem="xs5_%d" % i)

                ld5(0)
                for J in range(NST):
                    if J + 1 < NST:
                        ld5(J + 1)
                    i = J % 2
                    for j in range(4):
                        tt = J * 4 + j
                        ps = psAB[tt % 2]
                        for ch in range(2):
                            s.op("pe", [(lambda e, fc=fc, c2=c2: e.matmul(ps[:, c2, :], lhsT=aTl[i][:, fc, j * 128:(j + 1) * 128],
                                                                          rhs=wd[:, fc, c2 * 512:(c2 + 1) * 512],
                                                                          start=(fc == 0), stop=(fc == NFC - 1)))
                                        for c2 in range(2) for fc in range(ch * 11, (ch + 1) * 11)],
                                 reads=[("wd", ch), ("aTl", i)], writes=[("psAB", tt % 2)])
                        s.op("dve", lambda e: e.tensor_tensor(out=xs5[i][:, j, :], in0=ps[:].rearrange("p a b -> p (a b)"),
                                                              in1=xs5[i][:, j, :], op=ALU.add),
                             reads=[("psAB", tt % 2), ("xs5", i)], writes=[("xs5", i)])
                        if do_final:
                            ssq = stat5[:, 0, tt:tt + 1]
                            s.op("act", lambda e: e.activation(out=junk5[:], in_=xs5[i][:, j, :], func=AF.Square, accum_out=ssq),
                                 reads=[("xs5", i)], writes=["junk5", ("f_ssq", tt)])
                            rstd_from_sumsq(ssq, stat5[:, 1, tt:tt + 1], stat5[:, 2, tt:tt + 1], D, [("f_ssq", tt)],
                                            ("f_var", tt), ("f_rstd", tt))
                            s.op("dve", lambda e: e.scalar_tensor_tensor(out=xs5[i][:, j, :], in0=xs5[i][:, j, :],
                                                                         scalar=stat5[:, 2, tt:tt + 1], in1=gfin[:],
                                                                         op0=ALU.mult, op1=ALU.mult),
                                 reads=[("xs5", i), ("f_rstd", tt), "gfin"], writes=[("xs5", i)])
                    s.dma("sp", x_dst[J * 512:(J + 1) * 512, :].rearrange("(j p) c -> p j c", p=128), xs5[i][:],
                          reads=[("xs5", i)], writes=[("X", J)], sem="xs5o_%d" % i)
            s.barrier()
        print("instructions emitted:", s.n_inst)
    return nc


_CONST_CACHE = {}


def _consts():
    if "c" not in _CONST_CACHE:
        inv_freq = (10000.0 ** (-np.arange(0, 64, 2, dtype=np.float32) / np.float32(64))).astype(np.float32)
        ang = (np.arange(S, dtype=np.float32)[:, None] * inv_freq[None, :]).astype(np.float32)
        cos = np.cos(ang).astype(np.float32)
        sin = np.sin(ang).astype(np.float32)
        d = np.arange(128) % 64
        cos_t = np.ascontiguousarray(cos[:, d % 32].T)
        sgn = np.where(d < 32, -1.0, 1.0).astype(np.float32)
        sin_t = np.ascontiguousarray((sin[:, d % 32] * sgn[None, :]).T)
        pm = np.zeros((128, 128), np.float32)
        for p in range(128):
            partner = p + 32 if (p % 64) < 32 else p - 32
            pm[partner, p] = 1.0
        ident = np.eye(128, dtype=np.float32)
        _CONST_CACHE["c"] = dict(cos_t=cos_t, sin_t=sin_t, pmat=pm, ident=ident)
    return _CONST_CACHE["c"]


_PROG_CACHE = {}


def _get_prog(layers, final_norm):
    key = (tuple(layers), final_norm)
    if key not in _PROG_CACHE:
        _PROG_CACHE[key] = build_program(list(layers), final_norm)
    return _PROG_CACHE[key]


def kernel(x, attn_norm_g, w_in, lam_q1, lam_k1, lam_q2, lam_k2, subln_g, sgu_ln_g, sgu_ln_b,
           w_spatial, b_spatial, w_proj_a, w_proj_b, w_out, ffn_norm_g, w_gate, w_up, w_down, final_norm_g):
    f = lambda a: np.ascontiguousarray(np.asarray(a, dtype=np.float32))
    shared = dict(
        attn_norm_g=f(attn_norm_g), w_in=f(w_in), lam_q1=f(lam_q1), lam_k1=f(lam_k1), lam_q2=f(lam_q2), lam_k2=f(lam_k2),
        subln_g=f(subln_g), sgu_ln_g=f(sgu_ln_g), sgu_ln_b=f(sgu_ln_b), w_spatial=f(w_spatial), b_spatial=f(b_spatial),
        w_proj_a=f(w_proj_a), w_proj_b=f(w_proj_b), w_out=f(w_out), ffn_norm_g=f(ffn_norm_g), w_gate=f(w_gate),
        w_up=f(w_up), w_down=f(w_down), final_norm_g=f(final_norm_g).reshape(1, D),
    )
    shared.update(_consts())
    xcur = f(x)
    groups = [list(range(i, min(i + LAYERS_PER_LAUNCH, DEPTH))) for i in range(0, DEPTH, LAYERS_PER_LAUNCH)]
    for gi, layers in enumerate(groups):
        final = gi == len(groups) - 1
        nc = _get_prog(layers, final)
        in_maps = [dict(shared, x=np.ascontiguousarray(xcur[b])) for b in range(N_CORES)]
        res = run_bass_kernel_spmd(nc, in_maps, core_ids=list(range(N_CORES)))
        xcur = np.stack([np.asarray(res.results[b]["y"], dtype=np.float32) for b in range(N_CORES)], axis=0)
    return xcur
```

```python
import math
from contextlib import ExitStack

import numpy as np
import concourse.bass as bass
import concourse.mybir as mybir
from concourse.bass_utils import run_bass_kernel_spmd

F32 = mybir.dt.float32
BF16 = mybir.dt.bfloat16
AF = mybir.ActivationFunctionType
ALU = mybir.AluOpType

S = 4096
D = 1024
NIN = 7168
DFF = 2816
NFC = DFF // 128
NH = 8
NT = S // 128
NST = S // 512
DEPTH = 4
EPS = 1e-6
N_CORES = 8

LAYERS_PER_LAUNCH = 4
DEBUG_SCRATCH = False


class Sched:
    def __init__(self, nc, es):
        self.nc = nc
        self.es = es
        self.eng = {"pe": nc.tensor, "act": nc.scalar, "dve": nc.vector, "pool": nc.gpsimd, "sp": nc.sync}
        self.sem = {e: es.enter_context(nc.semaphore("sem_" + e)) for e in self.eng}
        self.cnt = {e: 0 for e in self.eng}
        self.seen = {e: {} for e in self.eng}
        self.semobj = {}
        for e in self.eng:
            self.semobj[id(self.sem[e])] = self.sem[e]
        self.last_write = {}
        self.readers = {}
        self.dma_sems = {}
        self.n_inst = 0

    def _deps(self, reads, writes):
        deps = []
        for k in reads:
            t = self.last_write.get(k)
            if t is not None:
                deps.append(t)
        for k in writes:
            t = self.last_write.get(k)
            if t is not None:
                deps.append(t)
            r = self.readers.get(k)
            if r:
                deps.extend(r.values())
        return deps

    def _wait(self, engine, deps):
        need = {}
        for sem, val, src in deps:
            if src == engine and engine == "pe":
                continue
            sid = id(sem)
            if need.get(sid, (None, 0))[1] < val:
                need[sid] = (sem, val)
        seen = self.seen[engine]
        eo = self.eng[engine]
        for sid, (sem, val) in need.items():
            if seen.get(sid, 0) >= val:
                continue
            eo.wait_ge(sem, val)
            seen[sid] = val

    def _record(self, tok, reads, writes):
        for k in writes:
            self.last_write[k] = tok
            self.readers[k] = {}
        for k in reads:
            r = self.readers.setdefault(k, {})
            key = (id(tok[0]), tok[2])
            if key not in r or r[key][1] < tok[1]:
                r[key] = tok

    @staticmethod
    def _banks(k):
        if k == "psC":
            return [("bk", 4), ("bk", 5)]
        if isinstance(k, tuple) and len(k) == 2:
            n, a = k
            if n == "psA":
                return [("bk", a)]
            if n == "psB":
                return [("bk", 2 + a)]
            if n == "psAB":
                return [("bk", 2 * a), ("bk", 2 * a + 1)]
            if n == "psC":
                return [("bk", 4 + a)]
            if n == "psT":
                return [("bk", 6 + a)]
            if n == "psS":
                return [("bk", 0), ("bk", 1)] if a == 0 else ([("bk", 2), ("bk", 3)] if a == 1 else [("bk", 6), ("bk", 7)])
        return None

    def _xl(self, reads, writes):
        r2, w2 = [], []
        for k in reads:
            b = self._banks(k)
            if b is None:
                r2.append(k)
            else:
                w2.extend(b)
        for k in writes:
            b = self._banks(k)
            if b is None:
                w2.append(k)
            else:
                w2.extend(b)
        return r2, w2

    def op(self, engine, fns, reads=(), writes=()):
        reads, writes = self._xl(reads, writes)
        if not isinstance(fns, (list, tuple)):
            fns = [fns]
        self._wait(engine, self._deps(reads, writes))
        eo = self.eng[engine]
        inst = None
        for fn in fns:
            inst = fn(eo)
            self.n_inst += 1
        self.cnt[engine] += 1
        inst.then_inc(self.sem[engine], 1)
        tok = (self.sem[engine], self.cnt[engine], engine)
        self._record(tok, reads, writes)

    def dma(self, queue, out, in_, reads=(), writes=(), sem=None):
        assert sem is not None
        self._wait(queue, self._deps(reads, writes))
        if sem not in self.dma_sems:
            self.dma_sems[sem] = [self.es.enter_context(self.nc.semaphore("dsem_" + sem)), 0]
        d = self.dma_sems[sem]
        inst = self.eng[queue].dma_start(out=out, in_=in_)
        d[1] += 16
        inst.then_inc(d[0], 16)
        self.n_inst += 1
        tok = (d[0], d[1], "dma:" + sem)
        self._record(tok, reads, writes)

    def barrier(self):
        for e, eo in self.eng.items():
            seen = self.seen[e]
            for e2 in self.eng:
                if e2 == e:
                    continue
                v = self.cnt[e2]
                sid = id(self.sem[e2])
                if v > 0 and seen.get(sid, 0) < v:
                    eo.wait_ge(self.sem[e2], v)
                    seen[sid] = v
            for name, (sem, v) in self.dma_sems.items():
                sid = id(sem)
                if v > 0 and seen.get(sid, 0) < v:
                    eo.wait_ge(sem, v)
                    seen[sid] = v
        self.last_write = {}
        self.readers = {}


def build_program(layers, final_norm, x_is_input=True):
    nc = bass.Bass("TRN2", target_bir_lowering=False)
    dt = nc.dram_tensor

    def din(name, shape, dtype=F32):
        return dt(name, list(shape), dtype, kind="ExternalInput").ap()

    x_in = din("x", [S, D])
    attn_norm_g = din("attn_norm_g", [DEPTH, D])
    w_in = din("w_in", [DEPTH, D, NIN])
    lam_q1 = din("lam_q1", [DEPTH, 64])
    lam_k1 = din("lam_k1", [DEPTH, 64])
    lam_q2 = din("lam_q2", [DEPTH, 64])
    lam_k2 = din("lam_k2", [DEPTH, 64])
    subln_g = din("subln_g", [DEPTH, 128])
    sgu_ln_g = din("sgu_ln_g", [DEPTH, D])
    sgu_ln_b = din("sgu_ln_b", [DEPTH, D])
    w_spatial = din("w_spatial", [DEPTH, 8, 128, 128])
    b_spatial = din("b_spatial", [DEPTH, 8, 128])
    w_proj_a = din("w_proj_a", [DEPTH, D, D])
    w_proj_b = din("w_proj_b", [DEPTH, D, D])
    w_out = din("w_out", [DEPTH, D, D])
    ffn_norm_g = din("ffn_norm_g", [DEPTH, D])
    w_gate = din("w_gate", [DEPTH, D, DFF])
    w_up = din("w_up", [DEPTH, D, DFF])
    w_down = din("w_down", [DEPTH, DFF, D])
    final_norm_g = din("final_norm_g", [1, D])
    cos_t = din("cos_t", [128, S])
    sin_t = din("sin_t", [128, S])
    pmat_in = din("pmat", [128, 128])
    ident_in = din("ident", [128, 128])

    y_out = dt("y", [S, D], F32, kind="ExternalOutput").ap()

    def dscr(name, shape, dtype):
        return dt(name, list(shape), dtype, kind=("ExternalOutput" if DEBUG_SCRATCH else "Internal")).ap()

    XA = dscr("XA", [S, D], F32)
    XB = dscr("XB", [S, D], F32)
    QT = dscr("QT", [NH, 128, S], BF16)
    KT = dscr("KT", [NH, 128, S], BF16)
    VV = dscr("VV", [S, D], BF16)
    UT = dscr("UT", [D, S], BF16)
    OBT = dscr("OBT", [D, S], BF16)
    SGA = dscr("SGA", [D, S], BF16)
    SGB = dscr("SGB", [D, S], BF16)
    OA = dscr("OA", [S, D], BF16)
    AT = dscr("AT", [DFF, S], BF16)

    def fm(ap):
        return ap.rearrange("(cc p) t -> p cc t", p=128)

    with ExitStack() as es:
        s = Sched(nc, es)

        sbn = [0]

        def sb(stack, name, shape, dtype):
            sbn[0] += 1
            return stack.enter_context(nc.sbuf_tensor("sb%d_%s" % (sbn[0], name), list(shape), dtype))

        psA = es.enter_context(nc.psum_tensor("psA", [128, 2, 512], F32))
        psB = es.enter_context(nc.psum_tensor("psB", [128, 2, 512], F32))
        psC = es.enter_context(nc.psum_tensor("psC", [128, 2, 512], F32))
        psD = es.enter_context(nc.psum_tensor("psD", [128, 2, 512], F32))
        psT = [psD[:, i, :].bitcast(BF16).rearrange("p (a b) -> p a b", b=128) for i in range(2)]
        psAB = [psA, psB]
        psS = [psA, psB, psD]

        ident = sb(es, "ident", [128, 128], BF16)
        pmat = sb(es, "pmat", [128, 128], BF16)
        neghalf = sb(es, "neghalf", [128, 1], F32)
        ones_row = sb(es, "ones_row", [1, 128], F32)
        s.dma("pool", ident[:], ident_in, writes=["ident"], sem="c_ident")
        s.dma("pool", pmat[:], pmat_in, writes=["pmat"], sem="c_pmat")
        s.op("pool", lambda e: e.memset(neghalf[:], -0.5), writes=["neghalf"])
        s.op("pool", lambda e: e.memset(ones_row[:], 1.0), writes=["ones_row"])

        def load_w_cast(dst, dst_key, src2d, n_kc, sem, col_split=1):
            C = src2d.shape[1]
            cs = C // col_split
            for i in range(col_split):
                s.dma(
                    "pool",
                    dst[:, :, i * cs:(i + 1) * cs],
                    src2d[:, i * cs:(i + 1) * cs].rearrange("(kc p) c -> p kc c", p=128),
                    writes=[(dst_key, i)],
                    sem="%s_%d" % (sem, i),
                )
            return [(dst_key, i) for i in range(col_split)]

        def rstd_from_sumsq(sumsq_ap, var_ap, rstd_ap, n, keys_in, key_var, key_out):
            s.op("pool", lambda e: e.tensor_scalar(out=var_ap, in0=sumsq_ap, scalar1=1.0 / n, scalar2=EPS,
                                                   op0=ALU.mult, op1=ALU.add),
                 reads=keys_in, writes=[key_var])
            s.op("pool", lambda e: e.tensor_tensor(out=rstd_ap, in0=var_ap, in1=neghalf[:], op=ALU.pow),
                 reads=[key_var, "neghalf"], writes=[key_out])

        tcount = [0]

        def norm_front(xt_ap, xkey, gbc, gkey, hb, junk, stat, sidx, ph):
            k = tcount[0] % 2
            tcount[0] += 1
            ssq = stat[:, 0, sidx:sidx + 1]
            var = stat[:, 1, sidx:sidx + 1]
            rstd = stat[:, 2, sidx:sidx + 1]
            s.op("act", lambda e: e.activation(out=junk[:], in_=xt_ap, func=AF.Square, accum_out=ssq),
                 reads=[xkey], writes=[(ph, "junk"), (ph, "ssq", sidx)])
            rstd_from_sumsq(ssq, var, rstd, D, [(ph, "ssq", sidx)], (ph, "var", sidx), (ph, "rstd", sidx))
            s.op("dve", lambda e: e.scalar_tensor_tensor(out=hb[k][:], in0=xt_ap, scalar=rstd, in1=gbc[:],
                                                         op0=ALU.mult, op1=ALU.mult),
                 reads=[xkey, (ph, "rstd", sidx), gkey], writes=[(ph, "hb", k)])
            pt = psT[k]
            s.op("pe", [(lambda e, kc=kc: e.transpose(pt[:, kc, :], hb[k][:, kc * 128:(kc + 1) * 128], ident[:]))
                        for kc in range(8)],
                 reads=[(ph, "hb", k), "ident"], writes=[("psT", k)])
            return k

        def norm_back(k, hT_dst, hkeys):
            s.op("act", lambda e: e.copy(out=hT_dst, in_=psT[k]), reads=[("psT", k)], writes=hkeys)

        for li, l in enumerate(layers):
            lam_init = 0.8 - 0.6 * math.exp(-0.3 * l)
            if li == 0:
                x_src = x_in
            else:
                x_src = XA
            last = li == len(layers) - 1

            with ExitStack() as p1:
                hT = sb(p1, "hT", [128, 8, S], BF16)
                wF = [sb(p1, "wF%d" % i, [128, 8, 1024], BF16) for i in range(2)]

                def load_family(F):
                    return load_w_cast(wF[F % 2], ("wF", F % 2), w_in[l, :, F * 1024:(F + 1) * 1024], 8,
                                       "wF%d" % (F % 2))

                load_family(0)
                with ExitStack() as pa:
                    cosb = sb(pa, "cosb", [128, S], F32)
                    sinb = sb(pa, "sinb", [128, S], F32)
                    qs = [sb(pa, "qs%d" % i, [128, 512], BF16) for i in range(2)]
                    t1 = [sb(pa, "t1_%d" % i, [128, 512], F32) for i in range(2)]
                    t2 = [sb(pa, "t2_%d" % i, [128, 512], F32) for i in range(2)]
                    qrot = [sb(pa, "qrot%d" % i, [128, 8, 512], BF16) for i in range(2)]
                    s.dma("sp", cosb[:], cos_t, writes=["cosb"], sem="cosb")
                    s.dma("sp", sinb[:], sin_t, writes=["sinb"], sem="sinb")
                    gbc = sb(pa, "gbc", [128, D], F32)
                    xs = [sb(pa, "xs%d" % i, [128, 4, D], F32) for i in range(2)]
                    hb = [sb(pa, "hb%d" % i, [128, D], BF16) for i in range(2)]
                    junk = sb(pa, "junk", [128, D], BF16)
                    stat = sb(pa, "stat", [128, 3, NT], F32)
                    s.dma("sp", gbc[:], attn_norm_g[l:l + 1, :].to_broadcast([128, D]), writes=["gbc"], sem="gbc")

                    def ldx(J):
                        s.dma("sp", xs[J % 2][:], x_src[J * 512:(J + 1) * 512, :].rearrange("(j p) c -> p j c", p=128),
                              reads=[("X", J)], writes=[("xs", J % 2)], sem="xs%d" % (J % 2))

                    ldx(0)
                    pend1 = []
                    for J in range(NST):
                        if J + 1 < NST:
                            ldx(J + 1)
                        for j in range(4):
                            tt = J * 4 + j
                            k_ = norm_front(xs[J % 2][:, j, :], ("xs", J % 2), gbc, "gbc", hb, junk, stat, tt, "p1a")
                            if pend1:
                                norm_back(*pend1.pop())
                            pend1.append((k_, hT[:, :, tt * 128:(tt + 1) * 128], [("hT", J)]))
                    norm_back(*pend1.pop())

                    ucount = [0]

                    for F in (0, 1):
                        load_family(F + 1)
                        w = wF[F % 2]
                        dst = QT if F == 0 else KT
                        units = [(J, h) for J in range(NST) for h in range(NH)]

                        def qk_front(u):
                            J, h = units[u]
                            a = u % 2
                            tsl = slice(J * 512, (J + 1) * 512)
                            s.op("pe", [(lambda e, kc=kc: e.matmul(psA[:, a, :], lhsT=w[:, kc, h * 128:(h + 1) * 128],
                                                                   rhs=hT[:, kc, tsl], start=(kc == 0), stop=(kc == 7)))
                                        for kc in range(8)],
                                 reads=[(("wF", F % 2), 0), ("hT", J)], writes=[("psA", a)])
                            s.op("act", lambda e: e.copy(out=qs[a][:], in_=psA[:, a, :]),
                                 reads=[("psA", a)], writes=[("qs", a)])

                        def qk_back(u):
                            J, h = units[u]
                            a = u % 2
                            r = (F * NST + J) % 2
                            tsl = slice(J * 512, (J + 1) * 512)
                            s.op("pe", lambda e: e.matmul(psB[:, a, :], lhsT=pmat[:], rhs=qs[a][:], start=True, stop=True),
                                 reads=[("qs", a), "pmat"], writes=[("psB", a)])
                            s.op("pool", lambda e: e.tensor_tensor(out=t1[a][:], in0=qs[a][:], in1=cosb[:, tsl], op=ALU.mult),
                                 reads=[("qs", a), "cosb"], writes=[("t1", a)])
                            s.op("dve", lambda e: e.tensor_tensor(out=t2[a][:], in0=psB[:, a, :], in1=sinb[:, tsl], op=ALU.mult),
                                 reads=[("psB", a), "sinb"], writes=[("t2", a)])
                            s.op("dve", lambda e: e.tensor_tensor(out=qrot[r][:, h, :], in0=t1[a][:], in1=t2[a][:], op=ALU.add),
                                 reads=[("t1", a), ("t2", a)], writes=[("qrot", r)])
                            if h == NH - 1:
                                s.dma("sp", dst.rearrange("h p t -> p h t")[:, :, tsl], qrot[r][:],
                                      reads=[("qrot", r)], writes=[("QK", F, J)], sem="qrot%d" % r)

                        qk_front(0)
                        for u in range(len(units)):
                            if u + 1 < len(units):
                                qk_front(u + 1)
                            qk_back(u)
                s.barrier()

                pf = ExitStack()
                fb = [sb(pf, "fb%d" % i, [128, 4096], BF16) for i in range(2)]
                if True:
                    vs = [fb[i][:].rearrange("p (a b) -> p a b", b=D) for i in range(2)]
                    F = 2
                    load_family(F + 1)
                    w = wF[F % 2]
                    for J in range(NST):
                        r = J % 2
                        for j in range(4):
                            tt = J * 4 + j
                            ps = psAB[tt % 2]
                            s.op("pe", [(lambda e, kc=kc, c2=c2: e.matmul(ps[:, c2, :], lhsT=hT[:, kc, tt * 128:(tt + 1) * 128],
                                                                          rhs=w[:, kc, c2 * 512:(c2 + 1) * 512],
                                                                          start=(kc == 0), stop=(kc == 7)))
                                        for c2 in range(2) for kc in range(8)],
                                 reads=[(("wF", F % 2), 0), ("hT", J)], writes=[("psAB", tt % 2)])
                            s.op("act", lambda e: e.copy(out=vs[r][:, j, :], in_=ps[:].rearrange("p a b -> p (a b)")),
                                 reads=[("psAB", tt % 2)], writes=[("fb", r)])
                        s.dma("sp", VV[J * 512:(J + 1) * 512, :].rearrange("(j p) c -> p j c", p=128), vs[r],
                              reads=[("fb", r)], writes=[("VV", J)], sem="fb%d" % r)

                def fm_family(F, func, dst, stack_name):
                    if True:
                        us = [fb[i][:].rearrange("p (a b) -> p a b", b=512) for i in range(2)]
                        if F + 1 < 7:
                            load_family(F + 1)
                        w = wF[F % 2]
                        for J in range(NST):
                            r = J % 2
                            tsl = slice(J * 512, (J + 1) * 512)
                            for cc in range(8):
                                a = ucount[0] % 2
                                ucount[0] += 1
                                s.op("pe", [(lambda e, kc=kc: e.matmul(psA[:, a, :], lhsT=w[:, kc, cc * 128:(cc + 1) * 128],
                                                                       rhs=hT[:, kc, tsl], start=(kc == 0), stop=(kc == 7)))
                                            for kc in range(8)],
                                     reads=[(("wF", F % 2), 0), ("hT", J)], writes=[("psA", a)])
                                s.op("act", lambda e: e.activation(out=us[r][:, cc, :], in_=psA[:, a, :], func=func),
                                     reads=[("psA", a)], writes=[("fb", r)])
                            s.dma("sp", fm(dst)[:, :, tsl], us[r], reads=[("fb", r)], writes=[("FM", F, J)],
                                  sem="fb%d" % r)

                fm_family(3, AF.Gelu, UT, "us")

                pb = pf
                if True:
                    F = 4
                    load_family(F + 1)
                    w = wF[F % 2]
                    lng = sb(pb, "lng", [128, D], F32)
                    lnb = sb(pb, "lnb", [128, D], F32)
                    wsp = sb(pb, "wsp", [128, 8, 128], F32)
                    wspb = sb(pb, "wspb", [128, 8, 128], BF16)
                    wmT = sb(pb, "wmT", [128, 8, 128], BF16)
                    bsp = sb(pb, "bsp", [1, 8, 128], F32)
                    bhi = sb(pb, "bhi", [1, 8, 128], BF16)
                    blo = sb(pb, "blo", [1, 8, 128], BF16)
                    ones_bf = sb(pb, "ones_bf", [1, 128], BF16)
                    vg = [sb(pb, "vg%d" % i, [128, D], F32) for i in range(2)]
                    tln = sb(pb, "tln", [128, D], F32)
                    vh = [sb(pb, "vh%d" % i, [128, D], BF16) for i in range(2)]
                    ul = [sb(pb, "ul%d" % i, [128, 8, 512], BF16) for i in range(2)]
                    ob = [sb(pb, "ob%d" % i, [128, 8, 512], BF16) for i in range(2)]
                    bst = sb(pb, "bst", [128, NT, 2, 6], F32)
                    mv = sb(pb, "mv", [128, NT, 4], F32)
                    s.dma("sp", lng[:], sgu_ln_g[l:l + 1, :].to_broadcast([128, D]), writes=["lng"], sem="lng")
                    s.dma("sp", lnb[:], sgu_ln_b[l:l + 1, :].to_broadcast([128, D]), writes=["lnb"], sem="lnb")
                    s.dma("sp", wsp[:], w_spatial[l].rearrange("g t s -> t g s"), writes=["wsp"], sem="wsp")
                    s.dma("sp", bsp[:], b_spatial[l:l + 1, :, :], writes=["bsp"], sem="bsp")
                    s.op("dve", lambda e: e.tensor_copy(out=wspb[:], in_=wsp[:]), reads=["wsp"], writes=["wspb"])
                    s.op("dve", lambda e: e.memset(ones_bf[:], 1.0), writes=["ones_bf"])
                    s.op("dve", lambda e: e.tensor_copy(out=bhi[:], in_=bsp[:]), reads=["bsp"], writes=["bhi"])
                    s.op("dve", lambda e: e.tensor_tensor(out=blo[:], in0=bsp[:], in1=bhi[:], op=ALU.subtract),
                         reads=["bsp", "bhi"], writes=["blo"])
                    s.op("pe", [(lambda e, g=g: e.transpose(psT[0][:, g, :], wspb[:, g, :], ident[:])) for g in range(8)],
                         reads=["wspb", "ident"], writes=[("psT", 0)])
                    s.op("dve", lambda e: e.tensor_copy(out=wmT[:], in_=psT[0]), reads=[("psT", 0)], writes=["wmT"])
                    s.op("dve", lambda e: e.memset(wmT[64:128, :, 0:64], 0.0), reads=[], writes=["wmT"])
                    pm = psC[:].rearrange("p a (g t) -> p (a g) t", t=128)

                    def ldu(J):
                        s.dma("sp", ul[J % 2][:], fm(UT)[:, :, J * 512:(J + 1) * 512], reads=[("FM", 3, J)],
                              writes=[("ul", J % 2)], sem="ul%d" % (J % 2))

                    def vb_front(tt):
                        J = tt // 4
                        k = tt % 2
                        ps = psAB[k]
                        s.op("pe", [(lambda e, kc=kc, c2=c2: e.matmul(ps[:, c2, :], lhsT=hT[:, kc, tt * 128:(tt + 1) * 128],
                                                                      rhs=w[:, kc, c2 * 512:(c2 + 1) * 512],
                                                                      start=(kc == 0), stop=(kc == 7)))
                                    for c2 in range(2) for kc in range(8)],
                             reads=[(("wF", F % 2), 0), ("hT", J)], writes=[("psAB", k)])
                        s.op("act", lambda e: e.activation(out=vg[k][:], in_=ps[:].rearrange("p a b -> p (a b)"), func=AF.Gelu),
                             reads=[("psAB", k)], writes=[("vg", k)])
                        s.op("dve", [lambda e: e.bn_stats(out=bst[:, tt, 0, :], in_=vg[k][:, 0:512]),
                                     lambda e: e.bn_stats(out=bst[:, tt, 1, :], in_=vg[k][:, 512:1024])],
                             reads=[("vg", k)], writes=[("bst", tt)])
                        s.op("dve", lambda e: e.bn_aggr(out=mv[:, tt, 0:2], in_=bst[:, tt, :, :].rearrange("p a b -> p (a b)")),
                             reads=[("bst", tt)], writes=[("mv", tt)])
                        s.op("pool", lambda e: e.tensor_scalar(out=mv[:, tt, 2:3], in0=mv[:, tt, 1:2], scalar1=1.0, scalar2=EPS,
                                                               op0=ALU.mult, op1=ALU.add),
                             reads=[("mv", tt)], writes=[("mv2", tt)])
                        s.op("pool", lambda e: e.tensor_tensor(out=mv[:, tt, 3:4], in0=mv[:, tt, 2:3], in1=neghalf[:], op=ALU.pow),
                             reads=[("mv2", tt), "neghalf"], writes=[("mv3", tt)])
                        s.op("dve", lambda e: e.scalar_tensor_tensor(out=tln[:], in0=vg[k][:], scalar=mv[:, tt, 0:1], in1=lng[:],
                                                                     op0=ALU.subtract, op1=ALU.mult),
                             reads=[("vg", k), ("mv", tt), "lng"], writes=["tln"])
                        s.op("dve", lambda e: e.scalar_tensor_tensor(out=vh[k][:], in0=tln[:], scalar=mv[:, tt, 3:4], in1=lnb[:],
                                                                     op0=ALU.mult, op1=ALU.add),
                             reads=["tln", ("mv3", tt), "lnb"], writes=[("vh", k)])

                    def vb_back(tt):
                        J, j = tt // 4, tt % 4
                        k = tt % 2
                        r = J % 2
                        fns = []
                        for g in range(8):
                            fns.append(lambda e, g=g: e.matmul(pm[:, g, :], lhsT=vh[k][:, g * 128:(g + 1) * 128],
                                                               rhs=wmT[:, g, :], start=True, stop=False))
                            fns.append(lambda e, g=g: e.matmul(pm[:, g, :], lhsT=ones_bf[0:1, :], rhs=bhi[0:1, g, :],
                                                               start=False, stop=False))
                            fns.append(lambda e, g=g: e.matmul(pm[:, g, :], lhsT=ones_bf[0:1, :], rhs=blo[0:1, g, :],
                                                               start=False, stop=True))
                        s.op("pe", fns, reads=[("vh", k), "wmT", "ones_bf", "bhi", "blo"], writes=["psC"])
                        s.op("dve", lambda e: e.tensor_tensor(out=ob[r][:, :, j * 128:(j + 1) * 128], in0=pm,
                                                              in1=ul[r][:, :, j * 128:(j + 1) * 128], op=ALU.mult),
                             reads=["psC", ("ul", r)], writes=[("ob", r)])
                        if j == 3:
                            s.dma("sp", fm(OBT)[:, :, J * 512:(J + 1) * 512], ob[r][:], reads=[("ob", r)],
                                  writes=[("OBT", J)], sem="ob%d" % r)

                    ldu(0)
                    vb_front(0)
                    for tt in range(NT):
                        if tt % 4 == 0 and tt // 4 + 1 < NST:
                            ldu(tt // 4 + 1)
                        if tt + 1 < NT:
                            vb_front(tt + 1)
                        vb_back(tt)
                fm_family(5, AF.Sigmoid, SGA, "ga")
                fm_family(6, AF.Sigmoid, SGB, "gb")
                s.barrier()
                pf.close()

            p23 = ExitStack()
            wpa = sb(p23, "wpa", [128, 8, D], BF16)
            wpb = sb(p23, "wpb", [128, 8, D], BF16)
            wo = sb(p23, "wo", [128, 8, D], BF16)
            load_w_cast(wpa, "wpa", w_proj_a[l], 8, "wpa")
            load_w_cast(wpb, "wpb", w_proj_b[l], 8, "wpb")
            load_w_cast(wo, "wo", w_out[l], 8, "wo")
            with ExitStack() as p2:
                KTh = [sb(p2, "KTh%d" % i, [128, S], BF16) for i in range(2)]
                QTh = [sb(p2, "QTh%d" % i, [128, S], BF16) for i in range(2)]
                Vh = [sb(p2, "Vh%d" % i, [128, NT, 129], BF16) for i in range(2)]
                OAtok = [sb(p2, "OAtok%d" % i, [128, NT, 128], BF16) for i in range(2)]
                ET = [sb(p2, "ET%d" % i, [128, 2, 512], BF16) for i in range(3)]
                lamv = sb(p2, "lamv", [128, 4, 64], F32)
                lamj = sb(p2, "lamj", [128, 64], F32)
                lams = sb(p2, "lams", [128, 8], F32)
                gsub = sb(p2, "gsub", [128, 128], F32)
                rz = sb(p2, "rz", [128, NT, 4], F32)
                tO = [sb(p2, "tO%d" % i, [128, 128], F32) for i in range(2)]
                oo = [sb(p2, "oo%d" % i, [128, 128], F32) for i in range(2)]
                ojunk = sb(p2, "ojunk", [128, 128], F32)
                sst = sb(p2, "sst", [128, NT, 3], F32)

                for i in range(2):
                    s.op("pool", lambda e, i=i: e.memset(Vh[i][:, :, 128:129], 1.0), writes=[("Vones", i)])
                for i, src in enumerate((lam_q1, lam_k1, lam_q2, lam_k2)):
                    s.dma("sp", lamv[:, i, :], src[l:l + 1, :].to_broadcast([128, 64]), writes=[("lamv", i)], sem="lamv%d" % i)
                s.dma("sp", gsub[:], subln_g[l:l + 1, :].to_broadcast([128, 128]), writes=["gsub_raw"], sem="gsub")
                for di in range(2):
                    s.op("dve", lambda e, di=di: e.tensor_tensor(out=lamj[:], in0=lamv[:, 2 * di, :], in1=lamv[:, 2 * di + 1, :], op=ALU.mult),
                         reads=[("lamv", 2 * di), ("lamv", 2 * di + 1)], writes=["lamj"])
                    s.op("dve", lambda e, di=di: e.tensor_reduce(out=lams[:, di:di + 1], in_=lamj[:], axis=mybir.AxisListType.X, op=ALU.add),
                         reads=["lamj"], writes=["lam_d%d" % (di + 1)])
                s.op("act", lambda e: e.activation(out=lams[:, 2:4], in_=lams[:, 0:2], func=AF.Exp),
                     reads=["lam_d1", "lam_d2"], writes=["lam_e"])
                s.op("dve", lambda e: e.scalar_tensor_tensor(out=lams[:, 4:5], in0=lams[:, 3:4], scalar=-lam_init, in1=lams[:, 2:3],
                                                             op0=ALU.add, op1=ALU.subtract),
                     reads=["lam_e"], writes=["neglam"])
                s.op("dve", lambda e: e.tensor_scalar(out=gsub[:], in0=gsub[:], scalar1=(1.0 - lam_init), scalar2=None, op0=ALU.mult),
                     reads=["gsub_raw"], writes=["gsub"])
                neglam = lams[:, 4:5]

                def ld_head(h):
                    i = h % 2
                    s.dma("sp", KTh[i][:], KT[h], reads=[("QK", 1, J) for J in range(NST)], writes=[("KTh", i)], sem="KTh%d" % i)
                    s.dma("sp", QTh[i][:], QT[h], reads=[("QK", 0, J) for J in range(NST)], writes=[("QTh", i)], sem="QTh%d" % i)
                    s.dma("sp", Vh[i][:, :, 0:128], VV[:, h * 128:(h + 1) * 128].rearrange("(t p) e -> p t e", p=128),
                          reads=[("VV", J) for J in range(NST)] + [("Vones", i)], writes=[("Vh", i)], sem="Vh%d" % i)

                groups = []
                for h in range(NH):
                    for i in range(NT):
                        for gi in range((i + 4) // 4):
                            groups.append((h, i, gi, list(range(4 * gi, min(4 * gi + 4, i + 1)))))
                NG = len(groups)
                NSLOT = 3
                NONB = 8
                TDEFER = 6

                def oslot(h, i):
                    return (h * NT + i) % 2

                def Oacc(h, i, br):
                    return psC[:, oslot(h, i), br * 129:(br + 1) * 129]

                def emit_G(idx):
                    h, i, gi, kbs = groups[idx]
                    hi = h % 2
                    Kt, Qt = KTh[hi], QTh[hi]
                    n = len(kbs)
                    sbi = idx % 3
                    ebi = idx % 3
                    Sps = psS[sbi]
                    qsl = slice(i * 128, (i + 1) * 128)
                    fns = []
                    for kl, kb in enumerate(kbs):
                        for br in range(2):
                            fns.append(lambda e, kl=kl, kb=kb, br=br: e.matmul(
                                Sps[:, br, kl * 128:(kl + 1) * 128],
                                lhsT=Kt[br * 64:(br + 1) * 64, kb * 128:(kb + 1) * 128],
                                rhs=Qt[br * 64:(br + 1) * 64, qsl], start=True, stop=True))
                    s.op("pe", fns, reads=[("KTh", hi), ("QTh", hi)], writes=[("psS", sbi)])
                    s.op("act", lambda e: e.activation(out=ET[ebi][:, :, 0:n * 128], in_=Sps[:, :, 0:n * 128],
                                                       func=AF.Exp, scale=0.125),
                         reads=[("psS", sbi)], writes=[("ET", ebi)])
                    if kbs[-1] == i:
                        kl = n - 1
                        s.op("pool", lambda e: e.memset(ET[ebi][64:128, :, kl * 128:kl * 128 + 64], 0.0),
                             reads=[], writes=[("ET", ebi)])

                def emit_A(idx):
                    h, i, gi, kbs = groups[idx]
                    hi = h % 2
                    Vt = Vh[hi]
                    ebi = idx % 3
                    fns = []
                    for kl, kb in enumerate(kbs):
                        for br in range(2):
                            fns.append(lambda e, kl=kl, kb=kb, br=br: e.matmul(
                                Oacc(h, i, br), lhsT=ET[ebi][:, br, kl * 128:(kl + 1) * 128], rhs=Vt[:, kb, :],
                                start=(kb == 0 and br == 0), stop=(kb == i), skip_group_check=True))
                    s.op("pe", fns, reads=[("ET", ebi), ("Vh", hi)], writes=[("psC", oslot(h, i))])

                def emit_F(h, i):
                    sl = oslot(h, i)
                    ts = (h * NT + i) % 2
                    hi = h % 2
                    O0, O1 = Oacc(h, i, 0), Oacc(h, i, 1)
                    s.op("dve", [lambda e: e.reciprocal(out=rz[:, i, 0:1], in_=O0[:, 128:129]),
                                 lambda e: e.reciprocal(out=rz[:, i, 1:2], in_=O1[:, 128:129])],
                         reads=[("psC", sl)], writes=[("rz", i)])
                    s.op("dve", lambda e: e.tensor_tensor(out=rz[:, i, 2:3], in0=rz[:, i, 1:2], in1=neglam, op=ALU.mult),
                         reads=[("rz", i), "neglam"], writes=[("rz2", i)])
                    s.op("dve", lambda e: e.tensor_scalar(out=tO[ts][:], in0=O0[:, 0:128], scalar1=rz[:, i, 0:1], scalar2=None,
                                                          op0=ALU.mult),
                         reads=[("psC", sl), ("rz", i)], writes=[("tO", ts)])
                    s.op("dve", lambda e: e.scalar_tensor_tensor(out=oo[ts][:], in0=O1[:, 0:128], scalar=rz[:, i, 2:3],
                                                                 in1=tO[ts][:], op0=ALU.mult, op1=ALU.add),
                         reads=[("psC", sl), ("rz2", i), ("tO", ts)], writes=[("oo", ts)])
                    s.op("dve", lambda e: e.tensor_tensor(out=ojunk[:], in0=oo[ts][:], in1=oo[ts][:], op=ALU.mult),
                         reads=[("oo", ts)], writes=["ojunk"])
                    s.op("dve", lambda e: e.tensor_reduce(out=sst[:, i, 0:1], in_=ojunk[:], axis=mybir.AxisListType.X, op=ALU.add),
                         reads=["ojunk"], writes=[("sst", i)])
                    rstd_from_sumsq(sst[:, i, 0:1], sst[:, i, 1:2], sst[:, i, 2:3], 128, [("sst", i)], ("sst1", i), ("sst2", i))
                    s.op("dve", lambda e: e.scalar_tensor_tensor(out=OAtok[hi][:, i, :], in0=oo[ts][:], scalar=sst[:, i, 2:3], in1=gsub[:],
                                                                 op0=ALU.mult, op1=ALU.mult),
                         reads=[("oo", ts), ("sst2", i), "gsub"], writes=[("OAtok", hi)])
                    if i == NT - 1:
                        s.dma("sp", OA[:, h * 128:(h + 1) * 128].rearrange("(t p) e -> p t e", p=128), OAtok[hi][:],
                              reads=[("OAtok", hi)], writes=[("OA", h)], sem="OAtok%d" % hi)

                ld_head(0)
                ld_head(1)
                emit_G(0)
                emit_G(1)
                for idx in range(NG):
                    if idx + 2 < NG:
                        emit_G(idx + 2)
                    emit_A(idx)
                    h, i, gi, kbs = groups[idx]
                    if kbs[-1] == i:
                        emit_F(h, i)
                        if i == NT - 1 and h + 2 < NH:
                            ld_head(h + 2)
            s.barrier()

            with ExitStack() as p3:
                oaT = [sb(p3, "oaT%d" % i, [128, 8, 512], BF16) for i in range(2)]
                oatok = [sb(p3, "oatok%d" % i, [128, 4, D], BF16) for i in range(2)]
                obT = [sb(p3, "obT%d" % i, [128, 8, 512], BF16) for i in range(2)]
                sga = [sb(p3, "sga%d" % i, [128, 8, 512], BF16) for i in range(2)]
                sgb = [sb(p3, "sgb%d" % i, [128, 8, 512], BF16) for i in range(2)]
                xs3 = [sb(p3, "xs3_%d" % i, [128, 4, D], F32) for i in range(2)]
                yT = sb(p3, "yT", [128, 8, 512], BF16)
                ta = [sb(p3, "ta%d" % i, [128, 512], F32) for i in range(2)]
                tb = [sb(p3, "tb%d" % i, [128, 512], F32) for i in range(2)]

                def ld3(J):
                    i = J % 2
                    tsl = slice(J * 512, (J + 1) * 512)
                    s.dma("sp", oatok[i][:], OA[tsl, :].rearrange("(j p) c -> p j c", p=128), writes=[("oatok", i)], sem="oatok%d" % i)
                    s.dma("sp", obT[i][:], fm(OBT)[:, :, tsl], writes=[("obT", i)], sem="obT%d" % i)
                    s.dma("sp", sga[i][:], fm(SGA)[:, :, tsl], writes=[("sga", i)], sem="sga%d" % i)
                    s.dma("sp", sgb[i][:], fm(SGB)[:, :, tsl], writes=[("sgb", i)], sem="sgb%d" % i)
                    s.dma("sp", xs3[i][:], x_src[tsl, :].rearrange("(j p) c -> p j c", p=128), writes=[("xs3", i)], sem="xs3_%d" % i)

                ld3(0)
                uc = 0
                for J in range(NST):
                    if J + 1 < NST:
                        ld3(J + 1)
                    i = J % 2
                    for j in range(4):
                        tk = j % 2
                        s.op("pe", [(lambda e, kc=kc: e.transpose(psT[tk][:, kc, :], oatok[i][:, j, kc * 128:(kc + 1) * 128], ident[:]))
                                    for kc in range(8)],
                             reads=[("oatok", i), "ident"], writes=[("psT", tk)])
                        s.op("act", lambda e: e.copy(out=oaT[i][:, :, j * 128:(j + 1) * 128], in_=psT[tk]),
                             reads=[("psT", tk)], writes=[("oaT", i)])
                    for cc in range(8):
                        a = uc % 2
                        uc += 1
                        s.op("pe", [(lambda e, kc=kc: e.matmul(psA[:, a, :], lhsT=wpa[:, kc, cc * 128:(cc + 1) * 128], rhs=oaT[i][:, kc, :],
                                                               start=(kc == 0), stop=(kc == 7))) for kc in range(8)],
                             reads=[("wpa", 0), ("oaT", i)], writes=[("psA", a)])
                        s.op("pe", [(lambda e, kc=kc: e.matmul(psB[:, a, :], lhsT=wpb[:, kc, cc * 128:(cc + 1) * 128], rhs=obT[i][:, kc, :],
                                                               start=(kc == 0), stop=(kc == 7))) for kc in range(8)],
                             reads=[("wpb", 0), ("obT", i)], writes=[("psB", a)])
                        s.op("dve", lambda e: e.tensor_tensor(out=ta[a][:], in0=psA[:, a, :], in1=sga[i][:, cc, :], op=ALU.mult),
                             reads=[("psA", a), ("sga", i)], writes=[("ta", a)])
                        s.op("dve", lambda e: e.tensor_tensor(out=tb[a][:], in0=psB[:, a, :], in1=sgb[i][:, cc, :], op=ALU.mult),
                             reads=[("psB", a), ("sgb", i)], writes=[("tb", a)])
                        s.op("pool", lambda e: e.tensor_tensor(out=yT[:, cc, :], in0=ta[a][:], in1=tb[a][:], op=ALU.add),
                             reads=[("ta", a), ("tb", a)], writes=[("yT", cc)])
                    for j in range(4):
                        for c2 in range(2):
                            s.op("pe", [(lambda e, kc=kc: e.matmul(psC[:, c2, :], lhsT=yT[:, kc, j * 128:(j + 1) * 128],
                                                                   rhs=wo[:, kc, c2 * 512:(c2 + 1) * 512],
                                                                   start=(kc == 0), stop=(kc == 7)))
                                        for kc in range(8)],
                                 reads=[("wo", 0)] + [("yT", cc) for cc in range(8)], writes=[("psC", c2)])
                            s.op("dve", lambda e: e.tensor_tensor(out=xs3[i][:, j, c2 * 512:(c2 + 1) * 512], in0=psC[:, c2, :],
                                                                  in1=xs3[i][:, j, c2 * 512:(c2 + 1) * 512], op=ALU.add),
                                 reads=[("psC", c2), ("xs3", i)], writes=[("xs3", i)])
                    s.dma("sp", XB[J * 512:(J + 1) * 512, :].rearrange("(j p) c -> p j c", p=128), xs3[i][:],
                          reads=[("xs3", i)], writes=[("XB", J)], sem="xs3o_%d" % i)
            s.barrier()
            p23.close()

            with ExitStack() as p4:
                wg = sb(p4, "wg", [128, 8, DFF], BF16)
                wu = sb(p4, "wu", [128, 8, DFF], BF16)
                gbc2 = sb(p4, "gbc2", [128, D], F32)
                xs4 = [sb(p4, "xs4_%d" % i, [128, 4, D], F32) for i in range(2)]
                hb4 = [sb(p4, "hb4_%d" % i, [128, D], BF16) for i in range(4)]
                junk4 = sb(p4, "junk4", [128, D], BF16)
                stat4 = sb(p4, "stat4", [128, 3, NT], F32)
                h2T = [sb(p4, "h2T%d" % i, [128, 8, 512], BF16) for i in range(2)]
                sgt = [sb(p4, "sgt%d" % i, [128, 512], F32) for i in range(2)]
                aT = [sb(p4, "aT%d" % i, [128, NFC, 512], BF16) for i in range(2)]
                s.dma("sp", gbc2[:], ffn_norm_g[l:l + 1, :].to_broadcast([128, D]), writes=["gbc2"], sem="gbc2")
                for hf in range(2):
                    for (dst_, key_, src_) in ((wg, "wg", w_gate[l]), (wu, "wu", w_up[l])):
                        cs_ = DFF // 2
                        s.dma("pool", dst_[:, :, hf * cs_:(hf + 1) * cs_],
                              src_[:, hf * cs_:(hf + 1) * cs_].rearrange("(kc p) c -> p kc c", p=128),
                              writes=[(key_, hf)], sem="%s_%d" % (key_, hf))

                def ld4(J):
                    s.dma("sp", xs4[J % 2][:], XB[J * 512:(J + 1) * 512, :].rearrange("(j p) c -> p j c", p=128),
                          writes=[("xs4", J % 2)], sem="xs4_%d" % (J % 2))

                def n4A(J, j):
                    i_ = J % 2
                    tt = J * 4 + j
                    xt = xs4[i_][:, j, :]
                    ssq = stat4[:, 0, tt:tt + 1]
                    s.op("act", lambda e: e.activation(out=junk4[:], in_=xt, func=AF.Square, accum_out=ssq),
                         reads=[("xs4", i_)], writes=[("p4a", "junk"), ("p4a", "ssq", tt)])
                    rstd_from_sumsq(ssq, stat4[:, 1, tt:tt + 1], stat4[:, 2, tt:tt + 1], D, [("p4a", "ssq", tt)],
                                    ("p4a", "var", tt), ("p4a", "rstd", tt))
                    s.op("dve", lambda e: e.scalar_tensor_tensor(out=hb4[j][:], in0=xt, scalar=stat4[:, 2, tt:tt + 1], in1=gbc2[:],
                                                                 op0=ALU.mult, op1=ALU.mult),
                         reads=[("xs4", i_), ("p4a", "rstd", tt), "gbc2"], writes=[("p4a", "hb", j)])

                def n4B(J, j):
                    i_ = J % 2
                    k = j % 2
                    s.op("pe", [(lambda e, kc=kc: e.transpose(psT[k][:, kc, :], hb4[j][:, kc * 128:(kc + 1) * 128], ident[:]))
                                for kc in range(8)],
                         reads=[("p4a", "hb", j), "ident"], writes=[("psT", k)])
                    s.op("act", lambda e: e.copy(out=h2T[i_][:, :, j * 128:(j + 1) * 128], in_=psT[k]),
                         reads=[("psT", k)], writes=[("h2T", i_)])

                ld4(0)
                uc = 0
                for j in range(4):
                    n4A(0, j)
                for j in range(4):
                    n4B(0, j)
                for J in range(NST):
                    if J + 1 < NST:
                        ld4(J + 1)
                    i = J % 2
                    for fc in range(NFC):
                        if fc == 8 and J + 1 < NST:
                            for j in range(4):
                                n4A(J + 1, j)
                        if fc == 17 and J + 1 < NST:
                            for j in range(4):
                                n4B(J + 1, j)
                        a = uc % 2
                        uc += 1
                        half = 0 if fc < NFC // 2 else 1
                        s.op("pe", [(lambda e, kc=kc: e.matmul(psA[:, a, :], lhsT=wg[:, kc, fc * 128:(fc + 1) * 128], rhs=h2T[i][:, kc, :],
                                                               start=(kc == 0), stop=(kc == 7))) for kc in range(8)],
                             reads=[("wg", half), ("h2T", i)], writes=[("psA", a)])
                        s.op("pe", [(lambda e, kc=kc: e.matmul(psB[:, a, :], lhsT=wu[:, kc, fc * 128:(fc + 1) * 128], rhs=h2T[i][:, kc, :],
                                                               start=(kc == 0), stop=(kc == 7))) for kc in range(8)],
                             reads=[("wu", half), ("h2T", i)], writes=[("psB", a)])
                        s.op("act", lambda e: e.activation(out=sgt[a][:], in_=psA[:, a, :], func=AF.Silu),
                             reads=[("psA", a)], writes=[("sgt", a)])
                        s.op("dve", lambda e: e.tensor_tensor(out=aT[i][:, fc, :], in0=psB[:, a, :], in1=sgt[a][:], op=ALU.mult),
                             reads=[("psB", a), ("sgt", a)], writes=[("aT", i)])
                    s.dma("sp", AT.rearrange("(fc p) t -> p fc t", p=128)[:, :, J * 512:(J + 1) * 512], aT[i][:],
                          reads=[("aT", i)], writes=[("AT", J)], sem="aT%d" % i)
            s.barrier()

            with ExitStack() as p5:
                wd = sb(p5, "wd", [128, NFC, D], BF16)
                aTl = [sb(p5, "aTl%d" % i, [128, NFC, 512], BF16) for i in range(2)]
                xs5 = [sb(p5, "xs5_%d" % i, [128, 4, D], F32) for i in range(2)]
                for ch in range(2):
                    s.dma("pool", wd[:, ch * 11:(ch + 1) * 11, :],
                          w_down[l, ch * 1408:(ch + 1) * 1408, :].rearrange("(kc p) c -> p kc c", p=128),
                          writes=[("wd", ch)], sem="wd_%d" % ch)
                do_final = last and final_norm
                if do_final:
                    gfin = sb(p5, "gfin", [128, D], F32)
                    junk5 = sb(p5, "junk5", [128, D], BF16)
                    stat5 = sb(p5, "stat5", [128, 3, NT], F32)
                    s.dma("sp", gfin[:], final_norm_g[0:1, :].to_broadcast([128, D]), writes=["gfin"], sem="gfin")
                x_dst = y_out if last else XA

                def ld5(J):
                    i = J % 2
                    tsl = slice(J * 512, (J + 1) * 512)
                    s.dma("sp", aTl[i][:], AT.rearrange("(fc p) t -> p fc t", p=128)[:, :, tsl], writes=[("aTl", i)], sem="aTl%d" % i)
                    s.dma("sp", xs5[i][:], XB[tsl, :].rearrange("(j p) c -> p j c", p=128), writes=[("xs5", i)], sem="xs5_%d" % i)

                ld5(0)
                for J in range(NST):
                    if J + 1 < NST:
                        ld5(J + 1)
                    i = J % 2
                    for j in range(4):
                        tt = J * 4 + j
                        ps = psAB[tt % 2]
                        for ch in range(2):
                            s.op("pe", [(lambda e, fc=fc, c2=c2: e.matmul(ps[:, c2, :], lhsT=aTl[i][:, fc, j * 128:(j + 1) * 128],
                                                                          rhs=wd[:, fc, c2 * 512:(c2 + 1) * 512],
                                                                          start=(fc == 0), stop=(fc == NFC - 1)))
                                        for c2 in range(2) for fc in range(ch * 11, (ch + 1) * 11)],
                                 reads=[("wd", ch), ("aTl", i)], writes=[("psAB", tt % 2)])
                        s.op("dve", lambda e: e.tensor_tensor(out=xs5[i][:, j, :], in0=ps[:].rearrange("p a b -> p (a b)"),
                                                              in1=xs5[i][:, j, :], op=ALU.add),
                             reads=[("psAB", tt % 2), ("xs5", i)], writes=[("xs5", i)])
                        if do_final:
                            ssq = stat5[:, 0, tt:tt + 1]
                            s.op("act", lambda e: e.activation(out=junk5[:], in_=xs5[i][:, j, :], func=AF.Square, accum_out=ssq),
                                 reads=[("xs5", i)], writes=["junk5", ("f_ssq", tt)])
                            rstd_from_sumsq(ssq, stat5[:, 1, tt:tt + 1], stat5[:, 2, tt:tt + 1], D, [("f_ssq", tt)],
                                            ("f_var", tt), ("f_rstd", tt))
                            s.op("dve", lambda e: e.scalar_tensor_tensor(out=xs5[i][:, j, :], in0=xs5[i][:, j, :],
                                                                         scalar=stat5[:, 2, tt:tt + 1], in1=gfin[:],
                                                                         op0=ALU.mult, op1=ALU.mult),
                                 reads=[("xs5", i), ("f_rstd", tt), "gfin"], writes=[("xs5", i)])
                    s.dma("sp", x_dst[J * 512:(J + 1) * 512, :].rearrange("(j p) c -> p j c", p=128), xs5[i][:],
                          reads=[("xs5", i)], writes=[("X", J)], sem="xs5o_%d" % i)
            s.barrier()
        print("instructions emitted:", s.n_inst)
    return nc


_CONST_CACHE = {}


def _consts():
    if "c" not in _CONST_CACHE:
        inv_freq = (10000.0 ** (-np.arange(0, 64, 2, dtype=np.float32) / np.float32(64))).astype(np.float32)
        ang = (np.arange(S, dtype=np.float32)[:, None] * inv_freq[None, :]).astype(np.float32)
        cos = np.cos(ang).astype(np.float32)
        sin = np.sin(ang).astype(np.float32)
        d = np.arange(128) % 64
        cos_t = np.ascontiguousarray(cos[:, d % 32].T)
        sgn = np.where(d < 32, -1.0, 1.0).astype(np.float32)
        sin_t = np.ascontiguousarray((sin[:, d % 32] * sgn[None, :]).T)
        pm = np.zeros((128, 128), np.float32)
        for p in range(128):
            partner = p + 32 if (p % 64) < 32 else p - 32
            pm[partner, p] = 1.0
        ident = np.eye(128, dtype=np.float32)
        _CONST_CACHE["c"] = dict(cos_t=cos_t, sin_t=sin_t, pmat=pm, ident=ident)
    return _CONST_CACHE["c"]


_PROG_CACHE = {}


def _get_prog(layers, final_norm):
    key = (tuple(layers), final_norm)
    if key not in _PROG_CACHE:
        _PROG_CACHE[key] = build_program(list(layers), final_norm)
    return _PROG_CACHE[key]


def kernel(x, attn_norm_g, w_in, lam_q1, lam_k1, lam_q2, lam_k2, subln_g, sgu_ln_g, sgu_ln_b,
           w_spatial, b_spatial, w_proj_a, w_proj_b, w_out, ffn_norm_g, w_gate, w_up, w_down, final_norm_g):
    f = lambda a: np.ascontiguousarray(np.asarray(a, dtype=np.float32))
    shared = dict(
        attn_norm_g=f(attn_norm_g), w_in=f(w_in), lam_q1=f(lam_q1), lam_k1=f(lam_k1), lam_q2=f(lam_q2), lam_k2=f(lam_k2),
        subln_g=f(subln_g), sgu_ln_g=f(sgu_ln_g), sgu_ln_b=f(sgu_ln_b), w_spatial=f(w_spatial), b_spatial=f(b_spatial),
        w_proj_a=f(w_proj_a), w_proj_b=f(w_proj_b), w_out=f(w_out), ffn_norm_g=f(ffn_norm_g), w_gate=f(w_gate),
        w_up=f(w_up), w_down=f(w_down), final_norm_g=f(final_norm_g).reshape(1, D),
    )
    shared.update(_consts())
    xcur = f(x)
    groups = [list(range(i, min(i + LAYERS_PER_LAUNCH, DEPTH))) for i in range(0, DEPTH, LAYERS_PER_LAUNCH)]
    for gi, layers in enumerate(groups):
        final = gi == len(groups) - 1
        nc = _get_prog(layers, final)
        in_maps = [dict(shared, x=np.ascontiguousarray(xcur[b])) for b in range(N_CORES)]
        res = run_bass_kernel_spmd(nc, in_maps, core_ids=list(range(N_CORES)))
        xcur = np.stack([np.asarray(res.results[b]["y"], dtype=np.float32) for b in range(N_CORES)], axis=0)
    return xcur
```
